# Optimizing a Trainium2 kernel written in Bass

```python
import math
import jax, jax.numpy as jnp
from jax import lax
import numpy as np

D_MODEL = 1024
BATCH = 4
SEQ = 4096
DEPTH = 1

D_MIX = D_MODEL
D_POOL = D_MIX // 2
D_ATTN = D_MIX - D_POOL
POOL_WINDOWS = (2, 4, 8, 16)
N_POOL_GROUPS = len(POOL_WINDOWS)
POOL_GROUP_DIM = D_POOL // N_POOL_GROUPS
HEAD_DIM = 64
N_HEADS = D_ATTN // HEAD_DIM
Q_BLOCK = 128
D_FF = 4 * D_MODEL
D_IN_PROJ = D_POOL + 3 * D_ATTN
EPS = 1e-6

kernel_name = "hymba_pool_stickbreak_block"


def rmsnorm(x, g):
    xf = x.astype(jnp.float32)
    r = lax.rsqrt(jnp.mean(xf * xf, axis=-1, keepdims=True) + EPS)
    return (xf * r * g.astype(jnp.float32)).astype(x.dtype)


def pool_mixer(u, pool_w, pool_scale):
    B, S, _ = u.shape
    ug = u.reshape(B, S, N_POOL_GROUPS, POOL_GROUP_DIM)
    pos = jnp.arange(S, dtype=jnp.int32)
    outs = []
    for g, w in enumerate(POOL_WINDOWS):
        xg = ug[:, :, g, :].astype(jnp.float32)
        cs = jnp.cumsum(xg, axis=1)
        cs_lag = jnp.pad(cs, ((0, 0), (w, 0), (0, 0)))[:, :S]
        count = jnp.minimum(pos + 1, w).astype(jnp.float32)[None, :, None]
        mean = (cs - cs_lag) / count
        outs.append(mean - xg)
    pooled = jnp.stack(outs, axis=2)
    mapped = jnp.einsum('bsgc,gcd->bsgd', pooled, pool_w.astype(jnp.float32))
    y = mapped.reshape(B, S, D_POOL) * pool_scale.astype(jnp.float32)
    return y.astype(u.dtype)


def stick_breaking_attention(q, k, v):
    B, H, S, Dh = q.shape
    scale = 1.0 / math.sqrt(Dh)
    n_blocks = S // Q_BLOCK
    outs = []
    for i in range(n_blocks):
        q0, end = i * Q_BLOCK, (i + 1) * Q_BLOCK
        qb = q[:, :, q0:end]
        kb = k[:, :, :end]
        vb = v[:, :, :end]
        z = jnp.einsum('bhqd,bhkd->bhqk', qb, kb).astype(jnp.float32) * scale
        t_idx = q0 + jnp.arange(Q_BLOCK, dtype=jnp.int32)
        s_idx = jnp.arange(end, dtype=jnp.int32)
        mask = s_idx[None, :] < t_idx[:, None]
        log1m = jnp.where(mask, jax.nn.log_sigmoid(-z), 0.0)
        tail = lax.cumsum(log1m, axis=3, reverse=True) - log1m
        log_a = jax.nn.log_sigmoid(z) + tail
        a = jnp.where(mask, jnp.exp(log_a), 0.0)
        ob = jnp.einsum('bhqk,bhkd->bhqd', a, vb.astype(jnp.float32))
        outs.append(ob)
    o = jnp.concatenate(outs, axis=2)
    return o.astype(q.dtype)


def setup_inputs(seed: int = 0) -> dict:
    key = jax.random.key(seed)
    ks = jax.random.split(key, 12)
    f32 = jnp.float32
    x = jax.random.normal(ks[0], (BATCH, SEQ, D_MODEL), f32)
    norm1_g = 1.0 + 0.02 * jax.random.normal(ks[1], (D_MODEL,), f32)
    w_in = jax.random.normal(ks[2], (D_MODEL, D_IN_PROJ), f32) * D_MODEL ** -0.5
    pool_w = jax.random.normal(ks[3], (N_POOL_GROUPS, POOL_GROUP_DIM, POOL_GROUP_DIM), f32) * POOL_GROUP_DIM ** -0.5
    pool_scale = 0.5 + 0.02 * jax.random.normal(ks[4], (D_POOL,), f32)
    pool_out_g = 1.0 + 0.02 * jax.random.normal(ks[5], (D_POOL,), f32)
    attn_out_g = 1.0 + 0.02 * jax.random.normal(ks[6], (D_ATTN,), f32)
    w_out = jax.random.normal(ks[7], (D_MIX, D_MODEL), f32) * D_MIX ** -0.5
    norm2_g = 1.0 + 0.02 * jax.random.normal(ks[8], (D_MODEL,), f32)
    w_up = jax.random.normal(ks[9], (D_MODEL, D_FF), f32) * D_MODEL ** -0.5
    w_down = jax.random.normal(ks[10], (D_FF, D_MODEL), f32) * D_FF ** -0.5
    final_g = 1.0 + 0.02 * jax.random.normal(ks[11], (D_MODEL,), f32)
    return {"x": x, "norm1_g": norm1_g, "w_in": w_in, "pool_w": pool_w,
            "pool_scale": pool_scale, "pool_out_g": pool_out_g, "attn_out_g": attn_out_g,
            "w_out": w_out, "norm2_g": norm2_g, "w_up": w_up, "w_down": w_down,
            "final_g": final_g}


def reference(x, norm1_g, w_in, pool_w, pool_scale, pool_out_g, attn_out_g,
              w_out, norm2_g, w_up, w_down, final_g):
    B, S, _ = x.shape
    h = x
    for _ in range(DEPTH):
        hn = rmsnorm(h, norm1_g)
        proj = jnp.einsum('bsd,de->bse', hn, w_in)
        u_pool = proj[..., :D_POOL]
        q = proj[..., D_POOL:D_POOL + D_ATTN]
        k = proj[..., D_POOL + D_ATTN:D_POOL + 2 * D_ATTN]
        v = proj[..., D_POOL + 2 * D_ATTN:]
        to_heads = lambda t: t.reshape(B, S, N_HEADS, HEAD_DIM).transpose(0, 2, 1, 3)
        y_pool = pool_mixer(u_pool, pool_w, pool_scale)
        o = stick_breaking_attention(to_heads(q), to_heads(k), to_heads(v))
        y_attn = o.transpose(0, 2, 1, 3).reshape(B, S, D_ATTN)
        mixed = jnp.concatenate([rmsnorm(y_pool, pool_out_g),
                                 rmsnorm(y_attn, attn_out_g)], axis=-1)
        h = h + jnp.einsum('bse,ed->bsd', mixed, w_out)
        hn2 = rmsnorm(h, norm2_g)
        up = jnp.einsum('bsd,df->bsf', hn2, w_up)
        act = jnp.square(jax.nn.relu(up))
        h = h + jnp.einsum('bsf,fd->bsd', act, w_down)
    return rmsnorm(h, final_g)
```

```python
import contextlib
import numpy as np
import ml_dtypes
import concourse.bass as bass
import concourse.mybir as mybir
from concourse.bass_utils import run_bass_kernel_spmd

F32 = mybir.dt.float32
BF16 = mybir.dt.bfloat16
AF = mybir.ActivationFunctionType
ALU = mybir.AluOpType

D = 1024
S = 4096
DFF = 4096
EPS = 1e-6
WINS = (2, 4, 8, 16)
ENGS = ("pe", "act", "dve", "pool", "sp")

C_ID = 0
C_TRI = 128
C_SEL = 256
C_NONE = 384
C_MASK = 448
C_BAND = C_MASK + 4 * 512
C_BAND0 = C_BAND + 512
C_BANDP = C_BAND0 + 512
C_ZERO = C_BANDP + 512
C_END = C_ZERO + 64

ARENA_F32 = 45056
OPTS = {"a_tiles": 8, "a_proj": True, "a_pool": True, "a_tr": True, "a_q": 1, "a_u": 1, "b_slots": 4, "b_pairs": 4, "b_nk": 0, "b_lvl": 10, "b_dummy": 8, "b_dummy2": 0, "b_order": 2}


class Buf:
    __slots__ = ("w", "r")

    def __init__(self):
        self.w = None
        self.r = {}


class Prog:
    def __init__(self, nc, stack):
        self.nc = nc
        self.stack = stack
        self.q = {e: [] for e in ENGS}
        self.cnt = {e: 0 for e in ENGS}
        self.sem = {e: stack.enter_context(nc.semaphore("prog_" + e)) for e in ENGS}
        self.waited = {e: {} for e in ENGS}
        self.nsem = 0
        self.dsems = []

    def _emit_waits(self, eng, deps):
        for d in deps:
            if d is None:
                continue
            kind, key, val = d
            if kind == "eng" and key == "pe" and eng == "pe":
                continue
            ident = key if kind == "eng" else id(key)
            if self.waited[eng].get(ident, 0) >= val:
                continue
            self.waited[eng][ident] = val
            sem = self.sem[key] if kind == "eng" else key
            self.q[eng].append(("wait", sem, val))

    @staticmethod
    def _deps(reads, writes):
        deps = []
        for b in reads:
            if b.w is not None:
                deps.append(b.w)
        for b in writes:
            if b.w is not None:
                deps.append(b.w)
            deps.extend(b.r.values())
        return deps

    @staticmethod
    def _record(tok, reads, writes):
        kind, key, val = tok
        rk = key if kind == "eng" else ("dma", id(key))
        for b in reads:
            b.r[rk] = tok
        for b in writes:
            b.w = tok
            b.r = {}

    def op(self, eng, fn, reads=(), writes=(), deps=()):
        self._emit_waits(eng, list(deps) + self._deps(reads, writes))
        self.cnt[eng] += 1
        self.q[eng].append(("op", fn, self.sem[eng]))
        tok = ("eng", eng, self.cnt[eng])
        self._record(tok, reads, writes)
        return tok

    def new_dma_sem(self):
        self.nsem += 1
        s = self.stack.enter_context(self.nc.semaphore("dsem%d" % self.nsem))
        rec = [s, 0]
        self.dsems.append(rec)
        return rec

    def dma(self, eng, semrec, fn, reads=(), writes=(), deps=()):
        self._emit_waits(eng, list(deps) + self._deps(reads, writes))
        semrec[1] += 16
        self.q[eng].append(("dma", fn, semrec[0]))
        tok = ("dma", semrec[0], semrec[1])
        self._record(tok, reads, writes)
        return tok

    def wait(self, eng, deps):
        self._emit_waits(eng, deps)

    def I(self, eng, method, *args, reads=(), writes=(), deps=(), **kw):
        return self.op(eng, lambda e: getattr(e, method)(*args, **kw), reads, writes, deps)

    def D(self, eng, semrec, out, in_, reads=(), writes=(), deps=()):
        return self.dma(eng, semrec, lambda e: e.dma_start(out=out, in_=in_), reads, writes, deps)

    def emit(self):
        nc = self.nc
        with nc.Block() as block:
            def run(engine, items):
                for it in items:
                    if it[0] == "wait":
                        engine.wait_ge(it[1], it[2])
                    elif it[0] == "op":
                        it[1](engine).then_inc(it[2], 1)
                    else:
                        it[1](engine).then_inc(it[2], 16)

            @block.tensor
            def _(e):
                run(e, self.q["pe"])

            @block.scalar
            def _(e):
                run(e, self.q["act"])

            @block.vector
            def _(e):
                run(e, self.q["dve"])

            @block.gpsimd
            def _(e):
                run(e, self.q["pool"])

            @block.sync
            def _(e):
                run(e, self.q["sp"])


def build_program(phases="ABC", debug=False):
    nc = bass.Bass("TRN2", target_bir_lowering=False)
    dram_in = lambda n, s, dt=F32: nc.dram_tensor(n, s, dt, kind="ExternalInput").ap()
    xk = dram_in("xk", [S, D])
    w_in = dram_in("w_in", [D, 2048])
    pool_w = dram_in("pool_w", [128, 4, 128])
    w_out = dram_in("w_out", [D, D])
    w_up = dram_in("w_up", [D, DFF])
    w_down = dram_in("w_down", [DFF, D])
    g1bc_d = dram_in("g1bc", [128, D])
    g2bc_d = dram_in("g2bc", [128, D])
    gfbc_d = dram_in("gfbc", [128, D])
    pscale_d = dram_in("pscale", [128, 4])
    gout_d = dram_in("gout", [128, 8])
    cst_d = dram_in("cst", [128, C_END], BF16)
    y = nc.dram_tensor("y", [2048, D], F32, kind="ExternalOutput").ap()
    dbg = {}
    if debug:
        dbg["kT"] = nc.dram_tensor("dbg_kT", [128, 4 * S], BF16, kind="ExternalOutput").ap()
        dbg["v"] = nc.dram_tensor("dbg_v", [128, 32 * 512], BF16, kind="ExternalOutput").ap()
        dbg["qT"] = nc.dram_tensor("dbg_qT", [128, 4 * 2048], BF16, kind="ExternalOutput").ap()
        dbg["yp"] = nc.dram_tensor("dbg_yp", [128, 4 * 2048], BF16, kind="ExternalOutput").ap()
        dbg["ya"] = nc.dram_tensor("dbg_ya", [128, 4 * 2048], BF16, kind="ExternalOutput").ap()
        dbg["rr"] = nc.dram_tensor("dbg_rr", [128, 32], F32, kind="ExternalOutput").ap()

    with contextlib.ExitStack() as st:
        P = Prog(nc, st)
        sb = lambda n, s, dt: st.enter_context(nc.sbuf_tensor("s_" + n, s, dt))

        cst = sb("cst", [128, C_END], BF16)
        pscale = sb("pscale", [128, 4], F32)
        gout = sb("gout", [128, 8], F32)
        pwst = sb("pwst", [128, 4, 128], F32)
        pwb = sb("pwb", [128, 4, 128], BF16)
        ones = sb("ones", [128, 2], F32)
        ypT = sb("ypT", [128, 4, 2048], BF16)
        rr = sb("rr", [128, 32], F32)
        stat = sb("stat", [128, 512], F32)
        AR = sb("arena", [128, ARENA_F32], F32)

        def view(off_bytes, nbytes, dt):
            assert off_bytes % 4 == 0 and nbytes % 4 == 0 and off_bytes + nbytes <= ARENA_F32 * 4
            a = AR[:, off_bytes // 4:(off_bytes + nbytes) // 4]
            return a if dt == F32 else a.bitcast(dt)

        KB = 1024
        PSALL = st.enter_context(nc.psum_tensor("psall", [128, 8 * 512], F32))
        ps = [PSALL[:, i * 512:(i + 1) * 512] for i in range(8)]
        psb = [Buf() for _ in range(8)]

        stat_col = [0]

        def new_stat(n):
            c = stat_col[0]
            stat_col[0] += n
            assert stat_col[0] <= 512
            return stat[:, c:c + n], Buf()

        def barrier():
            for e in ENGS:
                deps = [("eng", o, P.cnt[o]) for o in ENGS if P.cnt[o] > 0]
                P.wait(e, deps)

        csem = P.new_dma_sem()
        b_cst = Buf()
        g1bc = view(171 * KB, 4 * KB, F32)
        for dst, src in ((cst[:], cst_d), (g1bc, g1bc_d), (pscale[:], pscale_d), (gout[:], gout_d),
                         (pwst[:], pool_w)):
            P.D("sp", csem, out=dst, in_=src, writes=[b_cst])
        b_pwb = Buf()
        b_ones = Buf()
        P.I("dve", "tensor_copy", out=pwb[:], in_=pwst[:], reads=[b_cst], writes=[b_pwb])
        P.I("dve", "memset", ones[:], 1.0, writes=[b_ones])

        kT = view(0, 32 * KB, BF16).rearrange("p (c t) -> p c t", c=4)
        vv = view(32 * KB, 32 * KB, BF16).rearrange("p (b e) -> p b e", b=32)
        qT = view(64 * KB, 16 * KB, BF16).rearrange("p (c t) -> p c t", c=4)
        yaT = view(80 * KB, 16 * KB, BF16).rearrange("p (c t) -> p c t", c=4)
        b_kT = [Buf() for _ in range(8)]
        b_v = [Buf() for _ in range(8)]
        b_qT = [Buf() for _ in range(4)]
        b_yp = [Buf() for _ in range(4)]
        b_ya = [Buf() for _ in range(4)]
        b_rr = Buf()
        P.I("dve", "memset", rr[:], 1.0, writes=[b_rr])

        def alias(dst, srcs):
            for sbuf in srcs:
                for k, t in list(sbuf.r.items()) + ([(None, sbuf.w)] if sbuf.w is not None else []):
                    key = (t[1] if t[0] == "eng" else ("dma", id(t[1])))
                    if key not in dst.r or dst.r[key][2] < t[2]:
                        dst.r[key] = t

        def phase_a():
            Y = 80 * KB
            w_inb = view(Y, 32 * KB, BF16).rearrange("p (c e) -> p c e", c=8)
            stg = [view(Y + 32 * KB + i * 4 * KB, 4 * KB, F32) for i in range(2)]
            xbs = [view(Y + 40 * KB + i * 4 * KB, 4 * KB, F32) for i in range(3)]
            xh = [view(Y + 52 * KB + i * 2 * KB, 2 * KB, BF16) for i in range(2)]
            xhT = [view(Y + 56 * KB + i * 8 * KB, 8 * KB, BF16).rearrange("p (c t) -> p c t", c=8)
                   for i in range(2)]
            uu = view(Y + 72 * KB, 5 * KB, BF16).rearrange("p (b e) -> p b e", b=5)
            pooledT = view(Y + 77 * KB, 4 * KB, BF16).rearrange("p (g t) -> p g t", g=4)
            sqp = view(Y + 81 * KB, 8 * KB, F32).rearrange("p (g t) -> p g t", g=4)
            junk = view(Y + 89 * KB, 2 * KB, BF16)
            yp32 = view(Y + 32 * KB, 2 * KB, F32)
            stg = stg + [view(Y + 81 * KB + i * 4 * KB, 4 * KB, F32) for i in range(2)]
            b_stg = [Buf() for _ in range(4)]
            b_win = [Buf() for _ in range(8)]
            b_xb = [Buf() for _ in range(3)]
            b_xh = [Buf() for _ in range(2)]
            b_xhT = [Buf() for _ in range(2)]
            b_u = [Buf() for _ in range(5)]
            b_pooled = Buf()
            b_sqp = Buf()
            b_yp32 = b_stg[0]
            b_junk = Buf()
            xsem = [P.new_dma_sem() for _ in range(3)]
            ssem = [P.new_dma_sem() for _ in range(4)]

            w_in_v = w_in.rearrange("(c p) e -> p c e", p=128)
            WIN_ORDER = (2, 3, 0, 1)

            def load_win(i):
                cb, cp = WIN_ORDER[i // 4], i % 4
                s_ = i % 4
                P.D("sp", ssem[s_], out=stg[s_].rearrange("p (c e) -> p c e", c=2),
                    in_=w_in_v[:, cp * 2:cp * 2 + 2, cb * 512:(cb + 1) * 512], writes=[b_stg[s_]])
                eng = ("dve", "act")[i % 2]
                dst = w_inb[:, cp * 2:cp * 2 + 2, cb * 512:(cb + 1) * 512]
                src = stg[s_].rearrange("p (c e) -> p c e", c=2)
                if eng == "act":
                    P.I("act", "copy", out=dst, in_=src, reads=[b_stg[s_]], writes=[b_win[cb]])
                else:
                    P.I(eng, "tensor_copy", out=dst, in_=src, reads=[b_stg[s_]], writes=[b_win[cb]])

            def load_x(g):
                xb = xbs[g % 3]
                P.D("sp", xsem[g % 3], out=xb, in_=xk[g * 128:(g + 1) * 128, :],
                      writes=[b_xb[g % 3]])

            pacc_rr = [0]

            def pacc():
                i = 2 + pacc_rr[0] % 6
                pacc_rr[0] += 1
                return ps[i], psb[i]

            evac_rr = [0]

            def evac(out_ap, in_ap, reads, writes, scale=None):
                evac_rr[0] ^= 1
                if scale is not None and OPTS["a_q"] == 3:
                    return P.I("act", "mul", out=out_ap, in_=in_ap, mul=scale, reads=reads, writes=writes)
                if scale is not None and OPTS["a_q"] == 4:
                    return P.I("dve", "tensor_scalar", out=out_ap, in0=in_ap, scalar1=scale, scalar2=None,
                               op0=ALU.mult, reads=reads, writes=writes)
                if scale is not None and OPTS["a_q"] == 5:
                    return P.I("act", "activation", out=out_ap, in_=in_ap, func=AF.Copy, scale=scale, reads=reads, writes=writes)
                if evac_rr[0]:
                    if scale is None:
                        return P.I("act", "copy", out=out_ap, in_=in_ap, reads=reads, writes=writes)
                    return P.I("act", "activation", out=out_ap, in_=in_ap, func=AF.Copy, scale=scale,
                                reads=reads, writes=writes)
                if scale is None:
                    return P.I("dve", "tensor_copy", out=out_ap, in_=in_ap, reads=reads, writes=writes)
                return P.I("dve", "tensor_scalar", out=out_ap, in0=in_ap, scalar1=scale, scalar2=None,
                                                             op0=ALU.mult, reads=reads, writes=writes)

            def norm_front(g, kt, j):
                xb, bxb = xbs[g % 3], b_xb[g % 3]
                ss, bss = new_stat(3)
                P.I("act", "activation", out=junk, in_=xb, func=AF.Square, accum_out=ss[:, 0:1],
                     reads=[bxb], writes=[b_junk, bss])
                P.I("act", "activation", out=ss[:, 1:2], in_=ss[:, 0:1], func=AF.Sqrt, scale=1.0 / D, bias=EPS,
                     reads=[bss], writes=[bss])
                P.I("dve", "reciprocal", out=ss[:, 2:3], in_=ss[:, 1:2], reads=[bss], writes=[bss])
                xhb, bxh = xh[g % 2], b_xh[g % 2]
                P.I("dve", "scalar_tensor_tensor", out=xhb, in0=xb, scalar=ss[:, 2:3], in1=g1bc,
                                                             op0=ALU.mult, op1=ALU.mult,
                     reads=[bxb, bss, b_cst], writes=[bxh])

            def norm_back(g, kt, j):
                xhb, bxh = xh[g % 2], b_xh[g % 2]
                pt = ps[g % 2][:].bitcast(BF16).rearrange("p (c t) -> p c t", c=8)
                for c in range(8):
                    P.I("pe", "transpose", out=pt[:, c, :], in_=xhb[:, c * 128:(c + 1) * 128], identity=cst[:, C_ID:C_ID + 128],
                         reads=[bxh, b_cst], writes=[psb[g % 2]])
                evac(xhT[kt % 2][:, :, j * 128:(j + 1) * 128], pt, [psb[g % 2]], [b_xhT[kt % 2]])

            def proj_T(dst_ap, dst_buf, col0, xT, bxT, scale=None):
                pa, pb = pacc()
                for c in range(8):
                    P.I("pe", "matmul", pa[:], lhsT=w_inb[:, c, col0:col0 + 128], rhs=xT[:, c, :],
                                                       start=(c == 0), stop=(c == 7),
                         reads=[b_win[col0 // 512], bxT], writes=[pb])
                evac(dst_ap, pa[:], [pb], [dst_buf], scale=scale)

            def proj_tok(dst_ap, dst_buf, col0, xT, bxT, j):
                pa, pb = pacc()
                for c in range(8):
                    P.I("pe", "matmul", pa[:], lhsT=xT[:, c, j * 128:(j + 1) * 128], rhs=w_inb[:, c, col0:col0 + 512],
                                                       start=(c == 0), stop=(c == 7),
                         reads=[b_win[col0 // 512], bxT], writes=[pb])
                evac(dst_ap, pa[:], [pb], [dst_buf])

            NBLK = 32
            nx = 0
            for i in range(3):
                load_x(nx)
                nx += 1
            for i in range(4):
                load_win(i)

            def tile_items(kt):
                own = kt % 2 == 1
                m = kt // 2
                xT, bxT = xhT[kt % 2], b_xhT[kt % 2]
                items = []
                for ec in range(4):
                    items.append(lambda ec=ec: proj_T(kT[:, ec, kt * 512:(kt + 1) * 512], b_kT[kt], 1024 + ec * 128, xT, bxT))
                for j in range(4):
                    items.append(lambda j=j: proj_tok(vv[:, kt * 4 + j, :], b_v[kt], 1536, xT, bxT, j))
                if not own:
                    items.append(lambda: proj_tok(uu[:, 0, :], b_u[0], 0, xT, bxT, 3))
                else:
                    for ec in range(4):
                        items.append(lambda ec=ec: proj_T(qT[:, ec, m * 512:(m + 1) * 512], b_qT[m], 512 + ec * 128, xT, bxT, scale=0.125))
                    for j in range(4):
                        items.append(lambda j=j: proj_tok(uu[:, 1 + j, :], b_u[1 + j], 0, xT, bxT, j))
                return items

            def next_x():
                if nxs[0] < NBLK:
                    load_x(nxs[0])
                    nxs[0] += 1

            nxs = [nx]
            for j in range(4):
                norm_front(j, 0, j)
                norm_back(j, 0, j)
                next_x()
            for i in range(4, 16):
                load_win(i)
            alias(b_sqp, [b_stg[2], b_stg[3]])
            for kt in range(8):
                own = kt % 2 == 1
                m = kt // 2
                items = tile_items(kt)
                n = len(items)
                for q in range(4):
                    if kt + 1 < 8:
                        norm_front((kt + 1) * 4 + q, kt + 1, q)
                    for it in items[q * n // 4:(q + 1) * n // 4]:
                        it()
                    if kt + 1 < 8:
                        norm_back((kt + 1) * 4 + q, kt + 1, q)
                        next_x()
                if not own:
                    continue
                if not OPTS["a_pool"]:
                    continue
                ssq, bssq = new_stat(8)
                for g in range(4):
                    pa, pb = pacc()
                    for j in range(4):
                        bcol = (C_BAND0 if (m == 0 and j == 0) else C_BAND) + g * 128
                        P.I("pe", "matmul", pa[:, j * 128:(j + 1) * 128], lhsT=uu[:, 1 + j, g * 128:(g + 1) * 128],
                                                                      rhs=cst[:, bcol:bcol + 128], start=True, stop=False,
                             reads=[b_u[1 + j], b_cst], writes=[pb])
                        P.I("pe", "matmul", pa[:, j * 128:(j + 1) * 128], lhsT=uu[:, j, g * 128:(g + 1) * 128],
                                                           rhs=cst[:, C_BANDP + g * 128:C_BANDP + (g + 1) * 128], start=False, stop=True,
                             reads=[b_u[j], b_cst], writes=[pb])
                    evac(pooledT[:, g, :], pa[:], [pb], [b_pooled])
                    pm, pmb = pacc()
                    P.I("pe", "matmul", pm[:], lhsT=pwb[:, g, :], rhs=pooledT[:, g, :], start=True, stop=True,
                         reads=[b_pwb, b_pooled], writes=[pmb])
                    P.I("dve", "tensor_scalar", out=yp32, in0=pm[:], scalar1=pscale[:, g:g + 1], scalar2=None, op0=ALU.mult,
                        reads=[pmb, b_cst], writes=[b_yp32])
                    P.I("pool", "tensor_copy", out=ypT[:, g, m * 512:(m + 1) * 512], in_=yp32, reads=[b_yp32], writes=[b_yp[m]])
                    P.I("act", "activation", out=sqp[:, g, :], in_=yp32, func=AF.Square, reads=[b_yp32], writes=[b_sqp])
                if OPTS["a_pool"] != 1:
                    continue
                pq, pqb = pacc()
                for j in range(4):
                    for g in range(4):
                        P.I("pe", "matmul", pq[:, j:j + 1], lhsT=sqp[:, g, j * 128:(j + 1) * 128], rhs=ones[:, 0:1],
                                                                start=(g == 0), stop=(g == 3),
                             reads=[b_sqp, b_ones], writes=[pqb])
                P.I("act", "activation", out=ssq[:, 0:4], in_=pq[:, 0:4], func=AF.Sqrt, scale=1.0 / 512, bias=EPS,
                     reads=[pqb], writes=[bssq])
                P.I("dve", "reciprocal", out=rr[:, m * 4:(m + 1) * 4], in_=ssq[:, 0:4], reads=[bssq], writes=[b_rr])

        c_stg = [view(128 * KB + i * 8 * KB, 8 * KB, F32) for i in range(2)]
        c_w_outb = view(144 * KB, 16 * KB, BF16).rearrange("p (c d) -> p c d", c=8)
        c_g2bc = view(160 * KB, 4 * KB, F32)
        c_b_wout = Buf()
        c_b_stg = [Buf() for _ in range(2)]
        c_b_g = Buf()
        c_ssem = [P.new_dma_sem() for _ in range(2)]
        c_stg_rr = [0]

        def prefetch_c():
            gsem0 = P.new_dma_sem()
            P.D("sp", gsem0, c_g2bc, g2bc_d, writes=[c_b_g])
            w_out_v = w_out.rearrange("(c p) d -> p c d", p=128)
            for gi in range(4):
                s_ = c_stg_rr[0] % 2
                c_stg_rr[0] += 1
                P.D("sp", c_ssem[s_], c_stg[s_].rearrange("p (c d) -> p c d", c=2), w_out_v[:, gi * 2:gi * 2 + 2, :], writes=[c_b_stg[s_]])
                sv = c_stg[s_].rearrange("p (c d) -> p c d", c=2)
                for cc in range(2):
                    c = gi * 2 + cc
                    P.I("dve", "tensor_scalar", out=c_w_outb[:, c, :], in0=sv[:, cc, :], scalar1=gout[:, c:c + 1],
                        scalar2=None, op0=ALU.mult, reads=[c_b_stg[s_], b_cst], writes=[c_b_wout])

        def phase_b():
            Y = 96 * KB
            Eb = [view(Y + i * 4 * KB, 4 * KB, F32) for i in range(2)]
            Lb = [view(Y + 8 * KB + i * 2 * KB, 2 * KB, BF16) for i in range(2)]
            Ab = [view(Y + 12 * KB + i * 2 * KB, 2 * KB, BF16) for i in range(2)]
            Rt = [view(Y + 16 * KB + i * KB, KB, BF16) for i in range(4)]
            sqa = view(Y + 20 * KB, 8 * KB, F32).rearrange("p (c t) -> p c t", c=4)
            o32s = [view(Y + 28 * KB + i * 2 * KB, 2 * KB, F32) for i in range(2)]
            pending = []
            b_E = [Buf() for _ in range(2)]
            b_L = [Buf() for _ in range(2)]
            b_A = [Buf() for _ in range(2)]
            b_Rt = [Buf() for _ in range(4)]
            b_sqa = Buf()
            b_o32s = [Buf(), Buf()]
            for i in range(4):
                P.I("pool", "memset", Rt[i], 0.0, writes=[b_Rt[i]])
            mhalf, b_mhalf = new_stat(4)
            P.I("pool", "memset", mhalf, -0.5, writes=[b_mhalf])
            for i in (6, 7):
                P.I("dve", "memset", ps[i], 0.0, writes=[psb[i]])
            b_z = [[Buf(), Buf()] for _ in range(2)]
            b_ops = [[Buf(), Buf()] for _ in range(2)]
            b_rps = [[Buf(), Buf()] for _ in range(2)]
            for q in range(2):
                for hp in range(2):
                    b_z[q][hp].w, b_z[q][hp].r = psb[2 * q + hp].w, dict(psb[2 * q + hp].r)
                    b_ops[q][hp].w, b_ops[q][hp].r = psb[4 + q].w, dict(psb[4 + q].r)
                    b_rps[q][hp].w, b_rps[q][hp].r = psb[6 + q].w, dict(psb[6 + q].r)

            for m in range(4):
                nk = 8 * (m + 1)
                for dp in range(2):
                    def QK(q, tau):
                        ec = dp * 2 + q
                        kb = nk - 1 - tau
                        di = kb - (nk - 4)
                        for hp in range(2):
                            z = ps[2 * q + hp]
                            pr = slice(hp * 64, hp * 64 + 64)
                            P.I("pe", "matmul", z, lhsT=kT[pr, ec, kb * 128:(kb + 1) * 128], rhs=qT[pr, ec, m * 512:(m + 1) * 512],
                                start=True, stop=(di < 0), reads=[b_kT[kb // 4], b_qT[m]], writes=[b_z[q][hp]])
                        if di >= 0:
                            for hp in range(2):
                                P.I("pe", "matmul", ps[2 * q + hp], lhsT=cst[:, C_ID:C_ID + 128],
                                    rhs=cst[:, C_MASK + di * 512:C_MASK + (di + 1) * 512],
                                    start=False, stop=True, reads=[b_cst], writes=[b_z[q][hp]])

                    def zpair(q):
                        return PSALL[:, 2 * q * 512:(2 * q + 2) * 512]

                    def cols(ap, tau):
                        c0 = max(0, 3 - tau) * 128
                        if c0 == 0:
                            return ap
                        return ap.rearrange("p (h t) -> p h t", h=2)[:, :, c0:512]

                    def EXP1(q, tau):
                        P.I("act", "activation", out=cols(Eb[q], tau), in_=cols(zpair(q), tau), func=AF.Exp, reads=b_z[q], writes=[b_E[q]])

                    def LN(q, tau):
                        if tau == 0:
                            P.I("pool", "memset", Lb[q].rearrange("p (h t) -> p h t", h=2)[:, :, 0:384], 0.0, writes=[b_L[q]])
                            P.I("pool", "memset", Ab[q].rearrange("p (h t) -> p h t", h=2)[:, :, 0:384], 0.0, writes=[b_A[q]])
                        P.I("act", "activation", out=cols(Lb[q], tau), in_=cols(Eb[q], tau), func=AF.Ln, bias=1.0, reads=[b_E[q]], writes=[b_L[q]])

                    def TRISEL(q, tau):
                        ri = q * 2 + tau % 2
                        for hp in range(2):
                            z = ps[2 * q + hp]
                            P.I("pe", "matmul", z, lhsT=cst[:, C_TRI:C_TRI + 128], rhs=Lb[q][:, hp * 512:(hp + 1) * 512],
                                start=False, stop=True, skip_group_check=True, reads=[b_L[q], b_cst], writes=[b_z[q][hp]])
                            if tau > 0 and OPTS["b_lvl"] != 10:
                                r0 = hp * 64
                                P.I("pe", "matmul", z, lhsT=cst[r0:r0 + 33, C_SEL:C_SEL + 128], rhs=Rt[ri][r0:r0 + 33, :],
                                    start=False, stop=True, skip_group_check=True, reads=[b_Rt[ri], b_cst], writes=[b_z[q][hp]])
                        if tau > 0 and OPTS["b_lvl"] == 10:
                            for hp in range(2):
                                r0 = hp * 64
                                P.I("pe", "matmul", ps[2 * q + hp], lhsT=cst[r0:r0 + 33, C_SEL:C_SEL + 128], rhs=Rt[ri][r0:r0 + 33, :],
                                    start=False, stop=True, skip_group_check=True, reads=[b_Rt[ri], b_cst], writes=[b_z[q][hp]])

                    def COL(q, tau):
                        if tau >= nk - 1:
                            return
                        rn = q * 2 + (tau + 1) % 2
                        RPS = ps[6 + q]
                        for hp in range(2):
                            r0 = hp * 64
                            P.I("pe", "matmul", RPS[r0:r0 + 33, :], lhsT=cst[:, C_NONE:C_NONE + 33], rhs=Lb[q][:, hp * 512:(hp + 1) * 512],
                                start=(tau == 0), stop=True, skip_group_check=(tau > 0), reads=[b_L[q], b_cst], writes=[b_rps[q][hp]])
                        P.I("dve", "tensor_copy", out=Rt[rn][0:97, :], in_=RPS[0:97, :], reads=b_rps[q], writes=[b_Rt[rn]])
                        for hp in range(2):
                            r1 = hp * 64 + 32
                            P.I("dve", "tensor_tensor", out=Rt[rn][r1:r1 + 1, :], in0=RPS[r1:r1 + 1, :], in1=Rt[rn][r1:r1 + 1, :],
                                op=ALU.subtract, reads=[b_rps[q][hp], b_Rt[rn]], writes=[b_Rt[rn]])

                    def EXP2(q, tau):
                        P.I("act", "activation", out=cols(Ab[q], tau), in_=cols(zpair(q), tau), func=AF.Exp, reads=b_z[q], writes=[b_A[q]])

                    def AV(q, tau):
                        kb = nk - 1 - tau
                        OPS = ps[4 + q]
                        for hp in range(2):
                            h = (dp * 2 + q) * 2 + hp
                            pr = slice(hp * 64, hp * 64 + 64)
                            P.I("pe", "matmul", OPS[pr, :], lhsT=vv[:, kb, h * 64:(h + 1) * 64], rhs=Ab[q][:, hp * 512:(hp + 1) * 512],
                                start=(tau == 0), stop=(tau == nk - 1), reads=[b_v[kb // 4], b_A[q]], writes=[b_ops[q][hp]])

                    def WARM(q, tau, n):
                        if tau == 0 or tau == nk - 1:
                            return
                        for _ in range(n):
                            P.I("pe", "matmul", ps[4 + q][0:64, :], lhsT=cst[:, C_ZERO:C_ZERO + 64], rhs=cst[:, C_MASK:C_MASK + 512],
                                start=False, stop=False, reads=[b_cst], writes=[b_ops[q][0]])

                    for q in range(2):
                        QK(q, 0)
                    for tau in range(nk):
                        if tau == 2 and pending:
                            for fn in pending:
                                fn()
                            del pending[:]
                        if OPTS["b_order"] == 0:
                            EXP1(0, tau)
                            LN(0, tau)
                            TRISEL(0, tau)
                            EXP1(1, tau)
                            LN(1, tau)
                            TRISEL(1, tau)
                            COL(0, tau)
                            COL(1, tau)
                            WARM(0, tau, OPTS["b_dummy2"])
                            EXP2(0, tau)
                            AV(0, tau)
                            if tau + 1 < nk:
                                QK(0, tau + 1)
                            EXP2(1, tau)
                            AV(1, tau)
                            if tau + 1 < nk:
                                QK(1, tau + 1)
                            WARM(1, tau, OPTS["b_dummy"])
                        elif OPTS["b_order"] == 2:
                            EXP1(0, tau)
                            EXP1(1, tau)
                            LN(0, tau)
                            TRISEL(0, tau)
                            LN(1, tau)
                            TRISEL(1, tau)
                            COL(0, tau)
                            COL(1, tau)
                            WARM(0, tau, OPTS["b_dummy2"])
                            EXP2(0, tau)
                            AV(0, tau)
                            if tau + 1 < nk:
                                QK(0, tau + 1)
                            EXP2(1, tau)
                            AV(1, tau)
                            if tau + 1 < nk:
                                QK(1, tau + 1)
                            WARM(1, tau, OPTS["b_dummy"])
                        else:
                            EXP1(0, tau)
                            EXP1(1, tau)
                            WARM(0, tau, OPTS["b_dummy"])
                            LN(0, tau)
                            TRISEL(0, tau)
                            COL(0, tau)
                            LN(1, tau)
                            TRISEL(1, tau)
                            COL(1, tau)
                            WARM(1, tau, OPTS["b_dummy2"])
                            EXP2(0, tau)
                            AV(0, tau)
                            if tau + 1 < nk:
                                QK(0, tau + 1)
                            EXP2(1, tau)
                            AV(1, tau)
                            if tau + 1 < nk:
                                QK(1, tau + 1)
                    for q in range(2):
                        ec = dp * 2 + q
                        P.I("dve", "tensor_copy", out=o32s[q], in_=ps[4 + q], reads=b_ops[q], writes=[b_o32s[q]])
                        P.I("pool", "tensor_copy", out=yaT[:, ec, m * 512:(m + 1) * 512], in_=o32s[q], reads=[b_o32s[q]], writes=[b_ya[m]])
                        pending.append(lambda q=q, ec=ec: P.I("act", "activation", out=sqa[:, ec, :], in_=o32s[q], func=AF.Square,
                                                              reads=[b_o32s[q]], writes=[b_sqa]))
                def slot_stats(m=m):
                    ssq, bssq = new_stat(4)
                    RPS = ps[6]
                    for j in range(4):
                        for ec in range(4):
                            P.I("pe", "matmul", RPS[:, j:j + 1], lhsT=sqa[:, ec, j * 128:(j + 1) * 128], rhs=ones[:, 0:1],
                                start=(ec == 0), stop=(ec == 3), reads=[b_sqa, b_ones], writes=b_rps[0])
                    P.I("dve", "tensor_copy", out=ssq[:, 0:4], in_=RPS[:, 0:4], reads=b_rps[0], writes=[bssq])
                    P.I("pool", "tensor_scalar", out=ssq[:, 0:4], in0=ssq[:, 0:4], scalar1=1.0 / 512, scalar2=EPS, op0=ALU.mult, op1=ALU.add,
                        reads=[bssq], writes=[bssq])
                    P.I("pool", "tensor_tensor", out=rr[:, 16 + m * 4:16 + (m + 1) * 4], in0=ssq[:, 0:4], in1=mhalf[:, 0:4], op=ALU.pow,
                        reads=[bssq, b_mhalf], writes=[b_rr])
                for fn in pending:
                    fn()
                del pending[:]
                slot_stats()

        def phase_c():
            hb = view(0, 64 * KB, F32).rearrange("p (j d) -> p j d", j=16)
            wupb = [view(64 * KB + i * 8 * KB, 8 * KB, BF16).rearrange("p (c f) -> p c f", c=8) for i in range(2)]
            hT = view(96 * KB, 32 * KB, BF16).rearrange("p (c t) -> p c t", c=8)
            stg = c_stg
            w_outb = c_w_outb
            wdnb = [view(144 * KB + i * 8 * KB, 8 * KB, BF16).rearrange("p (c d) -> p c d", c=4) for i in range(2)]
            g2bc = c_g2bc
            hh = [view(164 * KB + i * 2 * KB, 2 * KB, BF16) for i in range(2)]
            junk = view(168 * KB, 2 * KB, BF16)
            actG = [view(160 * KB, 16 * KB, BF16).rearrange("p (f t) -> p f t", f=4),
                    view(80 * KB, 16 * KB, BF16).rearrange("p (f t) -> p f t", f=4)]
            ypf = ypT[:].rearrange("p c t -> p (c t)").bitcast(F32)
            gfbc = ypf[:, 0:1024]
            sqs = [ypf[:, 1024 + i * 512:1024 + (i + 1) * 512] for i in range(2)]
            junk2 = ypf[:, 2048:2560].bitcast(BF16)
            b_wout = c_b_wout
            b_stg = c_b_stg
            b_junk = Buf()
            b_junk2 = Buf()
            b_wup = [Buf() for _ in range(2)]
            b_wdn = [Buf() for _ in range(2)]
            b_g = c_b_g
            b_gf = Buf()
            b_hb = [Buf() for _ in range(16)]
            b_hh = [Buf() for _ in range(2)]
            b_hT = [Buf() for _ in range(4)]
            b_act = [Buf() for _ in range(2)]
            b_sqs = [Buf() for _ in range(2)]
            ssem = c_ssem
            hsem = [P.new_dma_sem() for _ in range(4)]
            osem = [P.new_dma_sem() for _ in range(4)]
            gsem = P.new_dma_sem()
            stg_rr = c_stg_rr
            cast_rr = [0]

            for m in range(4):
                src = xk[(2 * m + 1) * 512:(2 * m + 2) * 512, :].rearrange("(j p) d -> p j d", p=128)
                P.D("sp", hsem[m], hb[:, m * 4:(m + 1) * 4, :], src, writes=[b_hb[m * 4 + j] for j in range(4)])

            def stage_load(src_ap, pattern, **kw):
                s_ = stg_rr[0] % 2
                stg_rr[0] += 1
                P.D("sp", ssem[s_], stg[s_].rearrange(pattern, **kw), src_ap, writes=[b_stg[s_]])
                return s_

            def outproj(blk):
                m = blk // 4
                bh = b_hb[blk]
                for dh in range(2):
                    k = (blk * 2 + dh) % 2
                    p1, p1b = ps[k * 2], psb[k * 2]
                    p2, p2b = ps[k * 2 + 1], psb[k * 2 + 1]
                    for ec in range(4):
                        P.I("pe", "matmul", p1[:], lhsT=ypT[:, ec, blk * 128:(blk + 1) * 128], rhs=w_outb[:, ec, dh * 512:(dh + 1) * 512],
                            start=(ec == 0), stop=(ec == 3), reads=[b_yp[m], b_wout], writes=[p1b])
                    for ec in range(4):
                        P.I("pe", "matmul", p2[:], lhsT=yaT[:, ec, blk * 128:(blk + 1) * 128], rhs=w_outb[:, 4 + ec, dh * 512:(dh + 1) * 512],
                            start=(ec == 0), stop=(ec == 3), reads=[b_ya[m], b_wout], writes=[p2b])
                    hs = hb[:, blk, dh * 512:(dh + 1) * 512]
                    P.I("dve", "scalar_tensor_tensor", out=hs, in0=p1[:], scalar=rr[:, blk:blk + 1], in1=hs, op0=ALU.mult, op1=ALU.add,
                        reads=[p1b, b_rr, bh], writes=[bh])
                    P.I("dve", "scalar_tensor_tensor", out=hs, in0=p2[:], scalar=rr[:, 16 + blk:17 + blk], in1=hs, op0=ALU.mult, op1=ALU.add,
                        reads=[p2b, b_rr, bh], writes=[bh])

            def norm2(blk):
                m = blk // 4
                bh = b_hb[blk]
                ss, bss = new_stat(3)
                P.I("act", "activation", out=junk, in_=hb[:, blk, :], func=AF.Square, accum_out=ss[:, 0:1], reads=[bh], writes=[b_junk, bss])
                P.I("act", "activation", out=ss[:, 1:2], in_=ss[:, 0:1], func=AF.Sqrt, scale=1.0 / D, bias=EPS, reads=[bss], writes=[bss])
                P.I("dve", "reciprocal", out=ss[:, 2:3], in_=ss[:, 1:2], reads=[bss], writes=[bss])
                hhb, bhh = hh[blk % 2], b_hh[blk % 2]
                P.I("dve", "scalar_tensor_tensor", out=hhb, in0=hb[:, blk, :], scalar=ss[:, 2:3], in1=g2bc, op0=ALU.mult, op1=ALU.mult,
                    reads=[bh, bss, b_g], writes=[bhh])

            def norm2_back(blk):
                m = blk // 4
                hhb, bhh = hh[blk % 2], b_hh[blk % 2]
                pi = 4 + blk % 2
                pt = ps[pi][:].bitcast(BF16).rearrange("p (c t) -> p c t", c=8)
                for c in range(8):
                    P.I("pe", "transpose", out=pt[:, c, :], in_=hhb[:, c * 128:(c + 1) * 128], identity=cst[:, C_ID:C_ID + 128],
                        reads=[bhh, b_cst], writes=[psb[pi]])
                P.I("act", "copy", out=hT[:, :, blk * 128:(blk + 1) * 128], in_=pt, reads=[psb[pi]], writes=[b_hT[m]])

            w_up_v = w_up.rearrange("(c p) f -> p c f", p=128)
            w_dn_v = w_down.rearrange("(fc p) d -> p fc d", p=128)

            def cast(dst, src, reads, writes):
                cast_rr[0] += 1
                if cast_rr[0] % 2:
                    P.I("act", "copy", out=dst, in_=src, reads=reads, writes=writes)
                else:
                    P.I("pool", "tensor_copy", out=dst, in_=src, reads=reads, writes=writes)

            def load_wup(G):
                for half in range(2):
                    f0 = G * 512 + half * 256
                    s = stage_load(w_up_v[:, :, f0:f0 + 256], "p (c f) -> p c f", c=8)
                    cast(wupb[G % 2][:, :, half * 256:(half + 1) * 256], stg[s].rearrange("p (c f) -> p c f", c=8), [b_stg[s]], [b_wup[G % 2]])

            def load_wdn(G):
                for half in range(2):
                    fc0 = G * 4 + half * 2
                    s = stage_load(w_dn_v[:, fc0:fc0 + 2, :], "p (c d) -> p c d", c=2)
                    cast(wdnb[G % 2][:, half * 2:(half + 1) * 2, :], stg[s].rearrange("p (c d) -> p c d", c=2), [b_stg[s]], [b_wdn[G % 2]])

            def load_w(G):
                load_wup(G)
                load_wdn(G)

            load_wup(0)

            for b in range(18):
                if b < 16:
                    outproj(b)
                if 0 <= b - 1 < 16:
                    norm2(b - 1)
                if 0 <= b - 2 < 16:
                    norm2_back(b - 2)

            up_rr = [0]

            def up(G):
                wb, bw = wupb[G % 2], b_wup[G % 2]
                for f4 in range(4):
                    for tt in range(4):
                        pi = up_rr[0] % 4
                        up_rr[0] += 1
                        pu, pub = ps[pi], psb[pi]
                        for c in range(8):
                            P.I("pe", "matmul", pu[:], lhsT=wb[:, c, f4 * 128:(f4 + 1) * 128], rhs=hT[:, c, tt * 512:(tt + 1) * 512],
                                start=(c == 0), stop=(c == 7), reads=[bw, b_hT[tt]], writes=[pub])
                        sq, bsq = sqs[pi % 2], b_sqs[pi % 2]
                        P.I("act", "copy", out=sq, in_=pu[:], reads=[pub], writes=[bsq])
                        P.I("dve", "scalar_tensor_tensor", out=actG[G % 2][:, f4, tt * 512:(tt + 1) * 512], in0=pu[:], scalar=0.0, in1=sq,
                            op0=ALU.max, op1=ALU.mult, reads=[pub, bsq], writes=[b_act[G % 2]])

            dn_rr = [0]

            def down(G, last=False):
                wb, bw = wdnb[G % 2], b_wdn[G % 2]
                for blk in range(16):
                    if last and blk > 0:
                        final_norm(blk - 1)
                    for dh in range(2):
                        pi = 4 + dn_rr[0] % 4
                        dn_rr[0] += 1
                        for f4 in range(4):
                            P.I("pe", "matmul", ps[pi][:], lhsT=actG[G % 2][:, f4, blk * 128:(blk + 1) * 128], rhs=wb[:, f4, dh * 512:(dh + 1) * 512],
                                start=(f4 == 0), stop=(f4 == 3), reads=[bw, b_act[G % 2]], writes=[psb[pi]])
                        hs = hb[:, blk, dh * 512:(dh + 1) * 512]
                        P.I("dve", "tensor_tensor", out=hs, in0=ps[pi][:], in1=hs, op=ALU.add, reads=[psb[pi], b_hb[blk]], writes=[b_hb[blk]])
                if last:
                    final_norm(15)

            def final_norm(blk):
                m = blk // 4
                bh = b_hb[blk]
                ss, bss = new_stat(3)
                P.I("act", "activation", out=junk2, in_=hb[:, blk, :], func=AF.Square, accum_out=ss[:, 0:1], reads=[bh], writes=[b_junk2, bss])
                P.I("act", "activation", out=ss[:, 1:2], in_=ss[:, 0:1], func=AF.Sqrt, scale=1.0 / D, bias=EPS, reads=[bss], writes=[bss])
                P.I("dve", "reciprocal", out=ss[:, 2:3], in_=ss[:, 1:2], reads=[bss], writes=[bss])
                P.I("dve", "scalar_tensor_tensor", out=hb[:, blk, :], in0=hb[:, blk, :], scalar=ss[:, 2:3], in1=gfbc, op0=ALU.mult, op1=ALU.mult,
                    reads=[bh, bss, b_gf], writes=[bh])
                if blk % 4 == 3:
                    dst = y[m * 512:(m + 1) * 512, :].rearrange("(j p) d -> p j d", p=128)
                    P.D("sp", osem[m], dst, hb[:, m * 4:(m + 1) * 4, :], reads=[b_hb[m * 4 + j] for j in range(4)])

            alias(b_act[0], [b_g, b_hh[0], b_hh[1], b_junk])
            alias(b_act[1], b_ya)
            for bb in b_wdn:
                alias(bb, [b_wout])
            for bb in [b_gf, b_junk2] + b_sqs:
                alias(bb, b_yp)
            P.D("sp", gsem, gfbc, gfbc_d, writes=[b_gf])
            NG = 8
            load_wdn(0)
            up(0)
            for G in range(NG):
                if G + 1 < NG:
                    load_w(G + 1)
                    up(G + 1)
                down(G, last=(G == NG - 1))

        if "A" in phases:
            phase_a()
        if "B" in phases:
            barrier()
            prefetch_c()
            phase_b()
        if "C" in phases:
            if "B" not in phases:
                prefetch_c()
            barrier()
            phase_c()

        if debug:
            barrier()
            dsem = P.new_dma_sem()
            tk = None
            for name, src in (("kT", view(0, 32 * KB, BF16)), ("v", view(32 * KB, 32 * KB, BF16)),
                              ("qT", view(64 * KB, 16 * KB, BF16)), ("yp", ypT[:].rearrange("p c t -> p (c t)")),
                              ("ya", view(80 * KB, 16 * KB, BF16)), ("rr", rr[:])):
                tk = P.D("sp", dsem, out=dbg[name], in_=src)
            P.wait("sp", [tk])
        P.wait("sp", [("dma", r[0], r[1]) for r in P.dsems if r[1] > 0])
        P.emit()
    return nc


def _bf16(a):
    return a.astype(ml_dtypes.bfloat16)


def make_consts(parity):
    C = np.zeros((128, C_END), np.float32)
    idx = np.arange(128)
    C[:, C_ID:C_ID + 128] = np.eye(128)
    C[:, C_TRI:C_TRI + 128] = -(idx[:, None] >= idx[None, :]).astype(np.float32)
    for r in (0, 32, 64, 96):
        C[r, C_SEL:C_SEL + 128] = 1.0
    C[:, C_NONE:C_NONE + 64] = -1.0
    t = np.arange(512)
    for i in range(4):
        C[:, C_MASK + i * 512:C_MASK + (i + 1) * 512] = -30000.0 * ((i * 128 + idx[:, None]) >= t[None, :]).astype(np.float32)
    tt = idx[:, None]
    to = idx[None, :]
    for g, w in enumerate(WINS):
        win = ((to - tt >= 0) & (to - tt < w)).astype(np.float32)
        band = win / w - np.eye(128)
        C[:, C_BAND + g * 128:C_BAND + (g + 1) * 128] = band
        if parity == 0:
            cnt = np.minimum(to + 1, w).astype(np.float32)
            C[:, C_BAND0 + g * 128:C_BAND0 + (g + 1) * 128] = win / cnt - np.eye(128)
        else:
            C[:, C_BAND0 + g * 128:C_BAND0 + (g + 1) * 128] = band
        C[:, C_BANDP + g * 128:C_BANDP + (g + 1) * 128] = ((to + 128 - tt) < w).astype(np.float32) / w
    return _bf16(C)


def make_in_maps(x, norm1_g, w_in, pool_w, pool_scale, pool_out_g, attn_out_g, w_out, norm2_g, w_up, w_down, final_g):
    f = lambda a: np.ascontiguousarray(np.asarray(a, dtype=np.float32))
    x = f(x)
    shared = {
        "w_in": f(w_in),
        "pool_w": f(np.transpose(np.asarray(pool_w), (1, 0, 2))),
        "w_out": f(w_out), "w_up": f(w_up), "w_down": f(w_down),
        "g1bc": f(np.broadcast_to(np.asarray(norm1_g)[None, :], (128, D))),
        "g2bc": f(np.broadcast_to(np.asarray(norm2_g)[None, :], (128, D))),
        "gfbc": f(np.broadcast_to(np.asarray(final_g)[None, :], (128, D))),
        "pscale": f(np.asarray(pool_scale).reshape(4, 128).T),
        "gout": f(np.concatenate([np.asarray(pool_out_g), np.asarray(attn_out_g)]).reshape(8, 128).T),
    }
    csts = [make_consts(0), make_consts(1)]
    maps = []
    for c in range(8):
        b, p = c // 2, c % 2
        if p == 1:
            xkc = x[b]
        else:
            xkc = np.concatenate([np.zeros((512, D), np.float32), x[b, :S - 512]], axis=0)
        mp = dict(shared)
        mp["xk"] = np.ascontiguousarray(xkc)
        mp["cst"] = csts[p]
        maps.append(mp)
    return maps


_NC_CACHE = {}


def kernel(x, norm1_g, w_in, pool_w, pool_scale, pool_out_g, attn_out_g, w_out, norm2_g, w_up, w_down, final_g):
    if "nc" not in _NC_CACHE:
        _NC_CACHE["nc"] = build_program()
    nc = _NC_CACHE["nc"]
    maps = make_in_maps(x, norm1_g, w_in, pool_w, pool_scale, pool_out_g, attn_out_g, w_out, norm2_g, w_up, w_down, final_g)
    res = run_bass_kernel_spmd(nc, maps, core_ids=list(range(8)))
    out = np.empty((4, S, D), np.float32)
    for c in range(8):
        b, p = c // 2, c % 2
        yc = np.asarray(res.results[c]["y"]).reshape(4, 512, D)
        for m in range(4):
            t0 = (2 * m + p) * 512
            out[b, t0:t0 + 512] = yc[m]
    return out
```

```python
import contextlib
import numpy as np
import ml_dtypes
import concourse.bass as bass
import concourse.mybir as mybir
from concourse.bass_utils import run_bass_kernel_spmd

F32 = mybir.dt.float32
BF16 = mybir.dt.bfloat16
AF = mybir.ActivationFunctionType
ALU = mybir.AluOpType

D = 1024
S = 4096
DFF = 4096
EPS = 1e-6
WINS = (2, 4, 8, 16)
ENGS = ("pe", "act", "dve", "pool", "sp")

C_ID = 0
C_TRI = 128
C_SEL = 256
C_NONE = 384
C_MASK = 448
C_BAND = C_MASK + 4 * 512
C_BAND0 = C_BAND + 512
C_BANDP = C_BAND0 + 512
C_ZERO = C_BANDP + 512
C_END = C_ZERO + 64

ARENA_F32 = 45056
OPTS = {"a_tiles": 8, "a_proj": True, "a_pool": True, "a_tr": True, "a_q": 1, "a_u": 1, "b_slots": 4, "b_pairs": 4, "b_nk": 0, "b_lvl": 10, "b_dummy": 8, "b_dummy2": 0, "b_order": 2}


class Buf:
    __slots__ = ("w", "r")

    def __init__(self):
        self.w = None
        self.r = {}


class Prog:
    def __init__(self, nc, stack):
        self.nc = nc
        self.stack = stack
        self.q = {e: [] for e in ENGS}
        self.cnt = {e: 0 for e in ENGS}
        self.sem = {e: stack.enter_context(nc.semaphore("prog_" + e)) for e in ENGS}
        self.waited = {e: {} for e in ENGS}
        self.nsem = 0
        self.dsems = []

    def _emit_waits(self, eng, deps):
        for d in deps:
            if d is None:
                continue
            kind, key, val = d
            if kind == "eng" and key == "pe" and eng == "pe":
                continue
            ident = key if kind == "eng" else id(key)
            if self.waited[eng].get(ident, 0) >= val:
                continue
            self.waited[eng][ident] = val
            sem = self.sem[key] if kind == "eng" else key
            self.q[eng].append(("wait", sem, val))

    @staticmethod
    def _deps(reads, writes):
        deps = []
        for b in reads:
            if b.w is not None:
                deps.append(b.w)
        for b in writes:
            if b.w is not None:
                deps.append(b.w)
            deps.extend(b.r.values())
        return deps

    @staticmethod
    def _record(tok, reads, writes):
        kind, key, val = tok
        rk = key if kind == "eng" else ("dma", id(key))
        for b in reads:
            b.r[rk] = tok
        for b in writes:
            b.w = tok
            b.r = {}

    def op(self, eng, fn, reads=(), writes=(), deps=()):
        self._emit_waits(eng, list(deps) + self._deps(reads, writes))
        self.cnt[eng] += 1
        self.q[eng].append(("op", fn, self.sem[eng]))
        tok = ("eng", eng, self.cnt[eng])
        self._record(tok, reads, writes)
        return tok

    def new_dma_sem(self):
        self.nsem += 1
        s = self.stack.enter_context(self.nc.semaphore("dsem%d" % self.nsem))
        rec = [s, 0]
        self.dsems.append(rec)
        return rec

    def dma(self, eng, semrec, fn, reads=(), writes=(), deps=()):
        self._emit_waits(eng, list(deps) + self._deps(reads, writes))
        semrec[1] += 16
        self.q[eng].append(("dma", fn, semrec[0]))
        tok = ("dma", semrec[0], semrec[1])
        self._record(tok, reads, writes)
        return tok

    def wait(self, eng, deps):
        self._emit_waits(eng, deps)

    def I(self, eng, method, *args, reads=(), writes=(), deps=(), **kw):
        return self.op(eng, lambda e: getattr(e, method)(*args, **kw), reads, writes, deps)

    def D(self, eng, semrec, out, in_, reads=(), writes=(), deps=()):
        return self.dma(eng, semrec, lambda e: e.dma_start(out=out, in_=in_), reads, writes, deps)

    def emit(self):
        nc = self.nc
        with nc.Block() as block:
            def run(engine, items):
                for it in items:
                    if it[0] == "wait":
                        engine.wait_ge(it[1], it[2])
                    elif it[0] == "op":
                        it[1](engine).then_inc(it[2], 1)
                    else:
                        it[1](engine).then_inc(it[2], 16)

            @block.tensor
            def _(e):
                run(e, self.q["pe"])

            @block.scalar
            def _(e):
                run(e, self.q["act"])

            @block.vector
            def _(e):
                run(e, self.q["dve"])

            @block.gpsimd
            def _(e):
                run(e, self.q["pool"])

            @block.sync
            def _(e):
                run(e, self.q["sp"])


def build_program(phases="ABC", debug=False):
    nc = bass.Bass("TRN2", target_bir_lowering=False)
    dram_in = lambda n, s, dt=F32: nc.dram_tensor(n, s, dt, kind="ExternalInput").ap()
    xk = dram_in("xk", [S, D])
    w_in = dram_in("w_in", [D, 2048])
    pool_w = dram_in("pool_w", [128, 4, 128])
    w_out = dram_in("w_out", [D, D])
    w_up = dram_in("w_up", [D, DFF])
    w_down = dram_in("w_down", [DFF, D])
    g1bc_d = dram_in("g1bc", [128, D])
    g2bc_d = dram_in("g2bc", [128, D])
    gfbc_d = dram_in("gfbc", [128, D])
    pscale_d = dram_in("pscale", [128, 4])
    gout_d = dram_in("gout", [128, 8])
    cst_d = dram_in("cst", [128, C_END], BF16)
    y = nc.dram_tensor("y", [2048, D], F32, kind="ExternalOutput").ap()
    dbg = {}
    if debug:
        dbg["kT"] = nc.dram_tensor("dbg_kT", [128, 4 * S], BF16, kind="ExternalOutput").ap()
        dbg["v"] = nc.dram_tensor("dbg_v", [128, 32 * 512], BF16, kind="ExternalOutput").ap()
        dbg["qT"] = nc.dram_tensor("dbg_qT", [128, 4 * 2048], BF16, kind="ExternalOutput").ap()
        dbg["yp"] = nc.dram_tensor("dbg_yp", [128, 4 * 2048], BF16, kind="ExternalOutput").ap()
        dbg["ya"] = nc.dram_tensor("dbg_ya", [128, 4 * 2048], BF16, kind="ExternalOutput").ap()
        dbg["rr"] = nc.dram_tensor("dbg_rr", [128, 32], F32, kind="ExternalOutput").ap()

    with contextlib.ExitStack() as st:
        P = Prog(nc, st)
        sb = lambda n, s, dt: st.enter_context(nc.sbuf_tensor("s_" + n, s, dt))

        cst = sb("cst", [128, C_END], BF16)
        pscale = sb("pscale", [128, 4], F32)
        gout = sb("gout", [128, 8], F32)
        pwst = sb("pwst", [128, 4, 128], F32)
        pwb = sb("pwb", [128, 4, 128], BF16)
        ones = sb("ones", [128, 2], F32)
        ypT = sb("ypT", [128, 4, 2048], BF16)
        rr = sb("rr", [128, 32], F32)
        stat = sb("stat", [128, 512], F32)
        AR = sb("arena", [128, ARENA_F32], F32)

        def view(off_bytes, nbytes, dt):
            assert off_bytes % 4 == 0 and nbytes % 4 == 0 and off_bytes + nbytes <= ARENA_F32 * 4
            a = AR[:, off_bytes // 4:(off_bytes + nbytes) // 4]
            return a if dt == F32 else a.bitcast(dt)

        KB = 1024
        PSALL = st.enter_context(nc.psum_tensor("psall", [128, 8 * 512], F32))
        ps = [PSALL[:, i * 512:(i + 1) * 512] for i in range(8)]
        psb = [Buf() for _ in range(8)]

        stat_col = [0]

        def new_stat(n):
            c = stat_col[0]
            stat_col[0] += n
            assert stat_col[0] <= 512
            return stat[:, c:c + n], Buf()

        def barrier():
            for e in ENGS:
                deps = [("eng", o, P.cnt[o]) for o in ENGS if P.cnt[o] > 0]
                P.wait(e, deps)

        csem = P.new_dma_sem()
        b_cst = Buf()
        g1bc = view(171 * KB, 4 * KB, F32)
        for dst, src in ((cst[:], cst_d), (g1bc, g1bc_d), (pscale[:], pscale_d), (gout[:], gout_d),
                         (pwst[:], pool_w)):
            P.D("sp", csem, out=dst, in_=src, writes=[b_cst])
        b_pwb = Buf()
        b_ones = Buf()
        P.I("dve", "tensor_copy", out=pwb[:], in_=pwst[:], reads=[b_cst], writes=[b_pwb])
        P.I("dve", "memset", ones[:], 1.0, writes=[b_ones])

        kT = view(0, 32 * KB, BF16).rearrange("p (c t) -> p c t", c=4)
        vv = view(32 * KB, 32 * KB, BF16).rearrange("p (b e) -> p b e", b=32)
        qT = view(64 * KB, 16 * KB, BF16).rearrange("p (c t) -> p c t", c=4)
        yaT = view(80 * KB, 16 * KB, BF16).rearrange("p (c t) -> p c t", c=4)
        b_kT = [Buf() for _ in range(8)]
        b_v = [Buf() for _ in range(8)]
        b_qT = [Buf() for _ in range(4)]
        b_yp = [Buf() for _ in range(4)]
        b_ya = [Buf() for _ in range(4)]
        b_rr = Buf()
        P.I("dve", "memset", rr[:], 1.0, writes=[b_rr])

        def alias(dst, srcs):
            for sbuf in srcs:
                for k, t in list(sbuf.r.items()) + ([(None, sbuf.w)] if sbuf.w is not None else []):
                    key = (t[1] if t[0] == "eng" else ("dma", id(t[1])))
                    if key not in dst.r or dst.r[key][2] < t[2]:
                        dst.r[key] = t

        def phase_a():
            Y = 80 * KB
            w_inb = view(Y, 32 * KB, BF16).rearrange("p (c e) -> p c e", c=8)
            stg = [view(Y + 32 * KB + i * 4 * KB, 4 * KB, F32) for i in range(2)]
            xbs = [view(Y + 40 * KB + i * 4 * KB, 4 * KB, F32) for i in range(3)]
            xh = [view(Y + 52 * KB + i * 2 * KB, 2 * KB, BF16) for i in range(2)]
            xhT = [view(Y + 56 * KB + i * 8 * KB, 8 * KB, BF16).rearrange("p (c t) -> p c t", c=8)
                   for i in range(2)]
            uu = view(Y + 72 * KB, 5 * KB, BF16).rearrange("p (b e) -> p b e", b=5)
            pooledT = view(Y + 77 * KB, 4 * KB, BF16).rearrange("p (g t) -> p g t", g=4)
            sqp = view(Y + 81 * KB, 8 * KB, F32).rearrange("p (g t) -> p g t", g=4)
            junk = view(Y + 89 * KB, 2 * KB, BF16)
            yp32 = view(Y + 32 * KB, 2 * KB, F32)
            stg = stg + [view(Y + 81 * KB + i * 4 * KB, 4 * KB, F32) for i in range(2)]
            b_stg = [Buf() for _ in range(4)]
            b_win = [Buf() for _ in range(8)]
            b_xb = [Buf() for _ in range(3)]
            b_xh = [Buf() for _ in range(2)]
            b_xhT = [Buf() for _ in range(2)]
            b_u = [Buf() for _ in range(5)]
            b_pooled = Buf()
            b_sqp = Buf()
            b_yp32 = b_stg[0]
            b_junk = Buf()
            xsem = [P.new_dma_sem() for _ in range(3)]
            ssem = [P.new_dma_sem() for _ in range(4)]

            w_in_v = w_in.rearrange("(c p) e -> p c e", p=128)
            WIN_ORDER = (2, 3, 0, 1)

            def load_win(i):
                cb, cp = WIN_ORDER[i // 4], i % 4
                s_ = i % 4
                P.D("sp", ssem[s_], out=stg[s_].rearrange("p (c e) -> p c e", c=2),
                    in_=w_in_v[:, cp * 2:cp * 2 + 2, cb * 512:(cb + 1) * 512], writes=[b_stg[s_]])
                eng = ("dve", "act")[i % 2]
                dst = w_inb[:, cp * 2:cp * 2 + 2, cb * 512:(cb + 1) * 512]
                src = stg[s_].rearrange("p (c e) -> p c e", c=2)
                if eng == "act":
                    P.I("act", "copy", out=dst, in_=src, reads=[b_stg[s_]], writes=[b_win[cb]])
                else:
                    P.I(eng, "tensor_copy", out=dst, in_=src, reads=[b_stg[s_]], writes=[b_win[cb]])

            def load_x(g):
                xb = xbs[g % 3]
                P.D("sp", xsem[g % 3], out=xb, in_=xk[g * 128:(g + 1) * 128, :],
                      writes=[b_xb[g % 3]])

            pacc_rr = [0]

            def pacc():
                i = 2 + pacc_rr[0] % 6
                pacc_rr[0] += 1
                return ps[i], psb[i]

            evac_rr = [0]

            def evac(out_ap, in_ap, reads, writes, scale=None):
                evac_rr[0] ^= 1
                if scale is not None and OPTS["a_q"] == 3:
                    return P.I("act", "mul", out=out_ap, in_=in_ap, mul=scale, reads=reads, writes=writes)
                if scale is not None and OPTS["a_q"] == 4:
                    return P.I("dve", "tensor_scalar", out=out_ap, in0=in_ap, scalar1=scale, scalar2=None,
                               op0=ALU.mult, reads=reads, writes=writes)
                if scale is not None and OPTS["a_q"] == 5:
                    return P.I("act", "activation", out=out_ap, in_=in_ap, func=AF.Copy, scale=scale, reads=reads, writes=writes)
                if evac_rr[0]:
                    if scale is None:
                        return P.I("act", "copy", out=out_ap, in_=in_ap, reads=reads, writes=writes)
                    return P.I("act", "activation", out=out_ap, in_=in_ap, func=AF.Copy, scale=scale,
                                reads=reads, writes=writes)
                if scale is None:
                    return P.I("dve", "tensor_copy", out=out_ap, in_=in_ap, reads=reads, writes=writes)
                return P.I("dve", "tensor_scalar", out=out_ap, in0=in_ap, scalar1=scale, scalar2=None,
                                                             op0=ALU.mult, reads=reads, writes=writes)

            def norm_front(g, kt, j):
                xb, bxb = xbs[g % 3], b_xb[g % 3]
                ss, bss = new_stat(3)
                P.I("act", "activation", out=junk, in_=xb, func=AF.Square, accum_out=ss[:, 0:1],
                     reads=[bxb], writes=[b_junk, bss])
                P.I("act", "activation", out=ss[:, 1:2], in_=ss[:, 0:1], func=AF.Sqrt, scale=1.0 / D, bias=EPS,
                     reads=[bss], writes=[bss])
                P.I("dve", "reciprocal", out=ss[:, 2:3], in_=ss[:, 1:2], reads=[bss], writes=[bss])
                xhb, bxh = xh[g % 2], b_xh[g % 2]
                P.I("dve", "scalar_tensor_tensor", out=xhb, in0=xb, scalar=ss[:, 2:3], in1=g1bc,
                                                             op0=ALU.mult, op1=ALU.mult,
                     reads=[bxb, bss, b_cst], writes=[bxh])

            def norm_back(g, kt, j):
                xhb, bxh = xh[g % 2], b_xh[g % 2]
                pt = ps[g % 2][:].bitcast(BF16).rearrange("p (c t) -> p c t", c=8)
                for c in range(8):
                    P.I("pe", "transpose", out=pt[:, c, :], in_=xhb[:, c * 128:(c + 1) * 128], identity=cst[:, C_ID:C_ID + 128],
                         reads=[bxh, b_cst], writes=[psb[g % 2]])
                evac(xhT[kt % 2][:, :, j * 128:(j + 1) * 128], pt, [psb[g % 2]], [b_xhT[kt % 2]])

            def proj_T(dst_ap, dst_buf, col0, xT, bxT, scale=None):
                pa, pb = pacc()
                for c in range(8):
                    P.I("pe", "matmul", pa[:], lhsT=w_inb[:, c, col0:col0 + 128], rhs=xT[:, c, :],
                                                       start=(c == 0), stop=(c == 7),
                         reads=[b_win[col0 // 512], bxT], writes=[pb])
                evac(dst_ap, pa[:], [pb], [dst_buf], scale=scale)

            def proj_tok(dst_ap, dst_buf, col0, xT, bxT, j):
                pa, pb = pacc()
                for c in range(8):
                    P.I("pe", "matmul", pa[:], lhsT=xT[:, c, j * 128:(j + 1) * 128], rhs=w_inb[:, c, col0:col0 + 512],
                                                       start=(c == 0), stop=(c == 7),
                         reads=[b_win[col0 // 512], bxT], writes=[pb])
                evac(dst_ap, pa[:], [pb], [dst_buf])

            NBLK = 32
            nx = 0
            for i in range(3):
                load_x(nx)
                nx += 1
            for i in range(4):
                load_win(i)

            def tile_items(kt):
                own = kt % 2 == 1
                m = kt // 2
                xT, bxT = xhT[kt % 2], b_xhT[kt % 2]
                items = []
                for ec in range(4):
                    items.append(lambda ec=ec: proj_T(kT[:, ec, kt * 512:(kt + 1) * 512], b_kT[kt], 1024 + ec * 128, xT, bxT))
                for j in range(4):
                    items.append(lambda j=j: proj_tok(vv[:, kt * 4 + j, :], b_v[kt], 1536, xT, bxT, j))
                if not own:
                    items.append(lambda: proj_tok(uu[:, 0, :], b_u[0], 0, xT, bxT, 3))
                else:
                    for ec in range(4):
                        items.append(lambda ec=ec: proj_T(qT[:, ec, m * 512:(m + 1) * 512], b_qT[m], 512 + ec * 128, xT, bxT, scale=0.125))
                    for j in range(4):
                        items.append(lambda j=j: proj_tok(uu[:, 1 + j, :], b_u[1 + j], 0, xT, bxT, j))
                return items

            def next_x():
                if nxs[0] < NBLK:
                    load_x(nxs[0])
                    nxs[0] += 1

            nxs = [nx]
            for j in range(4):
                norm_front(j, 0, j)
                norm_back(j, 0, j)
                next_x()
            for i in range(4, 16):
                load_win(i)
            alias(b_sqp, [b_stg[2], b_stg[3]])
            for kt in range(8):
                own = kt % 2 == 1
                m = kt // 2
                items = tile_items(kt)
                n = len(items)
                for q in range(4):
                    if kt + 1 < 8:
                        norm_front((kt + 1) * 4 + q, kt + 1, q)
                    for it in items[q * n // 4:(q + 1) * n // 4]:
                        it()
                    if kt + 1 < 8:
                        norm_back((kt + 1) * 4 + q, kt + 1, q)
                        next_x()
                if not own:
                    continue
                if not OPTS["a_pool"]:
                    continue
                ssq, bssq = new_stat(8)
                for g in range(4):
                    pa, pb = pacc()
                    for j in range(4):
                        bcol = (C_BAND0 if (m == 0 and j == 0) else C_BAND) + g * 128
                        P.I("pe", "matmul", pa[:, j * 128:(j + 1) * 128], lhsT=uu[:, 1 + j, g * 128:(g + 1) * 128],
                                                                      rhs=cst[:, bcol:bcol + 128], start=True, stop=False,
                             reads=[b_u[1 + j], b_cst], writes=[pb])
                        P.I("pe", "matmul", pa[:, j * 128:(j + 1) * 128], lhsT=uu[:, j, g * 128:(g + 1) * 128],
                                                           rhs=cst[:, C_BANDP + g * 128:C_BANDP + (g + 1) * 128], start=False, stop=True,
                             reads=[b_u[j], b_cst], writes=[pb])
                    evac(pooledT[:, g, :], pa[:], [pb], [b_pooled])
                    pm, pmb = pacc()
                    P.I("pe", "matmul", pm[:], lhsT=pwb[:, g, :], rhs=pooledT[:, g, :], start=True, stop=True,
                         reads=[b_pwb, b_pooled], writes=[pmb])
                    P.I("dve", "tensor_scalar", out=yp32, in0=pm[:], scalar1=pscale[:, g:g + 1], scalar2=None, op0=ALU.mult,
                        reads=[pmb, b_cst], writes=[b_yp32])
                    P.I("pool", "tensor_copy", out=ypT[:, g, m * 512:(m + 1) * 512], in_=yp32, reads=[b_yp32], writes=[b_yp[m]])
                    P.I("act", "activation", out=sqp[:, g, :], in_=yp32, func=AF.Square, reads=[b_yp32], writes=[b_sqp])
                if OPTS["a_pool"] != 1:
                    continue
                pq, pqb = pacc()
                for j in range(4):
                    for g in range(4):
                        P.I("pe", "matmul", pq[:, j:j + 1], lhsT=sqp[:, g, j * 128:(j + 1) * 128], rhs=ones[:, 0:1],
                                                                start=(g == 0), stop=(g == 3),
                             reads=[b_sqp, b_ones], writes=[pqb])
                P.I("act", "activation", out=ssq[:, 0:4], in_=pq[:, 0:4], func=AF.Sqrt, scale=1.0 / 512, bias=EPS,
                     reads=[pqb], writes=[bssq])
                P.I("dve", "reciprocal", out=rr[:, m * 4:(m + 1) * 4], in_=ssq[:, 0:4], reads=[bssq], writes=[b_rr])

        c_stg = [view(128 * KB + i * 8 * KB, 8 * KB, F32) for i in range(2)]
        c_w_outb = view(144 * KB, 16 * KB, BF16).rearrange("p (c d) -> p c d", c=8)
        c_g2bc = view(160 * KB, 4 * KB, F32)
        c_b_wout = Buf()
        c_b_stg = [Buf() for _ in range(2)]
        c_b_g = Buf()
        c_ssem = [P.new_dma_sem() for _ in range(2)]
        c_stg_rr = [0]

        def prefetch_c():
            gsem0 = P.new_dma_sem()
            P.D("sp", gsem0, c_g2bc, g2bc_d, writes=[c_b_g])
            w_out_v = w_out.rearrange("(c p) d -> p c d", p=128)
            for gi in range(4):
                s_ = c_stg_rr[0] % 2
                c_stg_rr[0] += 1
                P.D("sp", c_ssem[s_], c_stg[s_].rearrange("p (c d) -> p c d", c=2), w_out_v[:, gi * 2:gi * 2 + 2, :], writes=[c_b_stg[s_]])
                sv = c_stg[s_].rearrange("p (c d) -> p c d", c=2)
                for cc in range(2):
                    c = gi * 2 + cc
                    P.I("dve", "tensor_scalar", out=c_w_outb[:, c, :], in0=sv[:, cc, :], scalar1=gout[:, c:c + 1],
                        scalar2=None, op0=ALU.mult, reads=[c_b_stg[s_], b_cst], writes=[c_b_wout])

        def phase_b():
            Y = 96 * KB
            Eb = [view(Y + i * 4 * KB, 4 * KB, F32) for i in range(2)]
            Lb = [view(Y + 8 * KB + i * 2 * KB, 2 * KB, BF16) for i in range(2)]
            Ab = [view(Y + 12 * KB + i * 2 * KB, 2 * KB, BF16) for i in range(2)]
            Rt = [view(Y + 16 * KB + i * KB, KB, BF16) for i in range(4)]
            sqa = view(Y + 20 * KB, 8 * KB, F32).rearrange("p (c t) -> p c t", c=4)
            o32s = [view(Y + 28 * KB + i * 2 * KB, 2 * KB, F32) for i in range(2)]
            pending = []
            b_E = [Buf() for _ in range(2)]
            b_L = [Buf() for _ in range(2)]
            b_A = [Buf() for _ in range(2)]
            b_Rt = [Buf() for _ in range(4)]
            b_sqa = Buf()
            b_o32s = [Buf(), Buf()]
            for i in range(4):
                P.I("pool", "memset", Rt[i], 0.0, writes=[b_Rt[i]])
            def zero_masked(q):
                P.I("pool", "memset", Lb[q].rearrange("p (h t) -> p h t", h=2)[:, :, 0:384], 0.0, writes=[b_L[q]])
                P.I("pool", "memset", Ab[q].rearrange("p (h t) -> p h t", h=2)[:, :, 0:384], 0.0, writes=[b_A[q]])

            for q in range(2):
                zero_masked(q)
            mhalf, b_mhalf = new_stat(4)
            P.I("pool", "memset", mhalf, -0.5, writes=[b_mhalf])
            for i in (6, 7):
                P.I("dve", "memset", ps[i], 0.0, writes=[psb[i]])
            b_z = [[Buf(), Buf()] for _ in range(2)]
            b_ops = [[Buf(), Buf()] for _ in range(2)]
            b_rps = [[Buf(), Buf()] for _ in range(2)]
            for q in range(2):
                for hp in range(2):
                    b_z[q][hp].w, b_z[q][hp].r = psb[2 * q + hp].w, dict(psb[2 * q + hp].r)
                    b_ops[q][hp].w, b_ops[q][hp].r = psb[4 + q].w, dict(psb[4 + q].r)
                    b_rps[q][hp].w, b_rps[q][hp].r = psb[6 + q].w, dict(psb[6 + q].r)

            for m in range(4):
                nk = 8 * (m + 1)
                for dp in range(2):
                    def QK(q, tau):
                        ec = dp * 2 + q
                        kb = nk - 1 - tau
                        di = kb - (nk - 4)
                        for hp in range(2):
                            z = ps[2 * q + hp]
                            pr = slice(hp * 64, hp * 64 + 64)
                            P.I("pe", "matmul", z, lhsT=kT[pr, ec, kb * 128:(kb + 1) * 128], rhs=qT[pr, ec, m * 512:(m + 1) * 512],
                                start=True, stop=(di < 0), reads=[b_kT[kb // 4], b_qT[m]], writes=[b_z[q][hp]])
                        if di >= 0:
                            for hp in range(2):
                                P.I("pe", "matmul", ps[2 * q + hp], lhsT=cst[:, C_ID:C_ID + 128],
                                    rhs=cst[:, C_MASK + di * 512:C_MASK + (di + 1) * 512],
                                    start=False, stop=True, reads=[b_cst], writes=[b_z[q][hp]])

                    def zpair(q):
                        return PSALL[:, 2 * q * 512:(2 * q + 2) * 512]

                    def cols(ap, tau):
                        c0 = max(0, 3 - tau) * 128
                        if c0 == 0:
                            return ap
                        return ap.rearrange("p (h t) -> p h t", h=2)[:, :, c0:512]

                    def EXP1(q, tau):
                        P.I("act", "activation", out=cols(Eb[q], tau), in_=cols(zpair(q), tau), func=AF.Exp, reads=b_z[q], writes=[b_E[q]])

                    def LN(q, tau):
                        P.I("act", "activation", out=cols(Lb[q], tau), in_=cols(Eb[q], tau), func=AF.Ln, bias=1.0, reads=[b_E[q]], writes=[b_L[q]])

                    def TRISEL(q, tau):
                        ri = q * 2 + tau % 2
                        for hp in range(2):
                            z = ps[2 * q + hp]
                            P.I("pe", "matmul", z, lhsT=cst[:, C_TRI:C_TRI + 128], rhs=Lb[q][:, hp * 512:(hp + 1) * 512],
                                start=False, stop=True, skip_group_check=True, reads=[b_L[q], b_cst], writes=[b_z[q][hp]])
                            if tau > 0 and OPTS["b_lvl"] != 10:
                                r0 = hp * 64
                                P.I("pe", "matmul", z, lhsT=cst[r0:r0 + 33, C_SEL:C_SEL + 128], rhs=Rt[ri][r0:r0 + 33, :],
                                    start=False, stop=True, skip_group_check=True, reads=[b_Rt[ri], b_cst], writes=[b_z[q][hp]])
                        if tau > 0 and OPTS["b_lvl"] == 10:
                            for hp in range(2):
                                r0 = hp * 64
                                P.I("pe", "matmul", ps[2 * q + hp], lhsT=cst[r0:r0 + 33, C_SEL:C_SEL + 128], rhs=Rt[ri][r0:r0 + 33, :],
                                    start=False, stop=True, skip_group_check=True, reads=[b_Rt[ri], b_cst], writes=[b_z[q][hp]])

                    def COL(q, tau):
                        if tau >= nk - 1:
                            return
                        rn = q * 2 + (tau + 1) % 2
                        RPS = ps[6 + q]
                        for hp in range(2):
                            r0 = hp * 64
                            P.I("pe", "matmul", RPS[r0:r0 + 33, :], lhsT=cst[:, C_NONE:C_NONE + 33], rhs=Lb[q][:, hp * 512:(hp + 1) * 512],
                                start=(tau == 0), stop=True, skip_group_check=(tau > 0), reads=[b_L[q], b_cst], writes=[b_rps[q][hp]])
                        P.I("dve", "tensor_copy", out=Rt[rn][0:97, :], in_=RPS[0:97, :], reads=b_rps[q], writes=[b_Rt[rn]])
                        for hp in range(2):
                            r1 = hp * 64 + 32
                            P.I("dve", "tensor_tensor", out=Rt[rn][r1:r1 + 1, :], in0=RPS[r1:r1 + 1, :], in1=Rt[rn][r1:r1 + 1, :],
                                op=ALU.subtract, reads=[b_rps[q][hp], b_Rt[rn]], writes=[b_Rt[rn]])

                    def EXP2(q, tau):
                        P.I("act", "activation", out=cols(Ab[q], tau), in_=cols(zpair(q), tau), func=AF.Exp, reads=b_z[q], writes=[b_A[q]])

                    def AV(q, tau):
                        kb = nk - 1 - tau
                        OPS = ps[4 + q]
                        for hp in range(2):
                            h = (dp * 2 + q) * 2 + hp
                            pr = slice(hp * 64, hp * 64 + 64)
                            P.I("pe", "matmul", OPS[pr, :], lhsT=vv[:, kb, h * 64:(h + 1) * 64], rhs=Ab[q][:, hp * 512:(hp + 1) * 512],
                                start=(tau == 0), stop=(tau == nk - 1), reads=[b_v[kb // 4], b_A[q]], writes=[b_ops[q][hp]])

                    def WARM(q, tau, n):
                        if tau == 0 or tau == nk - 1:
                            return
                        for _ in range(n):
                            P.I("pe", "matmul", ps[4 + q][0:64, :], lhsT=cst[:, C_ZERO:C_ZERO + 64], rhs=cst[:, C_MASK:C_MASK + 512],
                                start=False, stop=False, reads=[b_cst], writes=[b_ops[q][0]])

                    for q in range(2):
                        QK(q, 0)
                    for tau in range(nk):
                        if tau == 2 and pending:
                            for fn in pending:
                                fn()
                            del pending[:]
                        if OPTS["b_order"] == 0:
                            EXP1(0, tau)
                            LN(0, tau)
                            TRISEL(0, tau)
                            EXP1(1, tau)
                            LN(1, tau)
                            TRISEL(1, tau)
                            COL(0, tau)
                            COL(1, tau)
                            WARM(0, tau, OPTS["b_dummy2"])
                            EXP2(0, tau)
                            AV(0, tau)
                            if tau + 1 < nk:
                                QK(0, tau + 1)
                            EXP2(1, tau)
                            AV(1, tau)
                            if tau + 1 < nk:
                                QK(1, tau + 1)
                            WARM(1, tau, OPTS["b_dummy"])
                        elif OPTS["b_order"] == 2:
                            EXP1(0, tau)
                            EXP1(1, tau)
                            LN(0, tau)
                            TRISEL(0, tau)
                            LN(1, tau)
                            TRISEL(1, tau)
                            COL(0, tau)
                            COL(1, tau)
                            WARM(0, tau, OPTS["b_dummy2"])
                            EXP2(0, tau)
                            AV(0, tau)
                            if tau + 1 < nk:
                                QK(0, tau + 1)
                            else:
                                zero_masked(0)
                            EXP2(1, tau)
                            AV(1, tau)
                            if tau + 1 < nk:
                                QK(1, tau + 1)
                            else:
                                zero_masked(1)
                            WARM(1, tau, OPTS["b_dummy"])
                        else:
                            EXP1(0, tau)
                            EXP1(1, tau)
                            WARM(0, tau, OPTS["b_dummy"])
                            LN(0, tau)
                            TRISEL(0, tau)
                            COL(0, tau)
                            LN(1, tau)
                            TRISEL(1, tau)
                            COL(1, tau)
                            WARM(1, tau, OPTS["b_dummy2"])
                            EXP2(0, tau)
                            AV(0, tau)
                            if tau + 1 < nk:
                                QK(0, tau + 1)
                            EXP2(1, tau)
                            AV(1, tau)
                            if tau + 1 < nk:
                                QK(1, tau + 1)
                    for q in range(2):
                        ec = dp * 2 + q
                        P.I("dve", "tensor_copy", out=o32s[q], in_=ps[4 + q], reads=b_ops[q], writes=[b_o32s[q]])
                        P.I("pool", "tensor_copy", out=yaT[:, ec, m * 512:(m + 1) * 512], in_=o32s[q], reads=[b_o32s[q]], writes=[b_ya[m]])
                        pending.append(lambda q=q, ec=ec: P.I("act", "activation", out=sqa[:, ec, :], in_=o32s[q], func=AF.Square,
                                                              reads=[b_o32s[q]], writes=[b_sqa]))
                def slot_stats(m=m):
                    ssq, bssq = new_stat(4)
                    RPS = ps[6]
                    for j in range(4):
                        for ec in range(4):
                            P.I("pe", "matmul", RPS[:, j:j + 1], lhsT=sqa[:, ec, j * 128:(j + 1) * 128], rhs=ones[:, 0:1],
                                start=(ec == 0), stop=(ec == 3), reads=[b_sqa, b_ones], writes=b_rps[0])
                    P.I("dve", "tensor_copy", out=ssq[:, 0:4], in_=RPS[:, 0:4], reads=b_rps[0], writes=[bssq])
                    P.I("pool", "tensor_scalar", out=ssq[:, 0:4], in0=ssq[:, 0:4], scalar1=1.0 / 512, scalar2=EPS, op0=ALU.mult, op1=ALU.add,
                        reads=[bssq], writes=[bssq])
                    P.I("pool", "tensor_tensor", out=rr[:, 16 + m * 4:16 + (m + 1) * 4], in0=ssq[:, 0:4], in1=mhalf[:, 0:4], op=ALU.pow,
                        reads=[bssq, b_mhalf], writes=[b_rr])
                for fn in pending:
                    fn()
                del pending[:]
                slot_stats()

        def phase_c():
            hb = view(0, 64 * KB, F32).rearrange("p (j d) -> p j d", j=16)
            wupb = [view(64 * KB + i * 8 * KB, 8 * KB, BF16).rearrange("p (c f) -> p c f", c=8) for i in range(2)]
            hT = view(96 * KB, 32 * KB, BF16).rearrange("p (c t) -> p c t", c=8)
            stg = c_stg
            w_outb = c_w_outb
            wdnb = [view(144 * KB + i * 8 * KB, 8 * KB, BF16).rearrange("p (c d) -> p c d", c=4) for i in range(2)]
            g2bc = c_g2bc
            hh = [view(164 * KB + i * 2 * KB, 2 * KB, BF16) for i in range(2)]
            junk = view(168 * KB, 2 * KB, BF16)
            actG = [view(160 * KB, 16 * KB, BF16).rearrange("p (f t) -> p f t", f=4),
                    view(80 * KB, 16 * KB, BF16).rearrange("p (f t) -> p f t", f=4)]
            ypf = ypT[:].rearrange("p c t -> p (c t)").bitcast(F32)
            gfbc = ypf[:, 0:1024]
            sqs = [ypf[:, 1024 + i * 512:1024 + (i + 1) * 512] for i in range(2)]
            junk2 = ypf[:, 2048:2560].bitcast(BF16)
            b_wout = c_b_wout
            b_stg = c_b_stg
            b_junk = Buf()
            b_junk2 = Buf()
            b_wup = [Buf() for _ in range(2)]
            b_wdn = [Buf() for _ in range(2)]
            b_g = c_b_g
            b_gf = Buf()
            b_hb = [Buf() for _ in range(16)]
            b_hh = [Buf() for _ in range(2)]
            b_hT = [Buf() for _ in range(4)]
            b_act = [Buf() for _ in range(2)]
            b_sqs = [Buf() for _ in range(2)]
            ssem = c_ssem
            hsem = [P.new_dma_sem() for _ in range(4)]
            osem = [P.new_dma_sem() for _ in range(4)]
            gsem = P.new_dma_sem()
            stg_rr = c_stg_rr
            cast_rr = [0]

            for m in range(4):
                src = xk[(2 * m + 1) * 512:(2 * m + 2) * 512, :].rearrange("(j p) d -> p j d", p=128)
                P.D("sp", hsem[m], hb[:, m * 4:(m + 1) * 4, :], src, writes=[b_hb[m * 4 + j] for j in range(4)])

            def stage_load(src_ap, pattern, **kw):
                s_ = stg_rr[0] % 2
                stg_rr[0] += 1
                P.D("sp", ssem[s_], stg[s_].rearrange(pattern, **kw), src_ap, writes=[b_stg[s_]])
                return s_

            def outproj(blk):
                m = blk // 4
                bh = b_hb[blk]
                for dh in range(2):
                    k = (blk * 2 + dh) % 2
                    p1, p1b = ps[k * 2], psb[k * 2]
                    p2, p2b = ps[k * 2 + 1], psb[k * 2 + 1]
                    for ec in range(4):
                        P.I("pe", "matmul", p1[:], lhsT=ypT[:, ec, blk * 128:(blk + 1) * 128], rhs=w_outb[:, ec, dh * 512:(dh + 1) * 512],
                            start=(ec == 0), stop=(ec == 3), reads=[b_yp[m], b_wout], writes=[p1b])
                    for ec in range(4):
                        P.I("pe", "matmul", p2[:], lhsT=yaT[:, ec, blk * 128:(blk + 1) * 128], rhs=w_outb[:, 4 + ec, dh * 512:(dh + 1) * 512],
                            start=(ec == 0), stop=(ec == 3), reads=[b_ya[m], b_wout], writes=[p2b])
                    hs = hb[:, blk, dh * 512:(dh + 1) * 512]
                    P.I("dve", "scalar_tensor_tensor", out=hs, in0=p1[:], scalar=rr[:, blk:blk + 1], in1=hs, op0=ALU.mult, op1=ALU.add,
                        reads=[p1b, b_rr, bh], writes=[bh])
                    P.I("dve", "scalar_tensor_tensor", out=hs, in0=p2[:], scalar=rr[:, 16 + blk:17 + blk], in1=hs, op0=ALU.mult, op1=ALU.add,
                        reads=[p2b, b_rr, bh], writes=[bh])

            def norm2(blk):
                m = blk // 4
                bh = b_hb[blk]
                ss, bss = new_stat(3)
                P.I("act", "activation", out=junk, in_=hb[:, blk, :], func=AF.Square, accum_out=ss[:, 0:1], reads=[bh], writes=[b_junk, bss])
                P.I("act", "activation", out=ss[:, 1:2], in_=ss[:, 0:1], func=AF.Sqrt, scale=1.0 / D, bias=EPS, reads=[bss], writes=[bss])
                P.I("dve", "reciprocal", out=ss[:, 2:3], in_=ss[:, 1:2], reads=[bss], writes=[bss])
                hhb, bhh = hh[blk % 2], b_hh[blk % 2]
                P.I("dve", "scalar_tensor_tensor", out=hhb, in0=hb[:, blk, :], scalar=ss[:, 2:3], in1=g2bc, op0=ALU.mult, op1=ALU.mult,
                    reads=[bh, bss, b_g], writes=[bhh])

            def norm2_back(blk):
                m = blk // 4
                hhb, bhh = hh[blk % 2], b_hh[blk % 2]
                pi = 4 + blk % 2
                pt = ps[pi][:].bitcast(BF16).rearrange("p (c t) -> p c t", c=8)
                for c in range(8):
                    P.I("pe", "transpose", out=pt[:, c, :], in_=hhb[:, c * 128:(c + 1) * 128], identity=cst[:, C_ID:C_ID + 128],
                        reads=[bhh, b_cst], writes=[psb[pi]])
                P.I("act", "copy", out=hT[:, :, blk * 128:(blk + 1) * 128], in_=pt, reads=[psb[pi]], writes=[b_hT[m]])

            for b in range(18):
                if b < 16:
                    outproj(b)
                if 0 <= b - 1 < 16:
                    norm2(b - 1)
                if 0 <= b - 2 < 16:
                    norm2_back(b - 2)

            w_up_v = w_up.rearrange("(c p) f -> p c f", p=128)
            w_dn_v = w_down.rearrange("(fc p) d -> p fc d", p=128)

            def cast(dst, src, reads, writes):
                cast_rr[0] += 1
                if cast_rr[0] % 2:
                    P.I("act", "copy", out=dst, in_=src, reads=reads, writes=writes)
                else:
                    P.I("pool", "tensor_copy", out=dst, in_=src, reads=reads, writes=writes)

            def load_w(G):
                for half in range(2):
                    f0 = G * 512 + half * 256
                    s = stage_load(w_up_v[:, :, f0:f0 + 256], "p (c f) -> p c f", c=8)
                    cast(wupb[G % 2][:, :, half * 256:(half + 1) * 256], stg[s].rearrange("p (c f) -> p c f", c=8), [b_stg[s]], [b_wup[G % 2]])
                for half in range(2):
                    fc0 = G * 4 + half * 2
                    s = stage_load(w_dn_v[:, fc0:fc0 + 2, :], "p (c d) -> p c d", c=2)
                    cast(wdnb[G % 2][:, half * 2:(half + 1) * 2, :], stg[s].rearrange("p (c d) -> p c d", c=2), [b_stg[s]], [b_wdn[G % 2]])

            up_rr = [0]

            def up(G):
                wb, bw = wupb[G % 2], b_wup[G % 2]
                for f4 in range(4):
                    for tt in range(4):
                        pi = up_rr[0] % 4
                        up_rr[0] += 1
                        pu, pub = ps[pi], psb[pi]
                        for c in range(8):
                            P.I("pe", "matmul", pu[:], lhsT=wb[:, c, f4 * 128:(f4 + 1) * 128], rhs=hT[:, c, tt * 512:(tt + 1) * 512],
                                start=(c == 0), stop=(c == 7), reads=[bw, b_hT[tt]], writes=[pub])
                        sq, bsq = sqs[pi % 2], b_sqs[pi % 2]
                        P.I("act", "copy", out=sq, in_=pu[:], reads=[pub], writes=[bsq])
                        P.I("dve", "scalar_tensor_tensor", out=actG[G % 2][:, f4, tt * 512:(tt + 1) * 512], in0=pu[:], scalar=0.0, in1=sq,
                            op0=ALU.max, op1=ALU.mult, reads=[pub, bsq], writes=[b_act[G % 2]])

            dn_rr = [0]

            def down(G, last=False):
                wb, bw = wdnb[G % 2], b_wdn[G % 2]
                for blk in range(16):
                    if last and blk > 0:
                        final_norm(blk - 1)
                    for dh in range(2):
                        pi = 4 + dn_rr[0] % 4
                        dn_rr[0] += 1
                        for f4 in range(4):
                            P.I("pe", "matmul", ps[pi][:], lhsT=actG[G % 2][:, f4, blk * 128:(blk + 1) * 128], rhs=wb[:, f4, dh * 512:(dh + 1) * 512],
                                start=(f4 == 0), stop=(f4 == 3), reads=[bw, b_act[G % 2]], writes=[psb[pi]])
                        hs = hb[:, blk, dh * 512:(dh + 1) * 512]
                        P.I("dve", "tensor_tensor", out=hs, in0=ps[pi][:], in1=hs, op=ALU.add, reads=[psb[pi], b_hb[blk]], writes=[b_hb[blk]])
                if last:
                    final_norm(15)

            def final_norm(blk):
                m = blk // 4
                bh = b_hb[blk]
                ss, bss = new_stat(3)
                P.I("act", "activation", out=junk2, in_=hb[:, blk, :], func=AF.Square, accum_out=ss[:, 0:1], reads=[bh], writes=[b_junk2, bss])
                P.I("act", "activation", out=ss[:, 1:2], in_=ss[:, 0:1], func=AF.Sqrt, scale=1.0 / D, bias=EPS, reads=[bss], writes=[bss])
                P.I("dve", "reciprocal", out=ss[:, 2:3], in_=ss[:, 1:2], reads=[bss], writes=[bss])
                P.I("dve", "scalar_tensor_tensor", out=hb[:, blk, :], in0=hb[:, blk, :], scalar=ss[:, 2:3], in1=gfbc, op0=ALU.mult, op1=ALU.mult,
                    reads=[bh, bss, b_gf], writes=[bh])
                if blk % 4 == 3:
                    dst = y[m * 512:(m + 1) * 512, :].rearrange("(j p) d -> p j d", p=128)
                    P.D("sp", osem[m], dst, hb[:, m * 4:(m + 1) * 4, :], reads=[b_hb[m * 4 + j] for j in range(4)])

            alias(b_act[0], [b_g, b_hh[0], b_hh[1], b_junk])
            alias(b_act[1], b_ya)
            for bb in b_wdn:
                alias(bb, [b_wout])
            for bb in [b_gf, b_junk2] + b_sqs:
                alias(bb, b_yp)
            P.D("sp", gsem, gfbc, gfbc_d, writes=[b_gf])
            NG = 8
            load_w(0)
            up(0)
            for G in range(NG):
                if G + 1 < NG:
                    load_w(G + 1)
                    up(G + 1)
                down(G, last=(G == NG - 1))

        if "A" in phases:
            phase_a()
        if "B" in phases:
            barrier()
            prefetch_c()
            phase_b()
        if "C" in phases:
            if "B" not in phases:
                prefetch_c()
            barrier()
            phase_c()

        if debug:
            barrier()
            dsem = P.new_dma_sem()
            tk = None
            for name, src in (("kT", view(0, 32 * KB, BF16)), ("v", view(32 * KB, 32 * KB, BF16)),
                              ("qT", view(64 * KB, 16 * KB, BF16)), ("yp", ypT[:].rearrange("p c t -> p (c t)")),
                              ("ya", view(80 * KB, 16 * KB, BF16)), ("rr", rr[:])):
                tk = P.D("sp", dsem, out=dbg[name], in_=src)
            P.wait("sp", [tk])
        P.wait("sp", [("dma", r[0], r[1]) for r in P.dsems if r[1] > 0])
        P.emit()
    return nc


def _bf16(a):
    return a.astype(ml_dtypes.bfloat16)


def make_consts(parity):
    C = np.zeros((128, C_END), np.float32)
    idx = np.arange(128)
    C[:, C_ID:C_ID + 128] = np.eye(128)
    C[:, C_TRI:C_TRI + 128] = -(idx[:, None] >= idx[None, :]).astype(np.float32)
    for r in (0, 32, 64, 96):
        C[r, C_SEL:C_SEL + 128] = 1.0
    C[:, C_NONE:C_NONE + 64] = -1.0
    t = np.arange(512)
    for i in range(4):
        C[:, C_MASK + i * 512:C_MASK + (i + 1) * 512] = -30000.0 * ((i * 128 + idx[:, None]) >= t[None, :]).astype(np.float32)
    tt = idx[:, None]
    to = idx[None, :]
    for g, w in enumerate(WINS):
        win = ((to - tt >= 0) & (to - tt < w)).astype(np.float32)
        band = win / w - np.eye(128)
        C[:, C_BAND + g * 128:C_BAND + (g + 1) * 128] = band
        if parity == 0:
            cnt = np.minimum(to + 1, w).astype(np.float32)
            C[:, C_BAND0 + g * 128:C_BAND0 + (g + 1) * 128] = win / cnt - np.eye(128)
        else:
            C[:, C_BAND0 + g * 128:C_BAND0 + (g + 1) * 128] = band
        C[:, C_BANDP + g * 128:C_BANDP + (g + 1) * 128] = ((to + 128 - tt) < w).astype(np.float32) / w
    return _bf16(C)


def make_in_maps(x, norm1_g, w_in, pool_w, pool_scale, pool_out_g, attn_out_g, w_out, norm2_g, w_up, w_down, final_g):
    f = lambda a: np.ascontiguousarray(np.asarray(a, dtype=np.float32))
    x = f(x)
    shared = {
        "w_in": f(w_in),
        "pool_w": f(np.transpose(np.asarray(pool_w), (1, 0, 2))),
        "w_out": f(w_out), "w_up": f(w_up), "w_down": f(w_down),
        "g1bc": f(np.broadcast_to(np.asarray(norm1_g)[None, :], (128, D))),
        "g2bc": f(np.broadcast_to(np.asarray(norm2_g)[None, :], (128, D))),
        "gfbc": f(np.broadcast_to(np.asarray(final_g)[None, :], (128, D))),
        "pscale": f(np.asarray(pool_scale).reshape(4, 128).T),
        "gout": f(np.concatenate([np.asarray(pool_out_g), np.asarray(attn_out_g)]).reshape(8, 128).T),
    }
    csts = [make_consts(0), make_consts(1)]
    maps = []
    for c in range(8):
        b, p = c // 2, c % 2
        if p == 1:
            xkc = x[b]
        else:
            xkc = np.concatenate([np.zeros((512, D), np.float32), x[b, :S - 512]], axis=0)
        mp = dict(shared)
        mp["xk"] = np.ascontiguousarray(xkc)
        mp["cst"] = csts[p]
        maps.append(mp)
    return maps


_NC_CACHE = {}


def kernel(x, norm1_g, w_in, pool_w, pool_scale, pool_out_g, attn_out_g, w_out, norm2_g, w_up, w_down, final_g):
    if "nc" not in _NC_CACHE:
        _NC_CACHE["nc"] = build_program()
    nc = _NC_CACHE["nc"]
    maps = make_in_maps(x, norm1_g, w_in, pool_w, pool_scale, pool_out_g, attn_out_g, w_out, norm2_g, w_up, w_down, final_g)
    res = run_bass_kernel_spmd(nc, maps, core_ids=list(range(8)))
    out = np.empty((4, S, D), np.float32)
    for c in range(8):
        b, p = c // 2, c % 2
        yc = np.asarray(res.results[c]["y"]).reshape(4, 512, D)
        for m in range(4):
            t0 = (2 * m + p) * 512
            out[b, t0:t0 + 512] = yc[m]
    return out
```

```python
import contextlib
import numpy as np
import ml_dtypes
import concourse.bass as bass
import concourse.mybir as mybir
from concourse.bass_utils import run_bass_kernel_spmd

F32 = mybir.dt.float32
BF16 = mybir.dt.bfloat16
AF = mybir.ActivationFunctionType
ALU = mybir.AluOpType

D = 1024
S = 4096
DFF = 4096
EPS = 1e-6
WINS = (2, 4, 8, 16)
ENGS = ("pe", "act", "dve", "pool", "sp")

C_ID = 0
C_TRI = 128
C_SEL = 256
C_NONE = 384
C_MASK = 448
C_BAND = C_MASK + 4 * 512
C_BAND0 = C_BAND + 512
C_BANDP = C_BAND0 + 512
C_ZERO = C_BANDP + 512
C_END = C_ZERO + 64

ARENA_F32 = 45056
OPTS = {"a_tiles": 8, "a_proj": True, "a_pool": True, "a_tr": True, "a_q": 1, "a_u": 1, "b_slots": 4, "b_pairs": 4, "b_nk": 0, "b_lvl": 10, "b_dummy": 8, "b_dummy2": 0, "b_order": 2}


class Buf:
    __slots__ = ("w", "r")

    def __init__(self):
        self.w = None
        self.r = {}


class Prog:
    def __init__(self, nc, stack):
        self.nc = nc
        self.stack = stack
        self.q = {e: [] for e in ENGS}
        self.cnt = {e: 0 for e in ENGS}
        self.sem = {e: stack.enter_context(nc.semaphore("prog_" + e)) for e in ENGS}
        self.waited = {e: {} for e in ENGS}
        self.nsem = 0
        self.dsems = []

    def _emit_waits(self, eng, deps):
        for d in deps:
            if d is None:
                continue
            kind, key, val = d
            if kind == "eng" and key == "pe" and eng == "pe":
                continue
            ident = key if kind == "eng" else id(key)
            if self.waited[eng].get(ident, 0) >= val:
                continue
            self.waited[eng][ident] = val
            sem = self.sem[key] if kind == "eng" else key
            self.q[eng].append(("wait", sem, val))

    @staticmethod
    def _deps(reads, writes):
        deps = []
        for b in reads:
            if b.w is not None:
                deps.append(b.w)
        for b in writes:
            if b.w is not None:
                deps.append(b.w)
            deps.extend(b.r.values())
        return deps

    @staticmethod
    def _record(tok, reads, writes):
        kind, key, val = tok
        rk = key if kind == "eng" else ("dma", id(key))
        for b in reads:
            b.r[rk] = tok
        for b in writes:
            b.w = tok
            b.r = {}

    def op(self, eng, fn, reads=(), writes=(), deps=()):
        self._emit_waits(eng, list(deps) + self._deps(reads, writes))
        self.cnt[eng] += 1
        self.q[eng].append(("op", fn, self.sem[eng]))
        tok = ("eng", eng, self.cnt[eng])
        self._record(tok, reads, writes)
        return tok

    def new_dma_sem(self):
        self.nsem += 1
        s = self.stack.enter_context(self.nc.semaphore("dsem%d" % self.nsem))
        rec = [s, 0]
        self.dsems.append(rec)
        return rec

    def dma(self, eng, semrec, fn, reads=(), writes=(), deps=()):
        self._emit_waits(eng, list(deps) + self._deps(reads, writes))
        semrec[1] += 16
        self.q[eng].append(("dma", fn, semrec[0]))
        tok = ("dma", semrec[0], semrec[1])
        self._record(tok, reads, writes)
        return tok

    def wait(self, eng, deps):
        self._emit_waits(eng, deps)

    def I(self, eng, method, *args, reads=(), writes=(), deps=(), **kw):
        return self.op(eng, lambda e: getattr(e, method)(*args, **kw), reads, writes, deps)

    def D(self, eng, semrec, out, in_, reads=(), writes=(), deps=()):
        return self.dma(eng, semrec, lambda e: e.dma_start(out=out, in_=in_), reads, writes, deps)

    def emit(self):
        nc = self.nc
        with nc.Block() as block:
            def run(engine, items):
                for it in items:
                    if it[0] == "wait":
                        engine.wait_ge(it[1], it[2])
                    elif it[0] == "op":
                        it[1](engine).then_inc(it[2], 1)
                    else:
                        it[1](engine).then_inc(it[2], 16)

            @block.tensor
            def _(e):
                run(e, self.q["pe"])

            @block.scalar
            def _(e):
                run(e, self.q["act"])

            @block.vector
            def _(e):
                run(e, self.q["dve"])

            @block.gpsimd
            def _(e):
                run(e, self.q["pool"])

            @block.sync
            def _(e):
                run(e, self.q["sp"])


def build_program(phases="ABC", debug=False):
    nc = bass.Bass("TRN2", target_bir_lowering=False)
    dram_in = lambda n, s, dt=F32: nc.dram_tensor(n, s, dt, kind="ExternalInput").ap()
    xk = dram_in("xk", [S, D])
    w_in = dram_in("w_in", [D, 2048])
    pool_w = dram_in("pool_w", [128, 4, 128])
    w_out = dram_in("w_out", [D, D])
    w_up = dram_in("w_up", [D, DFF])
    w_down = dram_in("w_down", [DFF, D])
    g1bc_d = dram_in("g1bc", [128, D])
    g2bc_d = dram_in("g2bc", [128, D])
    gfbc_d = dram_in("gfbc", [128, D])
    pscale_d = dram_in("pscale", [128, 4])
    gout_d = dram_in("gout", [128, 8])
    cst_d = dram_in("cst", [128, C_END], BF16)
    y = nc.dram_tensor("y", [2048, D], F32, kind="ExternalOutput").ap()
    dbg = {}
    if debug:
        dbg["kT"] = nc.dram_tensor("dbg_kT", [128, 4 * S], BF16, kind="ExternalOutput").ap()
        dbg["v"] = nc.dram_tensor("dbg_v", [128, 32 * 512], BF16, kind="ExternalOutput").ap()
        dbg["qT"] = nc.dram_tensor("dbg_qT", [128, 4 * 2048], BF16, kind="ExternalOutput").ap()
        dbg["yp"] = nc.dram_tensor("dbg_yp", [128, 4 * 2048], BF16, kind="ExternalOutput").ap()
        dbg["ya"] = nc.dram_tensor("dbg_ya", [128, 4 * 2048], BF16, kind="ExternalOutput").ap()
        dbg["rr"] = nc.dram_tensor("dbg_rr", [128, 32], F32, kind="ExternalOutput").ap()

    with contextlib.ExitStack() as st:
        P = Prog(nc, st)
        sb = lambda n, s, dt: st.enter_context(nc.sbuf_tensor("s_" + n, s, dt))

        cst = sb("cst", [128, C_END], BF16)
        pscale = sb("pscale", [128, 4], F32)
        gout = sb("gout", [128, 8], F32)
        pwst = sb("pwst", [128, 4, 128], F32)
        pwb = sb("pwb", [128, 4, 128], BF16)
        ones = sb("ones", [128, 2], F32)
        ypT = sb("ypT", [128, 4, 2048], BF16)
        rr = sb("rr", [128, 32], F32)
        stat = sb("stat", [128, 512], F32)
        AR = sb("arena", [128, ARENA_F32], F32)

        def view(off_bytes, nbytes, dt):
            assert off_bytes % 4 == 0 and nbytes % 4 == 0 and off_bytes + nbytes <= ARENA_F32 * 4
            a = AR[:, off_bytes // 4:(off_bytes + nbytes) // 4]
            return a if dt == F32 else a.bitcast(dt)

        KB = 1024
        PSALL = st.enter_context(nc.psum_tensor("psall", [128, 8 * 512], F32))
        ps = [PSALL[:, i * 512:(i + 1) * 512] for i in range(8)]
        psb = [Buf() for _ in range(8)]

        stat_col = [0]

        def new_stat(n):
            c = stat_col[0]
            stat_col[0] += n
            assert stat_col[0] <= 512
            return stat[:, c:c + n], Buf()

        def barrier():
            for e in ENGS:
                deps = [("eng", o, P.cnt[o]) for o in ENGS if P.cnt[o] > 0]
                P.wait(e, deps)

        csem = P.new_dma_sem()
        b_cst = Buf()
        g1bc = view(171 * KB, 4 * KB, F32)
        for dst, src in ((cst[:], cst_d), (g1bc, g1bc_d), (pscale[:], pscale_d), (gout[:], gout_d),
                         (pwst[:], pool_w)):
            P.D("sp", csem, out=dst, in_=src, writes=[b_cst])
        b_pwb = Buf()
        b_ones = Buf()
        P.I("dve", "tensor_copy", out=pwb[:], in_=pwst[:], reads=[b_cst], writes=[b_pwb])
        P.I("dve", "memset", ones[:], 1.0, writes=[b_ones])

        kT = view(0, 32 * KB, BF16).rearrange("p (c t) -> p c t", c=4)
        vv = view(32 * KB, 32 * KB, BF16).rearrange("p (b e) -> p b e", b=32)
        qT = view(64 * KB, 16 * KB, BF16).rearrange("p (c t) -> p c t", c=4)
        yaT = view(80 * KB, 16 * KB, BF16).rearrange("p (c t) -> p c t", c=4)
        b_kT = [Buf() for _ in range(8)]
        b_v = [Buf() for _ in range(8)]
        b_qT = [Buf() for _ in range(4)]
        b_yp = [Buf() for _ in range(4)]
        b_ya = [Buf() for _ in range(4)]
        b_rr = Buf()
        P.I("dve", "memset", rr[:], 1.0, writes=[b_rr])

        def alias(dst, srcs):
            for sbuf in srcs:
                for k, t in list(sbuf.r.items()) + ([(None, sbuf.w)] if sbuf.w is not None else []):
                    key = (t[1] if t[0] == "eng" else ("dma", id(t[1])))
                    if key not in dst.r or dst.r[key][2] < t[2]:
                        dst.r[key] = t

        def phase_a():
            Y = 80 * KB
            w_inb = view(Y, 32 * KB, BF16).rearrange("p (c e) -> p c e", c=8)
            stg = [view(Y + 32 * KB + i * 4 * KB, 4 * KB, F32) for i in range(2)]
            xbs = [view(Y + 40 * KB + i * 4 * KB, 4 * KB, F32) for i in range(3)]
            xh = [view(Y + 52 * KB + i * 2 * KB, 2 * KB, BF16) for i in range(2)]
            xhT = [view(Y + 56 * KB + i * 8 * KB, 8 * KB, BF16).rearrange("p (c t) -> p c t", c=8)
                   for i in range(2)]
            uu = view(Y + 72 * KB, 5 * KB, BF16).rearrange("p (b e) -> p b e", b=5)
            pooledT = view(Y + 77 * KB, 4 * KB, BF16).rearrange("p (g t) -> p g t", g=4)
            sqp = view(Y + 81 * KB, 8 * KB, F32).rearrange("p (g t) -> p g t", g=4)
            junk = view(Y + 89 * KB, 2 * KB, BF16)
            yp32 = view(Y + 32 * KB, 2 * KB, F32)
            stg = stg + [view(Y + 81 * KB + i * 4 * KB, 4 * KB, F32) for i in range(2)]
            b_stg = [Buf() for _ in range(4)]
            b_win = [Buf() for _ in range(8)]
            b_xb = [Buf() for _ in range(3)]
            b_xh = [Buf() for _ in range(2)]
            b_xhT = [Buf() for _ in range(2)]
            b_u = [Buf() for _ in range(5)]
            b_pooled = Buf()
            b_sqp = Buf()
            b_yp32 = b_stg[0]
            b_junk = Buf()
            xsem = [P.new_dma_sem() for _ in range(3)]
            ssem = [P.new_dma_sem() for _ in range(4)]

            w_in_v = w_in.rearrange("(c p) e -> p c e", p=128)
            WIN_ORDER = (2, 3, 0, 1)

            def load_win(i):
                cb, cp = WIN_ORDER[i // 4], i % 4
                s_ = i % 4
                P.D("sp", ssem[s_], out=stg[s_].rearrange("p (c e) -> p c e", c=2),
                    in_=w_in_v[:, cp * 2:cp * 2 + 2, cb * 512:(cb + 1) * 512], writes=[b_stg[s_]])
                eng = ("dve", "act")[i % 2]
                dst = w_inb[:, cp * 2:cp * 2 + 2, cb * 512:(cb + 1) * 512]
                src = stg[s_].rearrange("p (c e) -> p c e", c=2)
                if eng == "act":
                    P.I("act", "copy", out=dst, in_=src, reads=[b_stg[s_]], writes=[b_win[cb]])
                else:
                    P.I(eng, "tensor_copy", out=dst, in_=src, reads=[b_stg[s_]], writes=[b_win[cb]])

            def load_x(g):
                xb = xbs[g % 3]
                P.D("sp", xsem[g % 3], out=xb, in_=xk[g * 128:(g + 1) * 128, :],
                      writes=[b_xb[g % 3]])

            pacc_rr = [0]

            def pacc():
                i = 2 + pacc_rr[0] % 6
                pacc_rr[0] += 1
                return ps[i], psb[i]

            evac_rr = [0]

            def evac(out_ap, in_ap, reads, writes, scale=None):
                evac_rr[0] ^= 1
                if scale is not None and OPTS["a_q"] == 3:
                    return P.I("act", "mul", out=out_ap, in_=in_ap, mul=scale, reads=reads, writes=writes)
                if scale is not None and OPTS["a_q"] == 4:
                    return P.I("dve", "tensor_scalar", out=out_ap, in0=in_ap, scalar1=scale, scalar2=None,
                               op0=ALU.mult, reads=reads, writes=writes)
                if scale is not None and OPTS["a_q"] == 5:
                    return P.I("act", "activation", out=out_ap, in_=in_ap, func=AF.Copy, scale=scale, reads=reads, writes=writes)
                if evac_rr[0]:
                    if scale is None:
                        return P.I("act", "copy", out=out_ap, in_=in_ap, reads=reads, writes=writes)
                    return P.I("act", "activation", out=out_ap, in_=in_ap, func=AF.Copy, scale=scale,
                                reads=reads, writes=writes)
                if scale is None:
                    return P.I("dve", "tensor_copy", out=out_ap, in_=in_ap, reads=reads, writes=writes)
                return P.I("dve", "tensor_scalar", out=out_ap, in0=in_ap, scalar1=scale, scalar2=None,
                                                             op0=ALU.mult, reads=reads, writes=writes)

            def norm_front(g, kt, j):
                xb, bxb = xbs[g % 3], b_xb[g % 3]
                ss, bss = new_stat(3)
                P.I("act", "activation", out=junk, in_=xb, func=AF.Square, accum_out=ss[:, 0:1],
                     reads=[bxb], writes=[b_junk, bss])
                P.I("act", "activation", out=ss[:, 1:2], in_=ss[:, 0:1], func=AF.Sqrt, scale=1.0 / D, bias=EPS,
                     reads=[bss], writes=[bss])
                P.I("dve", "reciprocal", out=ss[:, 2:3], in_=ss[:, 1:2], reads=[bss], writes=[bss])
                xhb, bxh = xh[g % 2], b_xh[g % 2]
                P.I("dve", "scalar_tensor_tensor", out=xhb, in0=xb, scalar=ss[:, 2:3], in1=g1bc,
                                                             op0=ALU.mult, op1=ALU.mult,
                     reads=[bxb, bss, b_cst], writes=[bxh])

            def norm_back(g, kt, j):
                xhb, bxh = xh[g % 2], b_xh[g % 2]
                pt = ps[g % 2][:].bitcast(BF16).rearrange("p (c t) -> p c t", c=8)
                for c in range(8):
                    P.I("pe", "transpose", out=pt[:, c, :], in_=xhb[:, c * 128:(c + 1) * 128], identity=cst[:, C_ID:C_ID + 128],
                         reads=[bxh, b_cst], writes=[psb[g % 2]])
                evac(xhT[kt % 2][:, :, j * 128:(j + 1) * 128], pt, [psb[g % 2]], [b_xhT[kt % 2]])

            def proj_T(dst_ap, dst_buf, col0, xT, bxT, scale=None):
                pa, pb = pacc()
                for c in range(8):
                    P.I("pe", "matmul", pa[:], lhsT=w_inb[:, c, col0:col0 + 128], rhs=xT[:, c, :],
                                                       start=(c == 0), stop=(c == 7),
                         reads=[b_win[col0 // 512], bxT], writes=[pb])
                evac(dst_ap, pa[:], [pb], [dst_buf], scale=scale)

            def proj_tok(dst_ap, dst_buf, col0, xT, bxT, j):
                pa, pb = pacc()
                for c in range(8):
                    P.I("pe", "matmul", pa[:], lhsT=xT[:, c, j * 128:(j + 1) * 128], rhs=w_inb[:, c, col0:col0 + 512],
                                                       start=(c == 0), stop=(c == 7),
                         reads=[b_win[col0 // 512], bxT], writes=[pb])
                evac(dst_ap, pa[:], [pb], [dst_buf])

            NBLK = 32
            nx = 0
            for i in range(3):
                load_x(nx)
                nx += 1
            for i in range(4):
                load_win(i)

            def tile_items(kt):
                own = kt % 2 == 1
                m = kt // 2
                xT, bxT = xhT[kt % 2], b_xhT[kt % 2]
                items = []
                for ec in range(4):
                    items.append(lambda ec=ec: proj_T(kT[:, ec, kt * 512:(kt + 1) * 512], b_kT[kt], 1024 + ec * 128, xT, bxT))
                for j in range(4):
                    items.append(lambda j=j: proj_tok(vv[:, kt * 4 + j, :], b_v[kt], 1536, xT, bxT, j))
                if not own:
                    items.append(lambda: proj_tok(uu[:, 0, :], b_u[0], 0, xT, bxT, 3))
                else:
                    for ec in range(4):
                        items.append(lambda ec=ec: proj_T(qT[:, ec, m * 512:(m + 1) * 512], b_qT[m], 512 + ec * 128, xT, bxT, scale=0.125))
                    for j in range(4):
                        items.append(lambda j=j: proj_tok(uu[:, 1 + j, :], b_u[1 + j], 0, xT, bxT, j))
                return items

            def next_x():
                if nxs[0] < NBLK:
                    load_x(nxs[0])
                    nxs[0] += 1

            nxs = [nx]
            for j in range(4):
                norm_front(j, 0, j)
                norm_back(j, 0, j)
                next_x()
            for i in range(4, 16):
                load_win(i)
            alias(b_sqp, [b_stg[2], b_stg[3]])
            for kt in range(8):
                own = kt % 2 == 1
                m = kt // 2
                items = tile_items(kt)
                n = len(items)
                for q in range(4):
                    if kt + 1 < 8:
                        norm_front((kt + 1) * 4 + q, kt + 1, q)
                    for it in items[q * n // 4:(q + 1) * n // 4]:
                        it()
                    if kt + 1 < 8:
                        norm_back((kt + 1) * 4 + q, kt + 1, q)
                        next_x()
                if not own:
                    continue
                if not OPTS["a_pool"]:
                    continue
                ssq, bssq = new_stat(8)
                for g in range(4):
                    pa, pb = pacc()
                    for j in range(4):
                        bcol = (C_BAND0 if (m == 0 and j == 0) else C_BAND) + g * 128
                        P.I("pe", "matmul", pa[:, j * 128:(j + 1) * 128], lhsT=uu[:, 1 + j, g * 128:(g + 1) * 128],
                                                                      rhs=cst[:, bcol:bcol + 128], start=True, stop=False,
                             reads=[b_u[1 + j], b_cst], writes=[pb])
                        P.I("pe", "matmul", pa[:, j * 128:(j + 1) * 128], lhsT=uu[:, j, g * 128:(g + 1) * 128],
                                                           rhs=cst[:, C_BANDP + g * 128:C_BANDP + (g + 1) * 128], start=False, stop=True,
                             reads=[b_u[j], b_cst], writes=[pb])
                    evac(pooledT[:, g, :], pa[:], [pb], [b_pooled])
                    pm, pmb = pacc()
                    P.I("pe", "matmul", pm[:], lhsT=pwb[:, g, :], rhs=pooledT[:, g, :], start=True, stop=True,
                         reads=[b_pwb, b_pooled], writes=[pmb])
                    P.I("dve", "tensor_scalar", out=yp32, in0=pm[:], scalar1=pscale[:, g:g + 1], scalar2=None, op0=ALU.mult,
                        reads=[pmb, b_cst], writes=[b_yp32])
                    P.I("pool", "tensor_copy", out=ypT[:, g, m * 512:(m + 1) * 512], in_=yp32, reads=[b_yp32], writes=[b_yp[m]])
                    P.I("act", "activation", out=sqp[:, g, :], in_=yp32, func=AF.Square, reads=[b_yp32], writes=[b_sqp])
                if OPTS["a_pool"] != 1:
                    continue
                pq, pqb = pacc()
                for j in range(4):
                    for g in range(4):
                        P.I("pe", "matmul", pq[:, j:j + 1], lhsT=sqp[:, g, j * 128:(j + 1) * 128], rhs=ones[:, 0:1],
                                                                start=(g == 0), stop=(g == 3),
                             reads=[b_sqp, b_ones], writes=[pqb])
                P.I("act", "activation", out=ssq[:, 0:4], in_=pq[:, 0:4], func=AF.Sqrt, scale=1.0 / 512, bias=EPS,
                     reads=[pqb], writes=[bssq])
                P.I("dve", "reciprocal", out=rr[:, m * 4:(m + 1) * 4], in_=ssq[:, 0:4], reads=[bssq], writes=[b_rr])

        c_stg = [view(128 * KB + i * 8 * KB, 8 * KB, F32) for i in range(2)]
        c_w_outb = view(144 * KB, 16 * KB, BF16).rearrange("p (c d) -> p c d", c=8)
        c_g2bc = view(160 * KB, 4 * KB, F32)
        c_b_wout = Buf()
        c_b_stg = [Buf() for _ in range(2)]
        c_b_g = Buf()
        c_ssem = [P.new_dma_sem() for _ in range(2)]
        c_stg_rr = [0]

        def prefetch_c():
            gsem0 = P.new_dma_sem()
            P.D("sp", gsem0, c_g2bc, g2bc_d, writes=[c_b_g])
            w_out_v = w_out.rearrange("(c p) d -> p c d", p=128)
            for gi in range(4):
                s_ = c_stg_rr[0] % 2
                c_stg_rr[0] += 1
                P.D("sp", c_ssem[s_], c_stg[s_].rearrange("p (c d) -> p c d", c=2), w_out_v[:, gi * 2:gi * 2 + 2, :], writes=[c_b_stg[s_]])
                sv = c_stg[s_].rearrange("p (c d) -> p c d", c=2)
                for cc in range(2):
                    c = gi * 2 + cc
                    P.I("dve", "tensor_scalar", out=c_w_outb[:, c, :], in0=sv[:, cc, :], scalar1=gout[:, c:c + 1],
                        scalar2=None, op0=ALU.mult, reads=[c_b_stg[s_], b_cst], writes=[c_b_wout])

        def phase_b():
            Y = 96 * KB
            Eb = [view(Y + i * 4 * KB, 4 * KB, F32) for i in range(2)]
            Lb = [view(Y + 8 * KB + i * 2 * KB, 2 * KB, BF16) for i in range(2)]
            Ab = [view(Y + 12 * KB + i * 2 * KB, 2 * KB, BF16) for i in range(2)]
            Rt = [view(Y + 16 * KB + i * KB, KB, BF16) for i in range(4)]
            sqa = view(Y + 20 * KB, 8 * KB, F32).rearrange("p (c t) -> p c t", c=4)
            o32s = [view(Y + 28 * KB + i * 2 * KB, 2 * KB, F32) for i in range(2)]
            pending = []
            b_E = [Buf() for _ in range(2)]
            b_L = [Buf() for _ in range(2)]
            b_A = [Buf() for _ in range(2)]
            b_Rt = [Buf() for _ in range(4)]
            b_sqa = Buf()
            b_o32s = [Buf(), Buf()]
            for i in range(4):
                P.I("pool", "memset", Rt[i], 0.0, writes=[b_Rt[i]])
            def zero_masked(q):
                P.I("pool", "memset", Lb[q].rearrange("p (h t) -> p h t", h=2)[:, :, 0:384], 0.0, writes=[b_L[q]])
                P.I("pool", "memset", Ab[q].rearrange("p (h t) -> p h t", h=2)[:, :, 0:384], 0.0, writes=[b_A[q]])

            for q in range(2):
                zero_masked(q)
            mhalf, b_mhalf = new_stat(4)
            P.I("pool", "memset", mhalf, -0.5, writes=[b_mhalf])
            for i in (6, 7):
                P.I("dve", "memset", ps[i], 0.0, writes=[psb[i]])
            b_z = [[Buf(), Buf()] for _ in range(2)]
            b_ops = [[Buf(), Buf()] for _ in range(2)]
            b_rps = [[Buf(), Buf()] for _ in range(2)]
            for q in range(2):
                for hp in range(2):
                    b_z[q][hp].w, b_z[q][hp].r = psb[2 * q + hp].w, dict(psb[2 * q + hp].r)
                    b_ops[q][hp].w, b_ops[q][hp].r = psb[4 + q].w, dict(psb[4 + q].r)
                    b_rps[q][hp].w, b_rps[q][hp].r = psb[6 + q].w, dict(psb[6 + q].r)

            preissued = set()
            for m in range(4):
                nk = 8 * (m + 1)
                for dp in range(2):
                    def QK(q, tau, m=m, dp=dp, nk=nk):
                        ec = dp * 2 + q
                        kb = nk - 1 - tau
                        di = kb - (nk - 4)
                        for hp in range(2):
                            z = ps[2 * q + hp]
                            pr = slice(hp * 64, hp * 64 + 64)
                            P.I("pe", "matmul", z, lhsT=kT[pr, ec, kb * 128:(kb + 1) * 128], rhs=qT[pr, ec, m * 512:(m + 1) * 512],
                                start=True, stop=(di < 0), reads=[b_kT[kb // 4], b_qT[m]], writes=[b_z[q][hp]])
                        if di >= 0:
                            for hp in range(2):
                                P.I("pe", "matmul", ps[2 * q + hp], lhsT=cst[:, C_ID:C_ID + 128],
                                    rhs=cst[:, C_MASK + di * 512:C_MASK + (di + 1) * 512],
                                    start=False, stop=True, reads=[b_cst], writes=[b_z[q][hp]])

                    def zpair(q):
                        return PSALL[:, 2 * q * 512:(2 * q + 2) * 512]

                    def cols(ap, tau):
                        c0 = max(0, 3 - tau) * 128
                        if c0 == 0:
                            return ap
                        return ap.rearrange("p (h t) -> p h t", h=2)[:, :, c0:512]

                    def EXP1(q, tau):
                        P.I("act", "activation", out=cols(Eb[q], tau), in_=cols(zpair(q), tau), func=AF.Exp, reads=b_z[q], writes=[b_E[q]])

                    def LN(q, tau):
                        P.I("act", "activation", out=cols(Lb[q], tau), in_=cols(Eb[q], tau), func=AF.Ln, bias=1.0, reads=[b_E[q]], writes=[b_L[q]])

                    def TRISEL(q, tau):
                        ri = q * 2 + tau % 2
                        for hp in range(2):
                            z = ps[2 * q + hp]
                            P.I("pe", "matmul", z, lhsT=cst[:, C_TRI:C_TRI + 128], rhs=Lb[q][:, hp * 512:(hp + 1) * 512],
                                start=False, stop=True, skip_group_check=True, reads=[b_L[q], b_cst], writes=[b_z[q][hp]])
                            if tau > 0 and OPTS["b_lvl"] != 10:
                                r0 = hp * 64
                                P.I("pe", "matmul", z, lhsT=cst[r0:r0 + 33, C_SEL:C_SEL + 128], rhs=Rt[ri][r0:r0 + 33, :],
                                    start=False, stop=True, skip_group_check=True, reads=[b_Rt[ri], b_cst], writes=[b_z[q][hp]])
                        if tau > 0 and OPTS["b_lvl"] == 10:
                            for hp in range(2):
                                r0 = hp * 64
                                P.I("pe", "matmul", ps[2 * q + hp], lhsT=cst[r0:r0 + 33, C_SEL:C_SEL + 128], rhs=Rt[ri][r0:r0 + 33, :],
                                    start=False, stop=True, skip_group_check=True, reads=[b_Rt[ri], b_cst], writes=[b_z[q][hp]])

                    def COL(q, tau):
                        if tau >= nk - 1:
                            return
                        rn = q * 2 + (tau + 1) % 2
                        RPS = ps[6 + q]
                        for hp in range(2):
                            r0 = hp * 64
                            P.I("pe", "matmul", RPS[r0:r0 + 33, :], lhsT=cst[:, C_NONE:C_NONE + 33], rhs=Lb[q][:, hp * 512:(hp + 1) * 512],
                                start=(tau == 0), stop=True, skip_group_check=(tau > 0), reads=[b_L[q], b_cst], writes=[b_rps[q][hp]])
                        P.I("dve", "tensor_copy", out=Rt[rn][0:97, :], in_=RPS[0:97, :], reads=b_rps[q], writes=[b_Rt[rn]])
                        for hp in range(2):
                            r1 = hp * 64 + 32
                            P.I("dve", "tensor_tensor", out=Rt[rn][r1:r1 + 1, :], in0=RPS[r1:r1 + 1, :], in1=Rt[rn][r1:r1 + 1, :],
                                op=ALU.subtract, reads=[b_rps[q][hp], b_Rt[rn]], writes=[b_Rt[rn]])

                    def EXP2(q, tau):
                        P.I("act", "activation", out=cols(Ab[q], tau), in_=cols(zpair(q), tau), func=AF.Exp, reads=b_z[q], writes=[b_A[q]])

                    def AV(q, tau):
                        kb = nk - 1 - tau
                        OPS = ps[4 + q]
                        for hp in range(2):
                            h = (dp * 2 + q) * 2 + hp
                            pr = slice(hp * 64, hp * 64 + 64)
                            P.I("pe", "matmul", OPS[pr, :], lhsT=vv[:, kb, h * 64:(h + 1) * 64], rhs=Ab[q][:, hp * 512:(hp + 1) * 512],
                                start=(tau == 0), stop=(tau == nk - 1), reads=[b_v[kb // 4], b_A[q]], writes=[b_ops[q][hp]])

                    def WARM(q, tau, n):
                        if tau == 0 or tau == nk - 1:
                            return
                        for _ in range(n):
                            P.I("pe", "matmul", ps[4 + q][0:64, :], lhsT=cst[:, C_ZERO:C_ZERO + 64], rhs=cst[:, C_MASK:C_MASK + 512],
                                start=False, stop=False, reads=[b_cst], writes=[b_ops[q][0]])

                    if (m, dp) not in preissued:
                        for q in range(2):
                            QK(q, 0)
                    for tau in range(nk):
                        if tau == 2 and pending:
                            for fn in pending:
                                fn()
                            del pending[:]
                        if OPTS["b_order"] == 0:
                            EXP1(0, tau)
                            LN(0, tau)
                            TRISEL(0, tau)
                            EXP1(1, tau)
                            LN(1, tau)
                            TRISEL(1, tau)
                            COL(0, tau)
                            COL(1, tau)
                            WARM(0, tau, OPTS["b_dummy2"])
                            EXP2(0, tau)
                            AV(0, tau)
                            if tau + 1 < nk:
                                QK(0, tau + 1)
                            EXP2(1, tau)
                            AV(1, tau)
                            if tau + 1 < nk:
                                QK(1, tau + 1)
                            WARM(1, tau, OPTS["b_dummy"])
                        elif OPTS["b_order"] == 2:
                            EXP1(0, tau)
                            EXP1(1, tau)
                            LN(0, tau)
                            TRISEL(0, tau)
                            LN(1, tau)
                            TRISEL(1, tau)
                            COL(0, tau)
                            COL(1, tau)
                            WARM(0, tau, OPTS["b_dummy2"])
                            EXP2(0, tau)
                            AV(0, tau)
                            if tau + 1 < nk:
                                QK(0, tau + 1)
                            else:
                                zero_masked(0)
                            EXP2(1, tau)
                            AV(1, tau)
                            if tau + 1 < nk:
                                QK(1, tau + 1)
                            else:
                                zero_masked(1)
                            WARM(1, tau, OPTS["b_dummy"])
                        else:
                            EXP1(0, tau)
                            EXP1(1, tau)
                            WARM(0, tau, OPTS["b_dummy"])
                            LN(0, tau)
                            TRISEL(0, tau)
                            COL(0, tau)
                            LN(1, tau)
                            TRISEL(1, tau)
                            COL(1, tau)
                            WARM(1, tau, OPTS["b_dummy2"])
                            EXP2(0, tau)
                            AV(0, tau)
                            if tau + 1 < nk:
                                QK(0, tau + 1)
                            EXP2(1, tau)
                            AV(1, tau)
                            if tau + 1 < nk:
                                QK(1, tau + 1)
                    for q in range(2):
                        ec = dp * 2 + q
                        P.I("dve", "tensor_copy", out=o32s[q], in_=ps[4 + q], reads=b_ops[q], writes=[b_o32s[q]])
                        P.I("pool", "tensor_copy", out=yaT[:, ec, m * 512:(m + 1) * 512], in_=o32s[q], reads=[b_o32s[q]], writes=[b_ya[m]])
                        pending.append(lambda q=q, ec=ec: P.I("act", "activation", out=sqa[:, ec, :], in_=o32s[q], func=AF.Square,
                                                              reads=[b_o32s[q]], writes=[b_sqa]))
                def slot_stats(m=m):
                    ssq, bssq = new_stat(4)
                    RPS = ps[6]
                    for j in range(4):
                        for ec in range(4):
                            P.I("pe", "matmul", RPS[:, j:j + 1], lhsT=sqa[:, ec, j * 128:(j + 1) * 128], rhs=ones[:, 0:1],
                                start=(ec == 0), stop=(ec == 3), reads=[b_sqa, b_ones], writes=b_rps[0])
                    P.I("dve", "tensor_copy", out=ssq[:, 0:4], in_=RPS[:, 0:4], reads=b_rps[0], writes=[bssq])
                    P.I("pool", "tensor_scalar", out=ssq[:, 0:4], in0=ssq[:, 0:4], scalar1=1.0 / 512, scalar2=EPS, op0=ALU.mult, op1=ALU.add,
                        reads=[bssq], writes=[bssq])
                    P.I("pool", "tensor_tensor", out=rr[:, 16 + m * 4:16 + (m + 1) * 4], in0=ssq[:, 0:4], in1=mhalf[:, 0:4], op=ALU.pow,
                        reads=[bssq, b_mhalf], writes=[b_rr])
                if m + 1 < 4:
                    for q in range(2):
                        QK(q, 0, m=m + 1, dp=0, nk=8 * (m + 2))
                    preissued.add((m + 1, 0))
                for fn in pending:
                    fn()
                del pending[:]
                slot_stats()

        def phase_c():
            hb = view(0, 64 * KB, F32).rearrange("p (j d) -> p j d", j=16)
            wupb = [view(64 * KB + i * 8 * KB, 8 * KB, BF16).rearrange("p (c f) -> p c f", c=8) for i in range(2)]
            hT = view(96 * KB, 32 * KB, BF16).rearrange("p (c t) -> p c t", c=8)
            stg = c_stg
            w_outb = c_w_outb
            wdnb = [view(144 * KB + i * 8 * KB, 8 * KB, BF16).rearrange("p (c d) -> p c d", c=4) for i in range(2)]
            g2bc = c_g2bc
            hh = [view(164 * KB + i * 2 * KB, 2 * KB, BF16) for i in range(2)]
            junk = view(168 * KB, 2 * KB, BF16)
            actG = [view(160 * KB, 16 * KB, BF16).rearrange("p (f t) -> p f t", f=4),
                    view(80 * KB, 16 * KB, BF16).rearrange("p (f t) -> p f t", f=4)]
            ypf = ypT[:].rearrange("p c t -> p (c t)").bitcast(F32)
            gfbc = ypf[:, 0:1024]
            sqs = [ypf[:, 1024 + i * 512:1024 + (i + 1) * 512] for i in range(2)]
            junk2 = ypf[:, 2048:2560].bitcast(BF16)
            b_wout = c_b_wout
            b_stg = c_b_stg
            b_junk = Buf()
            b_junk2 = Buf()
            b_wup = [Buf() for _ in range(2)]
            b_wdn = [Buf() for _ in range(2)]
            b_g = c_b_g
            b_gf = Buf()
            b_hb = [Buf() for _ in range(16)]
            b_hh = [Buf() for _ in range(2)]
            b_hT = [Buf() for _ in range(4)]
            b_act = [Buf() for _ in range(2)]
            b_sqs = [Buf() for _ in range(2)]
            ssem = c_ssem
            hsem = [P.new_dma_sem() for _ in range(4)]
            osem = [P.new_dma_sem() for _ in range(4)]
            gsem = P.new_dma_sem()
            stg_rr = c_stg_rr
            cast_rr = [0]

            for m in range(4):
                src = xk[(2 * m + 1) * 512:(2 * m + 2) * 512, :].rearrange("(j p) d -> p j d", p=128)
                P.D("sp", hsem[m], hb[:, m * 4:(m + 1) * 4, :], src, writes=[b_hb[m * 4 + j] for j in range(4)])

            def stage_load(src_ap, pattern, **kw):
                s_ = stg_rr[0] % 2
                stg_rr[0] += 1
                P.D("sp", ssem[s_], stg[s_].rearrange(pattern, **kw), src_ap, writes=[b_stg[s_]])
                return s_

            def outproj(blk):
                m = blk // 4
                bh = b_hb[blk]
                for dh in range(2):
                    k = (blk * 2 + dh) % 2
                    p1, p1b = ps[k * 2], psb[k * 2]
                    p2, p2b = ps[k * 2 + 1], psb[k * 2 + 1]
                    for ec in range(4):
                        P.I("pe", "matmul", p1[:], lhsT=ypT[:, ec, blk * 128:(blk + 1) * 128], rhs=w_outb[:, ec, dh * 512:(dh + 1) * 512],
                            start=(ec == 0), stop=(ec == 3), reads=[b_yp[m], b_wout], writes=[p1b])
                    for ec in range(4):
                        P.I("pe", "matmul", p2[:], lhsT=yaT[:, ec, blk * 128:(blk + 1) * 128], rhs=w_outb[:, 4 + ec, dh * 512:(dh + 1) * 512],
                            start=(ec == 0), stop=(ec == 3), reads=[b_ya[m], b_wout], writes=[p2b])
                    hs = hb[:, blk, dh * 512:(dh + 1) * 512]
                    P.I("dve", "scalar_tensor_tensor", out=hs, in0=p1[:], scalar=rr[:, blk:blk + 1], in1=hs, op0=ALU.mult, op1=ALU.add,
                        reads=[p1b, b_rr, bh], writes=[bh])
                    P.I("dve", "scalar_tensor_tensor", out=hs, in0=p2[:], scalar=rr[:, 16 + blk:17 + blk], in1=hs, op0=ALU.mult, op1=ALU.add,
                        reads=[p2b, b_rr, bh], writes=[bh])

            def norm2(blk):
                m = blk // 4
                bh = b_hb[blk]
                ss, bss = new_stat(3)
                P.I("act", "activation", out=junk, in_=hb[:, blk, :], func=AF.Square, accum_out=ss[:, 0:1], reads=[bh], writes=[b_junk, bss])
                P.I("act", "activation", out=ss[:, 1:2], in_=ss[:, 0:1], func=AF.Sqrt, scale=1.0 / D, bias=EPS, reads=[bss], writes=[bss])
                P.I("dve", "reciprocal", out=ss[:, 2:3], in_=ss[:, 1:2], reads=[bss], writes=[bss])
                hhb, bhh = hh[blk % 2], b_hh[blk % 2]
                P.I("dve", "scalar_tensor_tensor", out=hhb, in0=hb[:, blk, :], scalar=ss[:, 2:3], in1=g2bc, op0=ALU.mult, op1=ALU.mult,
                    reads=[bh, bss, b_g], writes=[bhh])

            def norm2_back(blk):
                m = blk // 4
                hhb, bhh = hh[blk % 2], b_hh[blk % 2]
                pi = 4 + blk % 2
                pt = ps[pi][:].bitcast(BF16).rearrange("p (c t) -> p c t", c=8)
                for c in range(8):
                    P.I("pe", "transpose", out=pt[:, c, :], in_=hhb[:, c * 128:(c + 1) * 128], identity=cst[:, C_ID:C_ID + 128],
                        reads=[bhh, b_cst], writes=[psb[pi]])
                P.I("act", "copy", out=hT[:, :, blk * 128:(blk + 1) * 128], in_=pt, reads=[psb[pi]], writes=[b_hT[m]])

            for b in range(18):
                if b < 16:
                    outproj(b)
                if 0 <= b - 1 < 16:
                    norm2(b - 1)
                if 0 <= b - 2 < 16:
                    norm2_back(b - 2)

            w_up_v = w_up.rearrange("(c p) f -> p c f", p=128)
            w_dn_v = w_down.rearrange("(fc p) d -> p fc d", p=128)

            def cast(dst, src, reads, writes):
                cast_rr[0] += 1
                if cast_rr[0] % 2:
                    P.I("act", "copy", out=dst, in_=src, reads=reads, writes=writes)
                else:
                    P.I("pool", "tensor_copy", out=dst, in_=src, reads=reads, writes=writes)

            def load_w(G):
                for half in range(2):
                    f0 = G * 512 + half * 256
                    s = stage_load(w_up_v[:, :, f0:f0 + 256], "p (c f) -> p c f", c=8)
                    cast(wupb[G % 2][:, :, half * 256:(half + 1) * 256], stg[s].rearrange("p (c f) -> p c f", c=8), [b_stg[s]], [b_wup[G % 2]])
                for half in range(2):
                    fc0 = G * 4 + half * 2
                    s = stage_load(w_dn_v[:, fc0:fc0 + 2, :], "p (c d) -> p c d", c=2)
                    cast(wdnb[G % 2][:, half * 2:(half + 1) * 2, :], stg[s].rearrange("p (c d) -> p c d", c=2), [b_stg[s]], [b_wdn[G % 2]])

            up_rr = [0]

            def up(G):
                wb, bw = wupb[G % 2], b_wup[G % 2]
                for f4 in range(4):
                    for tt in range(4):
                        pi = up_rr[0] % 4
                        up_rr[0] += 1
                        pu, pub = ps[pi], psb[pi]
                        for c in range(8):
                            P.I("pe", "matmul", pu[:], lhsT=wb[:, c, f4 * 128:(f4 + 1) * 128], rhs=hT[:, c, tt * 512:(tt + 1) * 512],
                                start=(c == 0), stop=(c == 7), reads=[bw, b_hT[tt]], writes=[pub])
                        sq, bsq = sqs[pi % 2], b_sqs[pi % 2]
                        P.I("act", "copy", out=sq, in_=pu[:], reads=[pub], writes=[bsq])
                        P.I("dve", "scalar_tensor_tensor", out=actG[G % 2][:, f4, tt * 512:(tt + 1) * 512], in0=pu[:], scalar=0.0, in1=sq,
                            op0=ALU.max, op1=ALU.mult, reads=[pub, bsq], writes=[b_act[G % 2]])

            dn_rr = [0]

            def down(G, last=False):
                wb, bw = wdnb[G % 2], b_wdn[G % 2]
                for blk in range(16):
                    if last and blk > 0:
                        final_norm(blk - 1)
                    for dh in range(2):
                        pi = 4 + dn_rr[0] % 4
                        dn_rr[0] += 1
                        for f4 in range(4):
                            P.I("pe", "matmul", ps[pi][:], lhsT=actG[G % 2][:, f4, blk * 128:(blk + 1) * 128], rhs=wb[:, f4, dh * 512:(dh + 1) * 512],
                                start=(f4 == 0), stop=(f4 == 3), reads=[bw, b_act[G % 2]], writes=[psb[pi]])
                        hs = hb[:, blk, dh * 512:(dh + 1) * 512]
                        P.I("dve", "tensor_tensor", out=hs, in0=ps[pi][:], in1=hs, op=ALU.add, reads=[psb[pi], b_hb[blk]], writes=[b_hb[blk]])
                if last:
                    final_norm(15)

            def final_norm(blk):
                m = blk // 4
                bh = b_hb[blk]
                ss, bss = new_stat(3)
                P.I("act", "activation", out=junk2, in_=hb[:, blk, :], func=AF.Square, accum_out=ss[:, 0:1], reads=[bh], writes=[b_junk2, bss])
                P.I("act", "activation", out=ss[:, 1:2], in_=ss[:, 0:1], func=AF.Sqrt, scale=1.0 / D, bias=EPS, reads=[bss], writes=[bss])
                P.I("dve", "reciprocal", out=ss[:, 2:3], in_=ss[:, 1:2], reads=[bss], writes=[bss])
                P.I("dve", "scalar_tensor_tensor", out=hb[:, blk, :], in0=hb[:, blk, :], scalar=ss[:, 2:3], in1=gfbc, op0=ALU.mult, op1=ALU.mult,
                    reads=[bh, bss, b_gf], writes=[bh])
                if blk % 4 == 3:
                    dst = y[m * 512:(m + 1) * 512, :].rearrange("(j p) d -> p j d", p=128)
                    P.D("sp", osem[m], dst, hb[:, m * 4:(m + 1) * 4, :], reads=[b_hb[m * 4 + j] for j in range(4)])

            alias(b_act[0], [b_g, b_hh[0], b_hh[1], b_junk])
            alias(b_act[1], b_ya)
            for bb in b_wdn:
                alias(bb, [b_wout])
            for bb in [b_gf, b_junk2] + b_sqs:
                alias(bb, b_yp)
            P.D("sp", gsem, gfbc, gfbc_d, writes=[b_gf])
            NG = 8
            load_w(0)
            up(0)
            for G in range(NG):
                if G + 1 < NG:
                    load_w(G + 1)
                    up(G + 1)
                down(G, last=(G == NG - 1))

        if "A" in phases:
            phase_a()
        if "B" in phases:
            barrier()
            prefetch_c()
            phase_b()
        if "C" in phases:
            if "B" not in phases:
                prefetch_c()
            barrier()
            phase_c()

        if debug:
            barrier()
            dsem = P.new_dma_sem()
            tk = None
            for name, src in (("kT", view(0, 32 * KB, BF16)), ("v", view(32 * KB, 32 * KB, BF16)),
                              ("qT", view(64 * KB, 16 * KB, BF16)), ("yp", ypT[:].rearrange("p c t -> p (c t)")),
                              ("ya", view(80 * KB, 16 * KB, BF16)), ("rr", rr[:])):
                tk = P.D("sp", dsem, out=dbg[name], in_=src)
            P.wait("sp", [tk])
        P.wait("sp", [("dma", r[0], r[1]) for r in P.dsems if r[1] > 0])
        P.emit()
    return nc


def _bf16(a):
    return a.astype(ml_dtypes.bfloat16)


def make_consts(parity):
    C = np.zeros((128, C_END), np.float32)
    idx = np.arange(128)
    C[:, C_ID:C_ID + 128] = np.eye(128)
    C[:, C_TRI:C_TRI + 128] = -(idx[:, None] >= idx[None, :]).astype(np.float32)
    for r in (0, 32, 64, 96):
        C[r, C_SEL:C_SEL + 128] = 1.0
    C[:, C_NONE:C_NONE + 64] = -1.0
    t = np.arange(512)
    for i in range(4):
        C[:, C_MASK + i * 512:C_MASK + (i + 1) * 512] = -30000.0 * ((i * 128 + idx[:, None]) >= t[None, :]).astype(np.float32)
    tt = idx[:, None]
    to = idx[None, :]
    for g, w in enumerate(WINS):
        win = ((to - tt >= 0) & (to - tt < w)).astype(np.float32)
        band = win / w - np.eye(128)
        C[:, C_BAND + g * 128:C_BAND + (g + 1) * 128] = band
        if parity == 0:
            cnt = np.minimum(to + 1, w).astype(np.float32)
            C[:, C_BAND0 + g * 128:C_BAND0 + (g + 1) * 128] = win / cnt - np.eye(128)
        else:
            C[:, C_BAND0 + g * 128:C_BAND0 + (g + 1) * 128] = band
        C[:, C_BANDP + g * 128:C_BANDP + (g + 1) * 128] = ((to + 128 - tt) < w).astype(np.float32) / w
    return _bf16(C)


def make_in_maps(x, norm1_g, w_in, pool_w, pool_scale, pool_out_g, attn_out_g, w_out, norm2_g, w_up, w_down, final_g):
    f = lambda a: np.ascontiguousarray(np.asarray(a, dtype=np.float32))
    x = f(x)
    shared = {
        "w_in": f(w_in),
        "pool_w": f(np.transpose(np.asarray(pool_w), (1, 0, 2))),
        "w_out": f(w_out), "w_up": f(w_up), "w_down": f(w_down),
        "g1bc": f(np.broadcast_to(np.asarray(norm1_g)[None, :], (128, D))),
        "g2bc": f(np.broadcast_to(np.asarray(norm2_g)[None, :], (128, D))),
        "gfbc": f(np.broadcast_to(np.asarray(final_g)[None, :], (128, D))),
        "pscale": f(np.asarray(pool_scale).reshape(4, 128).T),
        "gout": f(np.concatenate([np.asarray(pool_out_g), np.asarray(attn_out_g)]).reshape(8, 128).T),
    }
    csts = [make_consts(0), make_consts(1)]
    maps = []
    for c in range(8):
        b, p = c // 2, c % 2
        if p == 1:
            xkc = x[b]
        else:
            xkc = np.concatenate([np.zeros((512, D), np.float32), x[b, :S - 512]], axis=0)
        mp = dict(shared)
        mp["xk"] = np.ascontiguousarray(xkc)
        mp["cst"] = csts[p]
        maps.append(mp)
    return maps


_NC_CACHE = {}


def kernel(x, norm1_g, w_in, pool_w, pool_scale, pool_out_g, attn_out_g, w_out, norm2_g, w_up, w_down, final_g):
    if "nc" not in _NC_CACHE:
        _NC_CACHE["nc"] = build_program()
    nc = _NC_CACHE["nc"]
    maps = make_in_maps(x, norm1_g, w_in, pool_w, pool_scale, pool_out_g, attn_out_g, w_out, norm2_g, w_up, w_down, final_g)
    res = run_bass_kernel_spmd(nc, maps, core_ids=list(range(8)))
    out = np.empty((4, S, D), np.float32)
    for c in range(8):
        b, p = c // 2, c % 2
        yc = np.asarray(res.results[c]["y"]).reshape(4, 512, D)
        for m in range(4):
            t0 = (2 * m + p) * 512
            out[b, t0:t0 + 512] = yc[m]
    return out
```

```python
import contextlib
import numpy as np
import ml_dtypes
import concourse.bass as bass
import concourse.mybir as mybir
from concourse.bass_utils import run_bass_kernel_spmd

F32 = mybir.dt.float32
BF16 = mybir.dt.bfloat16
AF = mybir.ActivationFunctionType
ALU = mybir.AluOpType

D = 1024
S = 4096
DFF = 4096
EPS = 1e-6
WINS = (2, 4, 8, 16)
ENGS = ("pe", "act", "dve", "pool", "sp")

C_ID = 0
C_TRI = 128
C_SEL = 256
C_NONE = 384
C_MASK = 448
C_BAND = C_MASK + 4 * 512
C_BAND0 = C_BAND + 512
C_BANDP = C_BAND0 + 512
C_ZERO = C_BANDP + 512
C_END = C_ZERO + 64

ARENA_F32 = 45056
OPTS = {"a_tiles": 8, "a_proj": True, "a_pool": True, "a_tr": True, "a_q": 1, "a_u": 1, "b_slots": 4, "b_pairs": 4, "b_nk": 0, "b_lvl": 10, "b_dummy": 8, "b_dummy2": 0, "b_order": 2}


class Buf:
    __slots__ = ("w", "r")

    def __init__(self):
        self.w = None
        self.r = {}


class Prog:
    def __init__(self, nc, stack):
        self.nc = nc
        self.stack = stack
        self.q = {e: [] for e in ENGS}
        self.cnt = {e: 0 for e in ENGS}
        self.sem = {e: stack.enter_context(nc.semaphore("prog_" + e)) for e in ENGS}
        self.waited = {e: {} for e in ENGS}
        self.nsem = 0
        self.dsems = []

    def _emit_waits(self, eng, deps):
        for d in deps:
            if d is None:
                continue
            kind, key, val = d
            if kind == "eng" and key == "pe" and eng == "pe":
                continue
            ident = key if kind == "eng" else id(key)
            if self.waited[eng].get(ident, 0) >= val:
                continue
            self.waited[eng][ident] = val
            sem = self.sem[key] if kind == "eng" else key
            self.q[eng].append(("wait", sem, val))

    @staticmethod
    def _deps(reads, writes):
        deps = []
        for b in reads:
            if b.w is not None:
                deps.append(b.w)
        for b in writes:
            if b.w is not None:
                deps.append(b.w)
            deps.extend(b.r.values())
        return deps

    @staticmethod
    def _record(tok, reads, writes):
        kind, key, val = tok
        rk = key if kind == "eng" else ("dma", id(key))
        for b in reads:
            b.r[rk] = tok
        for b in writes:
            b.w = tok
            b.r = {}

    def op(self, eng, fn, reads=(), writes=(), deps=()):
        self._emit_waits(eng, list(deps) + self._deps(reads, writes))
        self.cnt[eng] += 1
        self.q[eng].append(("op", fn, self.sem[eng]))
        tok = ("eng", eng, self.cnt[eng])
        self._record(tok, reads, writes)
        return tok

    def new_dma_sem(self):
        self.nsem += 1
        s = self.stack.enter_context(self.nc.semaphore("dsem%d" % self.nsem))
        rec = [s, 0]
        self.dsems.append(rec)
        return rec

    def dma(self, eng, semrec, fn, reads=(), writes=(), deps=()):
        self._emit_waits(eng, list(deps) + self._deps(reads, writes))
        semrec[1] += 16
        self.q[eng].append(("dma", fn, semrec[0]))
        tok = ("dma", semrec[0], semrec[1])
        self._record(tok, reads, writes)
        return tok

    def wait(self, eng, deps):
        self._emit_waits(eng, deps)

    def I(self, eng, method, *args, reads=(), writes=(), deps=(), **kw):
        return self.op(eng, lambda e: getattr(e, method)(*args, **kw), reads, writes, deps)

    def D(self, eng, semrec, out, in_, reads=(), writes=(), deps=()):
        return self.dma(eng, semrec, lambda e: e.dma_start(out=out, in_=in_), reads, writes, deps)

    def emit(self):
        nc = self.nc
        with nc.Block() as block:
            def run(engine, items):
                for it in items:
                    if it[0] == "wait":
                        engine.wait_ge(it[1], it[2])
                    elif it[0] == "op":
                        it[1](engine).then_inc(it[2], 1)
                    else:
                        it[1](engine).then_inc(it[2], 16)

            @block.tensor
            def _(e):
                run(e, self.q["pe"])

            @block.scalar
            def _(e):
                run(e, self.q["act"])

            @block.vector
            def _(e):
                run(e, self.q["dve"])

            @block.gpsimd
            def _(e):
                run(e, self.q["pool"])

            @block.sync
            def _(e):
                run(e, self.q["sp"])


def build_program(phases="ABC", debug=False):
    nc = bass.Bass("TRN2", target_bir_lowering=False)
    dram_in = lambda n, s, dt=F32: nc.dram_tensor(n, s, dt, kind="ExternalInput").ap()
    xk = dram_in("xk", [S, D])
    w_in = dram_in("w_in", [D, 2048])
    pool_w = dram_in("pool_w", [128, 4, 128])
    w_out = dram_in("w_out", [D, D])
    w_up = dram_in("w_up", [D, DFF])
    w_down = dram_in("w_down", [DFF, D])
    g1bc_d = dram_in("g1bc", [128, D])
    g2bc_d = dram_in("g2bc", [128, D])
    gfbc_d = dram_in("gfbc", [128, D])
    pscale_d = dram_in("pscale", [128, 4])
    gout_d = dram_in("gout", [128, 8])
    cst_d = dram_in("cst", [128, C_END], BF16)
    y = nc.dram_tensor("y", [2048, D], F32, kind="ExternalOutput").ap()
    dbg = {}
    if debug:
        dbg["kT"] = nc.dram_tensor("dbg_kT", [128, 4 * S], BF16, kind="ExternalOutput").ap()
        dbg["v"] = nc.dram_tensor("dbg_v", [128, 32 * 512], BF16, kind="ExternalOutput").ap()
        dbg["qT"] = nc.dram_tensor("dbg_qT", [128, 4 * 2048], BF16, kind="ExternalOutput").ap()
        dbg["yp"] = nc.dram_tensor("dbg_yp", [128, 4 * 2048], BF16, kind="ExternalOutput").ap()
        dbg["ya"] = nc.dram_tensor("dbg_ya", [128, 4 * 2048], BF16, kind="ExternalOutput").ap()
        dbg["rr"] = nc.dram_tensor("dbg_rr", [128, 32], F32, kind="ExternalOutput").ap()

    with contextlib.ExitStack() as st:
        P = Prog(nc, st)
        sb = lambda n, s, dt: st.enter_context(nc.sbuf_tensor("s_" + n, s, dt))

        cst = sb("cst", [128, C_END], BF16)
        pscale = sb("pscale", [128, 4], F32)
        gout = sb("gout", [128, 8], F32)
        pwst = sb("pwst", [128, 4, 128], F32)
        pwb = sb("pwb", [128, 4, 128], BF16)
        ones = sb("ones", [128, 2], F32)
        ypT = sb("ypT", [128, 4, 2048], BF16)
        rr = sb("rr", [128, 32], F32)
        stat = sb("stat", [128, 512], F32)
        AR = sb("arena", [128, ARENA_F32], F32)

        def view(off_bytes, nbytes, dt):
            assert off_bytes % 4 == 0 and nbytes % 4 == 0 and off_bytes + nbytes <= ARENA_F32 * 4
            a = AR[:, off_bytes // 4:(off_bytes + nbytes) // 4]
            return a if dt == F32 else a.bitcast(dt)

        KB = 1024
        PSALL = st.enter_context(nc.psum_tensor("psall", [128, 8 * 512], F32))
        ps = [PSALL[:, i * 512:(i + 1) * 512] for i in range(8)]
        psb = [Buf() for _ in range(8)]

        stat_col = [0]

        def new_stat(n):
            c = stat_col[0]
            stat_col[0] += n
            assert stat_col[0] <= 512
            return stat[:, c:c + n], Buf()

        def barrier():
            for e in ENGS:
                deps = [("eng", o, P.cnt[o]) for o in ENGS if P.cnt[o] > 0]
                P.wait(e, deps)

        csem = P.new_dma_sem()
        b_cst = Buf()
        g1bc = view(171 * KB, 4 * KB, F32)
        for dst, src in ((cst[:], cst_d), (g1bc, g1bc_d), (pscale[:], pscale_d), (gout[:], gout_d),
                         (pwst[:], pool_w)):
            P.D("sp", csem, out=dst, in_=src, writes=[b_cst])
        b_pwb = Buf()
        b_ones = Buf()
        P.I("dve", "tensor_copy", out=pwb[:], in_=pwst[:], reads=[b_cst], writes=[b_pwb])
        P.I("dve", "memset", ones[:], 1.0, writes=[b_ones])

        kT = view(0, 32 * KB, BF16).rearrange("p (c t) -> p c t", c=4)
        vv = view(32 * KB, 32 * KB, BF16).rearrange("p (b e) -> p b e", b=32)
        qT = view(64 * KB, 16 * KB, BF16).rearrange("p (c t) -> p c t", c=4)
        yaT = view(80 * KB, 16 * KB, BF16).rearrange("p (c t) -> p c t", c=4)
        b_kT = [Buf() for _ in range(8)]
        b_v = [Buf() for _ in range(8)]
        b_qT = [Buf() for _ in range(4)]
        b_yp = [Buf() for _ in range(4)]
        b_ya = [Buf() for _ in range(4)]
        b_rr = Buf()
        P.I("dve", "memset", rr[:], 1.0, writes=[b_rr])

        def alias(dst, srcs):
            for sbuf in srcs:
                for k, t in list(sbuf.r.items()) + ([(None, sbuf.w)] if sbuf.w is not None else []):
                    key = (t[1] if t[0] == "eng" else ("dma", id(t[1])))
                    if key not in dst.r or dst.r[key][2] < t[2]:
                        dst.r[key] = t

        def phase_a():
            Y = 80 * KB
            w_inb = view(Y, 32 * KB, BF16).rearrange("p (c e) -> p c e", c=8)
            stg = [view(Y + 32 * KB + i * 4 * KB, 4 * KB, F32) for i in range(2)]
            xbs = [view(Y + 40 * KB + i * 4 * KB, 4 * KB, F32) for i in range(3)]
            xh = [view(Y + 52 * KB + i * 2 * KB, 2 * KB, BF16) for i in range(2)]
            xhT = [view(Y + 56 * KB + i * 8 * KB, 8 * KB, BF16).rearrange("p (c t) -> p c t", c=8)
                   for i in range(2)]
            uu = view(Y + 72 * KB, 5 * KB, BF16).rearrange("p (b e) -> p b e", b=5)
            pooledT = view(Y + 77 * KB, 4 * KB, BF16).rearrange("p (g t) -> p g t", g=4)
            sqp = view(Y + 81 * KB, 8 * KB, F32).rearrange("p (g t) -> p g t", g=4)
            junk = view(Y + 89 * KB, 2 * KB, BF16)
            yp32 = view(Y + 32 * KB, 2 * KB, F32)
            stg = stg + [view(Y + 81 * KB + i * 4 * KB, 4 * KB, F32) for i in range(2)]
            b_stg = [Buf() for _ in range(4)]
            b_win = [Buf() for _ in range(8)]
            b_xb = [Buf() for _ in range(3)]
            b_xh = [Buf() for _ in range(2)]
            b_xhT = [Buf() for _ in range(2)]
            b_u = [Buf() for _ in range(5)]
            b_pooled = Buf()
            b_sqp = Buf()
            b_yp32 = b_stg[0]
            b_junk = Buf()
            xsem = [P.new_dma_sem() for _ in range(3)]
            ssem = [P.new_dma_sem() for _ in range(4)]

            w_in_v = w_in.rearrange("(c p) e -> p c e", p=128)
            WIN_ORDER = (2, 3, 0, 1)

            def load_win(i):
                cb, cp = WIN_ORDER[i // 4], i % 4
                s_ = i % 4
                P.D("sp", ssem[s_], out=stg[s_].rearrange("p (c e) -> p c e", c=2),
                    in_=w_in_v[:, cp * 2:cp * 2 + 2, cb * 512:(cb + 1) * 512], writes=[b_stg[s_]])
                eng = ("dve", "act")[i % 2]
                dst = w_inb[:, cp * 2:cp * 2 + 2, cb * 512:(cb + 1) * 512]
                src = stg[s_].rearrange("p (c e) -> p c e", c=2)
                if eng == "act":
                    P.I("act", "copy", out=dst, in_=src, reads=[b_stg[s_]], writes=[b_win[cb]])
                else:
                    P.I(eng, "tensor_copy", out=dst, in_=src, reads=[b_stg[s_]], writes=[b_win[cb]])

            def load_x(g):
                xb = xbs[g % 3]
                P.D("sp", xsem[g % 3], out=xb, in_=xk[g * 128:(g + 1) * 128, :],
                      writes=[b_xb[g % 3]])

            pacc_rr = [0]

            def pacc():
                i = 2 + pacc_rr[0] % 6
                pacc_rr[0] += 1
                return ps[i], psb[i]

            evac_rr = [0]

            def evac(out_ap, in_ap, reads, writes, scale=None):
                evac_rr[0] ^= 1
                if scale is not None and OPTS["a_q"] == 3:
                    return P.I("act", "mul", out=out_ap, in_=in_ap, mul=scale, reads=reads, writes=writes)
                if scale is not None and OPTS["a_q"] == 4:
                    return P.I("dve", "tensor_scalar", out=out_ap, in0=in_ap, scalar1=scale, scalar2=None,
                               op0=ALU.mult, reads=reads, writes=writes)
                if scale is not None and OPTS["a_q"] == 5:
                    return P.I("act", "activation", out=out_ap, in_=in_ap, func=AF.Copy, scale=scale, reads=reads, writes=writes)
                if evac_rr[0]:
                    if scale is None:
                        return P.I("act", "copy", out=out_ap, in_=in_ap, reads=reads, writes=writes)
                    return P.I("act", "activation", out=out_ap, in_=in_ap, func=AF.Copy, scale=scale,
                                reads=reads, writes=writes)
                if scale is None:
                    return P.I("dve", "tensor_copy", out=out_ap, in_=in_ap, reads=reads, writes=writes)
                return P.I("dve", "tensor_scalar", out=out_ap, in0=in_ap, scalar1=scale, scalar2=None,
                                                             op0=ALU.mult, reads=reads, writes=writes)

            def norm_front(g, kt, j):
                xb, bxb = xbs[g % 3], b_xb[g % 3]
                ss, bss = new_stat(3)
                P.I("act", "activation", out=junk, in_=xb, func=AF.Square, accum_out=ss[:, 0:1],
                     reads=[bxb], writes=[b_junk, bss])
                P.I("act", "activation", out=ss[:, 1:2], in_=ss[:, 0:1], func=AF.Sqrt, scale=1.0 / D, bias=EPS,
                     reads=[bss], writes=[bss])
                P.I("dve", "reciprocal", out=ss[:, 2:3], in_=ss[:, 1:2], reads=[bss], writes=[bss])
                xhb, bxh = xh[g % 2], b_xh[g % 2]
                P.I("dve", "scalar_tensor_tensor", out=xhb, in0=xb, scalar=ss[:, 2:3], in1=g1bc,
                                                             op0=ALU.mult, op1=ALU.mult,
                     reads=[bxb, bss, b_cst], writes=[bxh])

            def norm_back(g, kt, j):
                xhb, bxh = xh[g % 2], b_xh[g % 2]
                pt = ps[g % 2][:].bitcast(BF16).rearrange("p (c t) -> p c t", c=8)
                for c in range(8):
                    P.I("pe", "transpose", out=pt[:, c, :], in_=xhb[:, c * 128:(c + 1) * 128], identity=cst[:, C_ID:C_ID + 128],
                         reads=[bxh, b_cst], writes=[psb[g % 2]])
                evac(xhT[kt % 2][:, :, j * 128:(j + 1) * 128], pt, [psb[g % 2]], [b_xhT[kt % 2]])

            def proj_T(dst_ap, dst_buf, col0, xT, bxT, scale=None):
                pa, pb = pacc()
                for c in range(8):
                    P.I("pe", "matmul", pa[:], lhsT=w_inb[:, c, col0:col0 + 128], rhs=xT[:, c, :],
                                                       start=(c == 0), stop=(c == 7),
                         reads=[b_win[col0 // 512], bxT], writes=[pb])
                evac(dst_ap, pa[:], [pb], [dst_buf], scale=scale)

            def proj_tok(dst_ap, dst_buf, col0, xT, bxT, j):
                pa, pb = pacc()
                for c in range(8):
                    P.I("pe", "matmul", pa[:], lhsT=xT[:, c, j * 128:(j + 1) * 128], rhs=w_inb[:, c, col0:col0 + 512],
                                                       start=(c == 0), stop=(c == 7),
                         reads=[b_win[col0 // 512], bxT], writes=[pb])
                evac(dst_ap, pa[:], [pb], [dst_buf])

            NBLK = 32
            nx = 0
            for i in range(3):
                load_x(nx)
                nx += 1
            for i in range(4):
                load_win(i)

            def tile_items(kt):
                own = kt % 2 == 1
                m = kt // 2
                xT, bxT = xhT[kt % 2], b_xhT[kt % 2]
                items = []
                for ec in range(4):
                    items.append(lambda ec=ec: proj_T(kT[:, ec, kt * 512:(kt + 1) * 512], b_kT[kt], 1024 + ec * 128, xT, bxT))
                for j in range(4):
                    items.append(lambda j=j: proj_tok(vv[:, kt * 4 + j, :], b_v[kt], 1536, xT, bxT, j))
                if not own:
                    items.append(lambda: proj_tok(uu[:, 0, :], b_u[0], 0, xT, bxT, 3))
                    return items
                for ec in range(4):
                    items.append(lambda ec=ec: proj_T(qT[:, ec, m * 512:(m + 1) * 512], b_qT[m], 512 + ec * 128, xT, bxT, scale=0.125))
                first = [lambda j=j: proj_tok(uu[:, 1 + j, :], b_u[1 + j], 0, xT, bxT, j) for j in range(4)]
                stages = pool_stages(m)
                mixed = []
                for i, it in enumerate(items):
                    mixed.append(it)
                    if i < len(stages):
                        mixed.append(stages[i])
                mixed.extend(stages[len(items):])
                return first + mixed

            def pool_stages(m):
                st = []

                def band(g):
                    pa, pb = pacc()
                    for j in range(4):
                        bcol = (C_BAND0 if (m == 0 and j == 0) else C_BAND) + g * 128
                        P.I("pe", "matmul", pa[:, j * 128:(j + 1) * 128], lhsT=uu[:, 1 + j, g * 128:(g + 1) * 128],
                            rhs=cst[:, bcol:bcol + 128], start=True, stop=False, reads=[b_u[1 + j], b_cst], writes=[pb])
                        P.I("pe", "matmul", pa[:, j * 128:(j + 1) * 128], lhsT=uu[:, j, g * 128:(g + 1) * 128],
                            rhs=cst[:, C_BANDP + g * 128:C_BANDP + (g + 1) * 128], start=False, stop=True,
                            reads=[b_u[j], b_cst], writes=[pb])
                    evac(pooledT[:, g, :], pa[:], [pb], [b_pooled])

                def mapped(g):
                    pm, pmb = pacc()
                    P.I("pe", "matmul", pm[:], lhsT=pwb[:, g, :], rhs=pooledT[:, g, :], start=True, stop=True,
                        reads=[b_pwb, b_pooled], writes=[pmb])
                    P.I("dve", "tensor_scalar", out=yp32, in0=pm[:], scalar1=pscale[:, g:g + 1], scalar2=None, op0=ALU.mult,
                        reads=[pmb, b_cst], writes=[b_yp32])
                    P.I("pool", "tensor_copy", out=ypT[:, g, m * 512:(m + 1) * 512], in_=yp32, reads=[b_yp32], writes=[b_yp[m]])
                    P.I("act", "activation", out=sqp[:, g, :], in_=yp32, func=AF.Square, reads=[b_yp32], writes=[b_sqp])

                def stats():
                    ssq, bssq = new_stat(8)
                    pq, pqb = pacc()
                    for j in range(4):
                        for g in range(4):
                            P.I("pe", "matmul", pq[:, j:j + 1], lhsT=sqp[:, g, j * 128:(j + 1) * 128], rhs=ones[:, 0:1],
                                start=(g == 0), stop=(g == 3), reads=[b_sqp, b_ones], writes=[pqb])
                    P.I("act", "activation", out=ssq[:, 0:4], in_=pq[:, 0:4], func=AF.Sqrt, scale=1.0 / 512, bias=EPS,
                        reads=[pqb], writes=[bssq])
                    P.I("dve", "reciprocal", out=rr[:, m * 4:(m + 1) * 4], in_=ssq[:, 0:4], reads=[bssq], writes=[b_rr])

                for g in range(4):
                    st.append(lambda g=g: band(g))
                for g in range(4):
                    st.append(lambda g=g: mapped(g))
                st.append(stats)
                return st

            def next_x():
                if nxs[0] < NBLK:
                    load_x(nxs[0])
                    nxs[0] += 1

            nxs = [nx]
            for j in range(4):
                norm_front(j, 0, j)
                norm_back(j, 0, j)
                next_x()
            for i in range(4, 16):
                load_win(i)
            alias(b_sqp, [b_stg[2], b_stg[3]])
            for kt in range(8):
                own = kt % 2 == 1
                m = kt // 2
                items = tile_items(kt)
                n = len(items)
                for q in range(4):
                    if kt + 1 < 8:
                        norm_front((kt + 1) * 4 + q, kt + 1, q)
                    for it in items[q * n // 4:(q + 1) * n // 4]:
                        it()
                    if kt + 1 < 8:
                        norm_back((kt + 1) * 4 + q, kt + 1, q)
                        next_x()
                if not own:
                    continue

        c_stg = [view(128 * KB + i * 8 * KB, 8 * KB, F32) for i in range(2)]
        c_w_outb = view(144 * KB, 16 * KB, BF16).rearrange("p (c d) -> p c d", c=8)
        c_g2bc = view(160 * KB, 4 * KB, F32)
        c_b_wout = Buf()
        c_b_stg = [Buf() for _ in range(2)]
        c_b_g = Buf()
        c_ssem = [P.new_dma_sem() for _ in range(2)]
        c_stg_rr = [0]

        def prefetch_c():
            gsem0 = P.new_dma_sem()
            P.D("sp", gsem0, c_g2bc, g2bc_d, writes=[c_b_g])
            w_out_v = w_out.rearrange("(c p) d -> p c d", p=128)
            for gi in range(4):
                s_ = c_stg_rr[0] % 2
                c_stg_rr[0] += 1
                P.D("sp", c_ssem[s_], c_stg[s_].rearrange("p (c d) -> p c d", c=2), w_out_v[:, gi * 2:gi * 2 + 2, :], writes=[c_b_stg[s_]])
                sv = c_stg[s_].rearrange("p (c d) -> p c d", c=2)
                for cc in range(2):
                    c = gi * 2 + cc
                    P.I("dve", "tensor_scalar", out=c_w_outb[:, c, :], in0=sv[:, cc, :], scalar1=gout[:, c:c + 1],
                        scalar2=None, op0=ALU.mult, reads=[c_b_stg[s_], b_cst], writes=[c_b_wout])

        def phase_b():
            Y = 96 * KB
            Eb = [view(Y + i * 4 * KB, 4 * KB, F32) for i in range(2)]
            Lb = [view(Y + 8 * KB + i * 2 * KB, 2 * KB, BF16) for i in range(2)]
            Ab = [view(Y + 12 * KB + i * 2 * KB, 2 * KB, BF16) for i in range(2)]
            Rt = [view(Y + 16 * KB + i * KB, KB, BF16) for i in range(4)]
            sqa = view(Y + 20 * KB, 8 * KB, F32).rearrange("p (c t) -> p c t", c=4)
            o32s = [view(Y + 28 * KB + i * 2 * KB, 2 * KB, F32) for i in range(2)]
            pending = []
            b_E = [Buf() for _ in range(2)]
            b_L = [Buf() for _ in range(2)]
            b_A = [Buf() for _ in range(2)]
            b_Rt = [Buf() for _ in range(4)]
            b_sqa = Buf()
            b_o32s = [Buf(), Buf()]
            for i in range(4):
                P.I("pool", "memset", Rt[i], 0.0, writes=[b_Rt[i]])
            def zero_masked(q):
                P.I("pool", "memset", Lb[q].rearrange("p (h t) -> p h t", h=2)[:, :, 0:384], 0.0, writes=[b_L[q]])
                P.I("pool", "memset", Ab[q].rearrange("p (h t) -> p h t", h=2)[:, :, 0:384], 0.0, writes=[b_A[q]])

            for q in range(2):
                zero_masked(q)
            mhalf, b_mhalf = new_stat(4)
            P.I("pool", "memset", mhalf, -0.5, writes=[b_mhalf])
            for i in (6, 7):
                P.I("dve", "memset", ps[i], 0.0, writes=[psb[i]])
            b_z = [[Buf(), Buf()] for _ in range(2)]
            b_ops = [[Buf(), Buf()] for _ in range(2)]
            b_rps = [[Buf(), Buf()] for _ in range(2)]
            for q in range(2):
                for hp in range(2):
                    b_z[q][hp].w, b_z[q][hp].r = psb[2 * q + hp].w, dict(psb[2 * q + hp].r)
                    b_ops[q][hp].w, b_ops[q][hp].r = psb[4 + q].w, dict(psb[4 + q].r)
                    b_rps[q][hp].w, b_rps[q][hp].r = psb[6 + q].w, dict(psb[6 + q].r)

            preissued = set()
            for m in range(4):
                nk = 8 * (m + 1)
                for dp in range(2):
                    def QK(q, tau, m=m, dp=dp, nk=nk):
                        ec = dp * 2 + q
                        kb = nk - 1 - tau
                        di = kb - (nk - 4)
                        for hp in range(2):
                            z = ps[2 * q + hp]
                            pr = slice(hp * 64, hp * 64 + 64)
                            P.I("pe", "matmul", z, lhsT=kT[pr, ec, kb * 128:(kb + 1) * 128], rhs=qT[pr, ec, m * 512:(m + 1) * 512],
                                start=True, stop=(di < 0), reads=[b_kT[kb // 4], b_qT[m]], writes=[b_z[q][hp]])
                        if di >= 0:
                            for hp in range(2):
                                P.I("pe", "matmul", ps[2 * q + hp], lhsT=cst[:, C_ID:C_ID + 128],
                                    rhs=cst[:, C_MASK + di * 512:C_MASK + (di + 1) * 512],
                                    start=False, stop=True, reads=[b_cst], writes=[b_z[q][hp]])

                    def zpair(q):
                        return PSALL[:, 2 * q * 512:(2 * q + 2) * 512]

                    def cols(ap, tau):
                        c0 = max(0, 3 - tau) * 128
                        if c0 == 0:
                            return ap
                        return ap.rearrange("p (h t) -> p h t", h=2)[:, :, c0:512]

                    def EXP1(q, tau):
                        P.I("act", "activation", out=cols(Eb[q], tau), in_=cols(zpair(q), tau), func=AF.Exp, reads=b_z[q], writes=[b_E[q]])

                    def LN(q, tau):
                        P.I("act", "activation", out=cols(Lb[q], tau), in_=cols(Eb[q], tau), func=AF.Ln, bias=1.0, reads=[b_E[q]], writes=[b_L[q]])

                    def TRISEL(q, tau):
                        ri = q * 2 + tau % 2
                        for hp in range(2):
                            z = ps[2 * q + hp]
                            P.I("pe", "matmul", z, lhsT=cst[:, C_TRI:C_TRI + 128], rhs=Lb[q][:, hp * 512:(hp + 1) * 512],
                                start=False, stop=True, skip_group_check=True, reads=[b_L[q], b_cst], writes=[b_z[q][hp]])
                            if tau > 0 and OPTS["b_lvl"] != 10:
                                r0 = hp * 64
                                P.I("pe", "matmul", z, lhsT=cst[r0:r0 + 33, C_SEL:C_SEL + 128], rhs=Rt[ri][r0:r0 + 33, :],
                                    start=False, stop=True, skip_group_check=True, reads=[b_Rt[ri], b_cst], writes=[b_z[q][hp]])
                        if tau > 0 and OPTS["b_lvl"] == 10:
                            for hp in range(2):
                                r0 = hp * 64
                                P.I("pe", "matmul", ps[2 * q + hp], lhsT=cst[r0:r0 + 33, C_SEL:C_SEL + 128], rhs=Rt[ri][r0:r0 + 33, :],
                                    start=False, stop=True, skip_group_check=True, reads=[b_Rt[ri], b_cst], writes=[b_z[q][hp]])

                    def COL(q, tau):
                        if tau >= nk - 1:
                            return
                        rn = q * 2 + (tau + 1) % 2
                        RPS = ps[6 + q]
                        for hp in range(2):
                            r0 = hp * 64
                            P.I("pe", "matmul", RPS[r0:r0 + 33, :], lhsT=cst[:, C_NONE:C_NONE + 33], rhs=Lb[q][:, hp * 512:(hp + 1) * 512],
                                start=(tau == 0), stop=True, skip_group_check=(tau > 0), reads=[b_L[q], b_cst], writes=[b_rps[q][hp]])
                        P.I("dve", "tensor_copy", out=Rt[rn][0:97, :], in_=RPS[0:97, :], reads=b_rps[q], writes=[b_Rt[rn]])
                        for hp in range(2):
                            r1 = hp * 64 + 32
                            P.I("dve", "tensor_tensor", out=Rt[rn][r1:r1 + 1, :], in0=RPS[r1:r1 + 1, :], in1=Rt[rn][r1:r1 + 1, :],
                                op=ALU.subtract, reads=[b_rps[q][hp], b_Rt[rn]], writes=[b_Rt[rn]])

                    def EXP2(q, tau):
                        P.I("act", "activation", out=cols(Ab[q], tau), in_=cols(zpair(q), tau), func=AF.Exp, reads=b_z[q], writes=[b_A[q]])

                    def AV(q, tau):
                        kb = nk - 1 - tau
                        OPS = ps[4 + q]
                        for hp in range(2):
                            h = (dp * 2 + q) * 2 + hp
                            pr = slice(hp * 64, hp * 64 + 64)
                            P.I("pe", "matmul", OPS[pr, :], lhsT=vv[:, kb, h * 64:(h + 1) * 64], rhs=Ab[q][:, hp * 512:(hp + 1) * 512],
                                start=(tau == 0), stop=(tau == nk - 1), reads=[b_v[kb // 4], b_A[q]], writes=[b_ops[q][hp]])

                    def WARM(q, tau, n):
                        if tau == 0 or tau == nk - 1:
                            return
                        for _ in range(n):
                            P.I("pe", "matmul", ps[4 + q][0:64, :], lhsT=cst[:, C_ZERO:C_ZERO + 64], rhs=cst[:, C_MASK:C_MASK + 512],
                                start=False, stop=False, reads=[b_cst], writes=[b_ops[q][0]])

                    if (m, dp) not in preissued:
                        for q in range(2):
                            QK(q, 0)
                    for tau in range(nk):
                        if tau == 2 and pending:
                            for fn in pending:
                                fn()
                            del pending[:]
                        if OPTS["b_order"] == 0:
                            EXP1(0, tau)
                            LN(0, tau)
                            TRISEL(0, tau)
                            EXP1(1, tau)
                            LN(1, tau)
                            TRISEL(1, tau)
                            COL(0, tau)
                            COL(1, tau)
                            WARM(0, tau, OPTS["b_dummy2"])
                            EXP2(0, tau)
                            AV(0, tau)
                            if tau + 1 < nk:
                                QK(0, tau + 1)
                            EXP2(1, tau)
                            AV(1, tau)
                            if tau + 1 < nk:
                                QK(1, tau + 1)
                            WARM(1, tau, OPTS["b_dummy"])
                        elif OPTS["b_order"] == 2:
                            EXP1(0, tau)
                            EXP1(1, tau)
                            LN(0, tau)
                            TRISEL(0, tau)
                            LN(1, tau)
                            TRISEL(1, tau)
                            COL(0, tau)
                            COL(1, tau)
                            WARM(0, tau, OPTS["b_dummy2"])
                            EXP2(0, tau)
                            AV(0, tau)
                            if tau + 1 < nk:
                                QK(0, tau + 1)
                            else:
                                zero_masked(0)
                            EXP2(1, tau)
                            AV(1, tau)
                            if tau + 1 < nk:
                                QK(1, tau + 1)
                            else:
                                zero_masked(1)
                            WARM(1, tau, OPTS["b_dummy"])
                        else:
                            EXP1(0, tau)
                            EXP1(1, tau)
                            WARM(0, tau, OPTS["b_dummy"])
                            LN(0, tau)
                            TRISEL(0, tau)
                            COL(0, tau)
                            LN(1, tau)
                            TRISEL(1, tau)
                            COL(1, tau)
                            WARM(1, tau, OPTS["b_dummy2"])
                            EXP2(0, tau)
                            AV(0, tau)
                            if tau + 1 < nk:
                                QK(0, tau + 1)
                            EXP2(1, tau)
                            AV(1, tau)
                            if tau + 1 < nk:
                                QK(1, tau + 1)
                    for q in range(2):
                        ec = dp * 2 + q
                        P.I("dve", "tensor_copy", out=o32s[q], in_=ps[4 + q], reads=b_ops[q], writes=[b_o32s[q]])
                        P.I("pool", "tensor_copy", out=yaT[:, ec, m * 512:(m + 1) * 512], in_=o32s[q], reads=[b_o32s[q]], writes=[b_ya[m]])
                        pending.append(lambda q=q, ec=ec: P.I("act", "activation", out=sqa[:, ec, :], in_=o32s[q], func=AF.Square,
                                                              reads=[b_o32s[q]], writes=[b_sqa]))
                def slot_stats(m=m):
                    ssq, bssq = new_stat(4)
                    RPS = ps[6]
                    for j in range(4):
                        for ec in range(4):
                            P.I("pe", "matmul", RPS[:, j:j + 1], lhsT=sqa[:, ec, j * 128:(j + 1) * 128], rhs=ones[:, 0:1],
                                start=(ec == 0), stop=(ec == 3), reads=[b_sqa, b_ones], writes=b_rps[0])
                    P.I("dve", "tensor_copy", out=ssq[:, 0:4], in_=RPS[:, 0:4], reads=b_rps[0], writes=[bssq])
                    P.I("pool", "tensor_scalar", out=ssq[:, 0:4], in0=ssq[:, 0:4], scalar1=1.0 / 512, scalar2=EPS, op0=ALU.mult, op1=ALU.add,
                        reads=[bssq], writes=[bssq])
                    P.I("pool", "tensor_tensor", out=rr[:, 16 + m * 4:16 + (m + 1) * 4], in0=ssq[:, 0:4], in1=mhalf[:, 0:4], op=ALU.pow,
                        reads=[bssq, b_mhalf], writes=[b_rr])
                if m + 1 < 4:
                    for q in range(2):
                        QK(q, 0, m=m + 1, dp=0, nk=8 * (m + 2))
                    preissued.add((m + 1, 0))
                for fn in pending:
                    fn()
                del pending[:]
                slot_stats()

        def phase_c():
            hb = view(0, 64 * KB, F32).rearrange("p (j d) -> p j d", j=16)
            wupb = [view(64 * KB + i * 8 * KB, 8 * KB, BF16).rearrange("p (c f) -> p c f", c=8) for i in range(2)]
            hT = view(96 * KB, 32 * KB, BF16).rearrange("p (c t) -> p c t", c=8)
            stg = c_stg
            w_outb = c_w_outb
            wdnb = [view(144 * KB + i * 8 * KB, 8 * KB, BF16).rearrange("p (c d) -> p c d", c=4) for i in range(2)]
            g2bc = c_g2bc
            hh = [view(164 * KB + i * 2 * KB, 2 * KB, BF16) for i in range(2)]
            junk = view(168 * KB, 2 * KB, BF16)
            actG = [view(160 * KB, 16 * KB, BF16).rearrange("p (f t) -> p f t", f=4),
                    view(80 * KB, 16 * KB, BF16).rearrange("p (f t) -> p f t", f=4)]
            ypf = ypT[:].rearrange("p c t -> p (c t)").bitcast(F32)
            gfbc = ypf[:, 0:1024]
            sqs = [ypf[:, 1024 + i * 512:1024 + (i + 1) * 512] for i in range(2)]
            junk2 = ypf[:, 2048:2560].bitcast(BF16)
            b_wout = c_b_wout
            b_stg = c_b_stg
            b_junk = Buf()
            b_junk2 = Buf()
            b_wup = [Buf() for _ in range(2)]
            b_wdn = [Buf() for _ in range(2)]
            b_g = c_b_g
            b_gf = Buf()
            b_hb = [Buf() for _ in range(16)]
            b_hh = [Buf() for _ in range(2)]
            b_hT = [Buf() for _ in range(4)]
            b_act = [Buf() for _ in range(2)]
            b_sqs = [Buf() for _ in range(2)]
            ssem = c_ssem
            hsem = [P.new_dma_sem() for _ in range(4)]
            osem = [P.new_dma_sem() for _ in range(4)]
            gsem = P.new_dma_sem()
            stg_rr = c_stg_rr
            cast_rr = [0]

            for m in range(4):
                src = xk[(2 * m + 1) * 512:(2 * m + 2) * 512, :].rearrange("(j p) d -> p j d", p=128)
                P.D("sp", hsem[m], hb[:, m * 4:(m + 1) * 4, :], src, writes=[b_hb[m * 4 + j] for j in range(4)])

            def stage_load(src_ap, pattern, **kw):
                s_ = stg_rr[0] % 2
                stg_rr[0] += 1
                P.D("sp", ssem[s_], stg[s_].rearrange(pattern, **kw), src_ap, writes=[b_stg[s_]])
                return s_

            def outproj(blk):
                m = blk // 4
                bh = b_hb[blk]
                for dh in range(2):
                    k = (blk * 2 + dh) % 2
                    p1, p1b = ps[k * 2], psb[k * 2]
                    p2, p2b = ps[k * 2 + 1], psb[k * 2 + 1]
                    for ec in range(4):
                        P.I("pe", "matmul", p1[:], lhsT=ypT[:, ec, blk * 128:(blk + 1) * 128], rhs=w_outb[:, ec, dh * 512:(dh + 1) * 512],
                            start=(ec == 0), stop=(ec == 3), reads=[b_yp[m], b_wout], writes=[p1b])
                    for ec in range(4):
                        P.I("pe", "matmul", p2[:], lhsT=yaT[:, ec, blk * 128:(blk + 1) * 128], rhs=w_outb[:, 4 + ec, dh * 512:(dh + 1) * 512],
                            start=(ec == 0), stop=(ec == 3), reads=[b_ya[m], b_wout], writes=[p2b])
                    hs = hb[:, blk, dh * 512:(dh + 1) * 512]
                    P.I("dve", "scalar_tensor_tensor", out=hs, in0=p1[:], scalar=rr[:, blk:blk + 1], in1=hs, op0=ALU.mult, op1=ALU.add,
                        reads=[p1b, b_rr, bh], writes=[bh])
                    P.I("dve", "scalar_tensor_tensor", out=hs, in0=p2[:], scalar=rr[:, 16 + blk:17 + blk], in1=hs, op0=ALU.mult, op1=ALU.add,
                        reads=[p2b, b_rr, bh], writes=[bh])

            def norm2(blk):
                m = blk // 4
                bh = b_hb[blk]
                ss, bss = new_stat(3)
                P.I("act", "activation", out=junk, in_=hb[:, blk, :], func=AF.Square, accum_out=ss[:, 0:1], reads=[bh], writes=[b_junk, bss])
                P.I("act", "activation", out=ss[:, 1:2], in_=ss[:, 0:1], func=AF.Sqrt, scale=1.0 / D, bias=EPS, reads=[bss], writes=[bss])
                P.I("dve", "reciprocal", out=ss[:, 2:3], in_=ss[:, 1:2], reads=[bss], writes=[bss])
                hhb, bhh = hh[blk % 2], b_hh[blk % 2]
                P.I("dve", "scalar_tensor_tensor", out=hhb, in0=hb[:, blk, :], scalar=ss[:, 2:3], in1=g2bc, op0=ALU.mult, op1=ALU.mult,
                    reads=[bh, bss, b_g], writes=[bhh])

            def norm2_back(blk):
                m = blk // 4
                hhb, bhh = hh[blk % 2], b_hh[blk % 2]
                pi = 4 + blk % 2
                pt = ps[pi][:].bitcast(BF16).rearrange("p (c t) -> p c t", c=8)
                for c in range(8):
                    P.I("pe", "transpose", out=pt[:, c, :], in_=hhb[:, c * 128:(c + 1) * 128], identity=cst[:, C_ID:C_ID + 128],
                        reads=[bhh, b_cst], writes=[psb[pi]])
                P.I("act", "copy", out=hT[:, :, blk * 128:(blk + 1) * 128], in_=pt, reads=[psb[pi]], writes=[b_hT[m]])

            for b in range(18):
                if b < 16:
                    outproj(b)
                if 0 <= b - 1 < 16:
                    norm2(b - 1)
                if 0 <= b - 2 < 16:
                    norm2_back(b - 2)

            w_up_v = w_up.rearrange("(c p) f -> p c f", p=128)
            w_dn_v = w_down.rearrange("(fc p) d -> p fc d", p=128)

            def cast(dst, src, reads, writes):
                cast_rr[0] += 1
                if cast_rr[0] % 2:
                    P.I("act", "copy", out=dst, in_=src, reads=reads, writes=writes)
                else:
                    P.I("pool", "tensor_copy", out=dst, in_=src, reads=reads, writes=writes)

            def load_w(G):
                for half in range(2):
                    f0 = G * 512 + half * 256
                    s = stage_load(w_up_v[:, :, f0:f0 + 256], "p (c f) -> p c f", c=8)
                    cast(wupb[G % 2][:, :, half * 256:(half + 1) * 256], stg[s].rearrange("p (c f) -> p c f", c=8), [b_stg[s]], [b_wup[G % 2]])
                for half in range(2):
                    fc0 = G * 4 + half * 2
                    s = stage_load(w_dn_v[:, fc0:fc0 + 2, :], "p (c d) -> p c d", c=2)
                    cast(wdnb[G % 2][:, half * 2:(half + 1) * 2, :], stg[s].rearrange("p (c d) -> p c d", c=2), [b_stg[s]], [b_wdn[G % 2]])

            up_rr = [0]

            def up(G):
                wb, bw = wupb[G % 2], b_wup[G % 2]
                for f4 in range(4):
                    for tt in range(4):
                        pi = up_rr[0] % 4
                        up_rr[0] += 1
                        pu, pub = ps[pi], psb[pi]
                        for c in range(8):
                            P.I("pe", "matmul", pu[:], lhsT=wb[:, c, f4 * 128:(f4 + 1) * 128], rhs=hT[:, c, tt * 512:(tt + 1) * 512],
                                start=(c == 0), stop=(c == 7), reads=[bw, b_hT[tt]], writes=[pub])
                        sq, bsq = sqs[pi % 2], b_sqs[pi % 2]
                        P.I("act", "copy", out=sq, in_=pu[:], reads=[pub], writes=[bsq])
                        P.I("dve", "scalar_tensor_tensor", out=actG[G % 2][:, f4, tt * 512:(tt + 1) * 512], in0=pu[:], scalar=0.0, in1=sq,
                            op0=ALU.max, op1=ALU.mult, reads=[pub, bsq], writes=[b_act[G % 2]])

            dn_rr = [0]

            def down(G, last=False):
                wb, bw = wdnb[G % 2], b_wdn[G % 2]
                for blk in range(16):
                    if last and blk > 0:
                        final_norm(blk - 1)
                    for dh in range(2):
                        pi = 4 + dn_rr[0] % 4
                        dn_rr[0] += 1
                        for f4 in range(4):
                            P.I("pe", "matmul", ps[pi][:], lhsT=actG[G % 2][:, f4, blk * 128:(blk + 1) * 128], rhs=wb[:, f4, dh * 512:(dh + 1) * 512],
                                start=(f4 == 0), stop=(f4 == 3), reads=[bw, b_act[G % 2]], writes=[psb[pi]])
                        hs = hb[:, blk, dh * 512:(dh + 1) * 512]
                        P.I("dve", "tensor_tensor", out=hs, in0=ps[pi][:], in1=hs, op=ALU.add, reads=[psb[pi], b_hb[blk]], writes=[b_hb[blk]])
                if last:
                    final_norm(15)

            def final_norm(blk):
                m = blk // 4
                bh = b_hb[blk]
                ss, bss = new_stat(3)
                P.I("act", "activation", out=junk2, in_=hb[:, blk, :], func=AF.Square, accum_out=ss[:, 0:1], reads=[bh], writes=[b_junk2, bss])
                P.I("act", "activation", out=ss[:, 1:2], in_=ss[:, 0:1], func=AF.Sqrt, scale=1.0 / D, bias=EPS, reads=[bss], writes=[bss])
                P.I("dve", "reciprocal", out=ss[:, 2:3], in_=ss[:, 1:2], reads=[bss], writes=[bss])
                P.I("dve", "scalar_tensor_tensor", out=hb[:, blk, :], in0=hb[:, blk, :], scalar=ss[:, 2:3], in1=gfbc, op0=ALU.mult, op1=ALU.mult,
                    reads=[bh, bss, b_gf], writes=[bh])
                if blk % 4 == 3:
                    dst = y[m * 512:(m + 1) * 512, :].rearrange("(j p) d -> p j d", p=128)
                    P.D("sp", osem[m], dst, hb[:, m * 4:(m + 1) * 4, :], reads=[b_hb[m * 4 + j] for j in range(4)])

            alias(b_act[0], [b_g, b_hh[0], b_hh[1], b_junk])
            alias(b_act[1], b_ya)
            for bb in b_wdn:
                alias(bb, [b_wout])
            for bb in [b_gf, b_junk2] + b_sqs:
                alias(bb, b_yp)
            P.D("sp", gsem, gfbc, gfbc_d, writes=[b_gf])
            NG = 8
            load_w(0)
            up(0)
            for G in range(NG):
                if G + 1 < NG:
                    load_w(G + 1)
                    up(G + 1)
                down(G, last=(G == NG - 1))

        if "A" in phases:
            phase_a()
        if "B" in phases:
            barrier()
            prefetch_c()
            phase_b()
        if "C" in phases:
            if "B" not in phases:
                prefetch_c()
            barrier()
            phase_c()

        if debug:
            barrier()
            dsem = P.new_dma_sem()
            tk = None
            for name, src in (("kT", view(0, 32 * KB, BF16)), ("v", view(32 * KB, 32 * KB, BF16)),
                              ("qT", view(64 * KB, 16 * KB, BF16)), ("yp", ypT[:].rearrange("p c t -> p (c t)")),
                              ("ya", view(80 * KB, 16 * KB, BF16)), ("rr", rr[:])):
                tk = P.D("sp", dsem, out=dbg[name], in_=src)
            P.wait("sp", [tk])
        P.wait("sp", [("dma", r[0], r[1]) for r in P.dsems if r[1] > 0])
        P.emit()
    return nc


def _bf16(a):
    return a.astype(ml_dtypes.bfloat16)


def make_consts(parity):
    C = np.zeros((128, C_END), np.float32)
    idx = np.arange(128)
    C[:, C_ID:C_ID + 128] = np.eye(128)
    C[:, C_TRI:C_TRI + 128] = -(idx[:, None] >= idx[None, :]).astype(np.float32)
    for r in (0, 32, 64, 96):
        C[r, C_SEL:C_SEL + 128] = 1.0
    C[:, C_NONE:C_NONE + 64] = -1.0
    t = np.arange(512)
    for i in range(4):
        C[:, C_MASK + i * 512:C_MASK + (i + 1) * 512] = -30000.0 * ((i * 128 + idx[:, None]) >= t[None, :]).astype(np.float32)
    tt = idx[:, None]
    to = idx[None, :]
    for g, w in enumerate(WINS):
        win = ((to - tt >= 0) & (to - tt < w)).astype(np.float32)
        band = win / w - np.eye(128)
        C[:, C_BAND + g * 128:C_BAND + (g + 1) * 128] = band
        if parity == 0:
            cnt = np.minimum(to + 1, w).astype(np.float32)
            C[:, C_BAND0 + g * 128:C_BAND0 + (g + 1) * 128] = win / cnt - np.eye(128)
        else:
            C[:, C_BAND0 + g * 128:C_BAND0 + (g + 1) * 128] = band
        C[:, C_BANDP + g * 128:C_BANDP + (g + 1) * 128] = ((to + 128 - tt) < w).astype(np.float32) / w
    return _bf16(C)


def make_in_maps(x, norm1_g, w_in, pool_w, pool_scale, pool_out_g, attn_out_g, w_out, norm2_g, w_up, w_down, final_g):
    f = lambda a: np.ascontiguousarray(np.asarray(a, dtype=np.float32))
    x = f(x)
    shared = {
        "w_in": f(w_in),
        "pool_w": f(np.transpose(np.asarray(pool_w), (1, 0, 2))),
        "w_out": f(w_out), "w_up": f(w_up), "w_down": f(w_down),
        "g1bc": f(np.broadcast_to(np.asarray(norm1_g)[None, :], (128, D))),
        "g2bc": f(np.broadcast_to(np.asarray(norm2_g)[None, :], (128, D))),
        "gfbc": f(np.broadcast_to(np.asarray(final_g)[None, :], (128, D))),
        "pscale": f(np.asarray(pool_scale).reshape(4, 128).T),
        "gout": f(np.concatenate([np.asarray(pool_out_g), np.asarray(attn_out_g)]).reshape(8, 128).T),
    }
    csts = [make_consts(0), make_consts(1)]
    maps = []
    for c in range(8):
        b, p = c // 2, c % 2
        if p == 1:
            xkc = x[b]
        else:
            xkc = np.concatenate([np.zeros((512, D), np.float32), x[b, :S - 512]], axis=0)
        mp = dict(shared)
        mp["xk"] = np.ascontiguousarray(xkc)
        mp["cst"] = csts[p]
        maps.append(mp)
    return maps


_NC_CACHE = {}


def kernel(x, norm1_g, w_in, pool_w, pool_scale, pool_out_g, attn_out_g, w_out, norm2_g, w_up, w_down, final_g):
    if "nc" not in _NC_CACHE:
        _NC_CACHE["nc"] = build_program()
    nc = _NC_CACHE["nc"]
    maps = make_in_maps(x, norm1_g, w_in, pool_w, pool_scale, pool_out_g, attn_out_g, w_out, norm2_g, w_up, w_down, final_g)
    res = run_bass_kernel_spmd(nc, maps, core_ids=list(range(8)))
    out = np.empty((4, S, D), np.float32)
    for c in range(8):
        b, p = c // 2, c % 2
        yc = np.asarray(res.results[c]["y"]).reshape(4, 512, D)
        for m in range(4):
            t0 = (2 * m + p) * 512
            out[b, t0:t0 + 512] = yc[m]
    return out
```

```python
import contextlib
import numpy as np
import ml_dtypes
import concourse.bass as bass
import concourse.mybir as mybir
from concourse.bass_utils import run_bass_kernel_spmd

F32 = mybir.dt.float32
BF16 = mybir.dt.bfloat16
AF = mybir.ActivationFunctionType
ALU = mybir.AluOpType

D = 1024
S = 4096
DFF = 4096
EPS = 1e-6
WINS = (2, 4, 8, 16)
ENGS = ("pe", "act", "dve", "pool", "sp")

C_ID = 0
C_TRI = 128
C_SEL = 256
C_NONE = 384
C_MASK = 448
C_BAND = C_MASK + 4 * 512
C_BAND0 = C_BAND + 512
C_BANDP = C_BAND0 + 512
C_ZERO = C_BANDP + 512
C_END = C_ZERO + 64

ARENA_F32 = 45056
OPTS = {"a_tiles": 8, "a_proj": True, "a_pool": True, "a_tr": True, "a_q": 1, "a_u": 1, "b_slots": 4, "b_pairs": 4, "b_nk": 0, "b_lvl": 10, "b_dummy": 8, "b_dummy2": 0, "b_order": 2}


class Buf:
    __slots__ = ("w", "r")

    def __init__(self):
        self.w = None
        self.r = {}


class Prog:
    def __init__(self, nc, stack):
        self.nc = nc
        self.stack = stack
        self.q = {e: [] for e in ENGS}
        self.cnt = {e: 0 for e in ENGS}
        self.sem = {e: stack.enter_context(nc.semaphore("prog_" + e)) for e in ENGS}
        self.waited = {e: {} for e in ENGS}
        self.nsem = 0
        self.dsems = []

    def _emit_waits(self, eng, deps):
        for d in deps:
            if d is None:
                continue
            kind, key, val = d
            if kind == "eng" and key == "pe" and eng == "pe":
                continue
            ident = key if kind == "eng" else id(key)
            if self.waited[eng].get(ident, 0) >= val:
                continue
            self.waited[eng][ident] = val
            sem = self.sem[key] if kind == "eng" else key
            self.q[eng].append(("wait", sem, val))

    @staticmethod
    def _deps(reads, writes):
        deps = []
        for b in reads:
            if b.w is not None:
                deps.append(b.w)
        for b in writes:
            if b.w is not None:
                deps.append(b.w)
            deps.extend(b.r.values())
        return deps

    @staticmethod
    def _record(tok, reads, writes):
        kind, key, val = tok
        rk = key if kind == "eng" else ("dma", id(key))
        for b in reads:
            b.r[rk] = tok
        for b in writes:
            b.w = tok
            b.r = {}

    def op(self, eng, fn, reads=(), writes=(), deps=()):
        self._emit_waits(eng, list(deps) + self._deps(reads, writes))
        self.cnt[eng] += 1
        self.q[eng].append(("op", fn, self.sem[eng]))
        tok = ("eng", eng, self.cnt[eng])
        self._record(tok, reads, writes)
        return tok

    def new_dma_sem(self):
        self.nsem += 1
        s = self.stack.enter_context(self.nc.semaphore("dsem%d" % self.nsem))
        rec = [s, 0]
        self.dsems.append(rec)
        return rec

    def dma(self, eng, semrec, fn, reads=(), writes=(), deps=()):
        self._emit_waits(eng, list(deps) + self._deps(reads, writes))
        semrec[1] += 16
        self.q[eng].append(("dma", fn, semrec[0]))
        tok = ("dma", semrec[0], semrec[1])
        self._record(tok, reads, writes)
        return tok

    def wait(self, eng, deps):
        self._emit_waits(eng, deps)

    def I(self, eng, method, *args, reads=(), writes=(), deps=(), **kw):
        return self.op(eng, lambda e: getattr(e, method)(*args, **kw), reads, writes, deps)

    def D(self, eng, semrec, out, in_, reads=(), writes=(), deps=()):
        return self.dma(eng, semrec, lambda e: e.dma_start(out=out, in_=in_), reads, writes, deps)

    def emit(self):
        nc = self.nc
        with nc.Block() as block:
            def run(engine, items):
                for it in items:
                    if it[0] == "wait":
                        engine.wait_ge(it[1], it[2])
                    elif it[0] == "op":
                        it[1](engine).then_inc(it[2], 1)
                    else:
                        it[1](engine).then_inc(it[2], 16)

            @block.tensor
            def _(e):
                run(e, self.q["pe"])

            @block.scalar
            def _(e):
                run(e, self.q["act"])

            @block.vector
            def _(e):
                run(e, self.q["dve"])

            @block.gpsimd
            def _(e):
                run(e, self.q["pool"])

            @block.sync
            def _(e):
                run(e, self.q["sp"])


def build_program(phases="ABC", debug=False):
    nc = bass.Bass("TRN2", target_bir_lowering=False)
    dram_in = lambda n, s, dt=F32: nc.dram_tensor(n, s, dt, kind="ExternalInput").ap()
    xk = dram_in("xk", [S, D])
    w_in = dram_in("w_in", [D, 2048])
    pool_w = dram_in("pool_w", [128, 4, 128])
    w_out = dram_in("w_out", [D, D])
    w_up = dram_in("w_up", [D, DFF])
    w_down = dram_in("w_down", [DFF, D])
    g1bc_d = dram_in("g1bc", [128, D])
    g2bc_d = dram_in("g2bc", [128, D])
    gfbc_d = dram_in("gfbc", [128, D])
    pscale_d = dram_in("pscale", [128, 4])
    gout_d = dram_in("gout", [128, 8])
    cst_d = dram_in("cst", [128, C_END], BF16)
    y = nc.dram_tensor("y", [2048, D], F32, kind="ExternalOutput").ap()
    dbg = {}
    if debug:
        dbg["kT"] = nc.dram_tensor("dbg_kT", [128, 4 * S], BF16, kind="ExternalOutput").ap()
        dbg["v"] = nc.dram_tensor("dbg_v", [128, 32 * 512], BF16, kind="ExternalOutput").ap()
        dbg["qT"] = nc.dram_tensor("dbg_qT", [128, 4 * 2048], BF16, kind="ExternalOutput").ap()
        dbg["yp"] = nc.dram_tensor("dbg_yp", [128, 4 * 2048], BF16, kind="ExternalOutput").ap()
        dbg["ya"] = nc.dram_tensor("dbg_ya", [128, 4 * 2048], BF16, kind="ExternalOutput").ap()
        dbg["rr"] = nc.dram_tensor("dbg_rr", [128, 32], F32, kind="ExternalOutput").ap()

    with contextlib.ExitStack() as st:
        P = Prog(nc, st)
        sb = lambda n, s, dt: st.enter_context(nc.sbuf_tensor("s_" + n, s, dt))

        cst = sb("cst", [128, C_END], BF16)
        pscale = sb("pscale", [128, 4], F32)
        gout = sb("gout", [128, 8], F32)
        pwst = sb("pwst", [128, 4, 128], F32)
        pwb = sb("pwb", [128, 4, 128], BF16)
        ones = sb("ones", [128, 2], F32)
        ypT = sb("ypT", [128, 4, 2048], BF16)
        rr = sb("rr", [128, 32], F32)
        stat = sb("stat", [128, 512], F32)
        AR = sb("arena", [128, ARENA_F32], F32)

        def view(off_bytes, nbytes, dt):
            assert off_bytes % 4 == 0 and nbytes % 4 == 0 and off_bytes + nbytes <= ARENA_F32 * 4
            a = AR[:, off_bytes // 4:(off_bytes + nbytes) // 4]
            return a if dt == F32 else a.bitcast(dt)

        KB = 1024
        PSALL = st.enter_context(nc.psum_tensor("psall", [128, 8 * 512], F32))
        ps = [PSALL[:, i * 512:(i + 1) * 512] for i in range(8)]
        psb = [Buf() for _ in range(8)]

        stat_col = [0]

        def new_stat(n):
            c = stat_col[0]
            stat_col[0] += n
            assert stat_col[0] <= 512
            return stat[:, c:c + n], Buf()

        def barrier():
            for e in ENGS:
                deps = [("eng", o, P.cnt[o]) for o in ENGS if P.cnt[o] > 0]
                P.wait(e, deps)

        csem = P.new_dma_sem()
        b_cst = Buf()
        g1bc = view(171 * KB, 4 * KB, F32)
        for dst, src in ((cst[:], cst_d), (g1bc, g1bc_d), (pscale[:], pscale_d), (gout[:], gout_d),
                         (pwst[:], pool_w)):
            P.D("sp", csem, out=dst, in_=src, writes=[b_cst])
        b_pwb = Buf()
        b_ones = Buf()
        P.I("dve", "tensor_copy", out=pwb[:], in_=pwst[:], reads=[b_cst], writes=[b_pwb])
        P.I("dve", "memset", ones[:], 1.0, writes=[b_ones])

        kT = view(0, 32 * KB, BF16).rearrange("p (c t) -> p c t", c=4)
        vv = view(32 * KB, 32 * KB, BF16).rearrange("p (b e) -> p b e", b=32)
        qT = view(64 * KB, 16 * KB, BF16).rearrange("p (c t) -> p c t", c=4)
        yaT = view(80 * KB, 16 * KB, BF16).rearrange("p (c t) -> p c t", c=4)
        b_kT = [Buf() for _ in range(8)]
        b_v = [Buf() for _ in range(8)]
        b_qT = [Buf() for _ in range(4)]
        b_yp = [Buf() for _ in range(4)]
        b_ya = [Buf() for _ in range(4)]
        b_rr = Buf()
        P.I("dve", "memset", rr[:], 1.0, writes=[b_rr])

        def alias(dst, srcs):
            for sbuf in srcs:
                for k, t in list(sbuf.r.items()) + ([(None, sbuf.w)] if sbuf.w is not None else []):
                    key = (t[1] if t[0] == "eng" else ("dma", id(t[1])))
                    if key not in dst.r or dst.r[key][2] < t[2]:
                        dst.r[key] = t

        def phase_a():
            Y = 80 * KB
            w_inb = view(Y, 32 * KB, BF16).rearrange("p (c e) -> p c e", c=8)
            stg = [view(Y + 32 * KB + i * 4 * KB, 4 * KB, F32) for i in range(2)]
            xbs = [view(Y + 40 * KB + i * 4 * KB, 4 * KB, F32) for i in range(3)]
            xh = [view(Y + 52 * KB + i * 2 * KB, 2 * KB, BF16) for i in range(2)]
            xhT = [view(Y + 56 * KB + i * 8 * KB, 8 * KB, BF16).rearrange("p (c t) -> p c t", c=8)
                   for i in range(2)]
            uu = view(Y + 72 * KB, 5 * KB, BF16).rearrange("p (b e) -> p b e", b=5)
            pooledT = view(Y + 77 * KB, 4 * KB, BF16).rearrange("p (g t) -> p g t", g=4)
            sqp = view(Y + 81 * KB, 8 * KB, F32).rearrange("p (g t) -> p g t", g=4)
            junk = view(Y + 89 * KB, 2 * KB, BF16)
            yp32 = view(Y + 32 * KB, 2 * KB, F32)
            stg = stg + [view(Y + 81 * KB + i * 4 * KB, 4 * KB, F32) for i in range(2)]
            b_stg = [Buf() for _ in range(4)]
            b_win = [Buf() for _ in range(8)]
            b_xb = [Buf() for _ in range(3)]
            b_xh = [Buf() for _ in range(2)]
            b_xhT = [Buf() for _ in range(2)]
            b_u = [Buf() for _ in range(5)]
            b_pooled = Buf()
            b_sqp = Buf()
            b_yp32 = b_stg[0]
            b_junk = Buf()
            xsem = [P.new_dma_sem() for _ in range(3)]
            ssem = [P.new_dma_sem() for _ in range(4)]

            w_in_v = w_in.rearrange("(c p) e -> p c e", p=128)
            WIN_ORDER = (2, 3, 0, 1)

            def load_win(i):
                cb, cp = WIN_ORDER[i // 4], i % 4
                s_ = i % 4
                P.D("sp", ssem[s_], out=stg[s_].rearrange("p (c e) -> p c e", c=2),
                    in_=w_in_v[:, cp * 2:cp * 2 + 2, cb * 512:(cb + 1) * 512], writes=[b_stg[s_]])
                eng = ("dve", "act")[i % 2]
                dst = w_inb[:, cp * 2:cp * 2 + 2, cb * 512:(cb + 1) * 512]
                src = stg[s_].rearrange("p (c e) -> p c e", c=2)
                if eng == "act":
                    P.I("act", "copy", out=dst, in_=src, reads=[b_stg[s_]], writes=[b_win[cb]])
                else:
                    P.I(eng, "tensor_copy", out=dst, in_=src, reads=[b_stg[s_]], writes=[b_win[cb]])

            def load_x(g):
                xb = xbs[g % 3]
                P.D("sp", xsem[g % 3], out=xb, in_=xk[g * 128:(g + 1) * 128, :],
                      writes=[b_xb[g % 3]])

            pacc_rr = [0]

            def pacc():
                i = 2 + pacc_rr[0] % 6
                pacc_rr[0] += 1
                return ps[i], psb[i]

            evac_rr = [0]

            def evac(out_ap, in_ap, reads, writes, scale=None):
                evac_rr[0] ^= 1
                if scale is not None and OPTS["a_q"] == 3:
                    return P.I("act", "mul", out=out_ap, in_=in_ap, mul=scale, reads=reads, writes=writes)
                if scale is not None and OPTS["a_q"] == 4:
                    return P.I("dve", "tensor_scalar", out=out_ap, in0=in_ap, scalar1=scale, scalar2=None,
                               op0=ALU.mult, reads=reads, writes=writes)
                if scale is not None and OPTS["a_q"] == 5:
                    return P.I("act", "activation", out=out_ap, in_=in_ap, func=AF.Copy, scale=scale, reads=reads, writes=writes)
                if evac_rr[0]:
                    if scale is None:
                        return P.I("act", "copy", out=out_ap, in_=in_ap, reads=reads, writes=writes)
                    return P.I("act", "activation", out=out_ap, in_=in_ap, func=AF.Copy, scale=scale,
                                reads=reads, writes=writes)
                if scale is None:
                    return P.I("dve", "tensor_copy", out=out_ap, in_=in_ap, reads=reads, writes=writes)
                return P.I("dve", "tensor_scalar", out=out_ap, in0=in_ap, scalar1=scale, scalar2=None,
                                                             op0=ALU.mult, reads=reads, writes=writes)

            def norm_front(g, kt, j):
                xb, bxb = xbs[g % 3], b_xb[g % 3]
                ss, bss = new_stat(3)
                P.I("act", "activation", out=junk, in_=xb, func=AF.Square, accum_out=ss[:, 0:1],
                     reads=[bxb], writes=[b_junk, bss])
                P.I("act", "activation", out=ss[:, 1:2], in_=ss[:, 0:1], func=AF.Sqrt, scale=1.0 / D, bias=EPS,
                     reads=[bss], writes=[bss])
                P.I("dve", "reciprocal", out=ss[:, 2:3], in_=ss[:, 1:2], reads=[bss], writes=[bss])
                xhb, bxh = xh[g % 2], b_xh[g % 2]
                P.I("dve", "scalar_tensor_tensor", out=xhb, in0=xb, scalar=ss[:, 2:3], in1=g1bc,
                                                             op0=ALU.mult, op1=ALU.mult,
                     reads=[bxb, bss, b_cst], writes=[bxh])

            def norm_back(g, kt, j):
                xhb, bxh = xh[g % 2], b_xh[g % 2]
                pt = ps[g % 2][:].bitcast(BF16).rearrange("p (c t) -> p c t", c=8)
                for c in range(8):
                    P.I("pe", "transpose", out=pt[:, c, :], in_=xhb[:, c * 128:(c + 1) * 128], identity=cst[:, C_ID:C_ID + 128],
                         reads=[bxh, b_cst], writes=[psb[g % 2]])
                evac(xhT[kt % 2][:, :, j * 128:(j + 1) * 128], pt, [psb[g % 2]], [b_xhT[kt % 2]])

            def proj_T(dst_ap, dst_buf, col0, xT, bxT, scale=None):
                pa, pb = pacc()
                for c in range(8):
                    P.I("pe", "matmul", pa[:], lhsT=w_inb[:, c, col0:col0 + 128], rhs=xT[:, c, :],
                                                       start=(c == 0), stop=(c == 7),
                         reads=[b_win[col0 // 512], bxT], writes=[pb])
                evac(dst_ap, pa[:], [pb], [dst_buf], scale=scale)

            def proj_tok(dst_ap, dst_buf, col0, xT, bxT, j):
                pa, pb = pacc()
                for c in range(8):
                    P.I("pe", "matmul", pa[:], lhsT=xT[:, c, j * 128:(j + 1) * 128], rhs=w_inb[:, c, col0:col0 + 512],
                                                       start=(c == 0), stop=(c == 7),
                         reads=[b_win[col0 // 512], bxT], writes=[pb])
                evac(dst_ap, pa[:], [pb], [dst_buf])

            NBLK = 32
            nx = 0
            for i in range(3):
                load_x(nx)
                nx += 1
            for i in range(4):
                load_win(i)

            def tile_items(kt):
                own = kt % 2 == 1
                m = kt // 2
                xT, bxT = xhT[kt % 2], b_xhT[kt % 2]
                items = []
                for ec in range(4):
                    items.append(lambda ec=ec: proj_T(kT[:, ec, kt * 512:(kt + 1) * 512], b_kT[kt], 1024 + ec * 128, xT, bxT))
                for j in range(4):
                    items.append(lambda j=j: proj_tok(vv[:, kt * 4 + j, :], b_v[kt], 1536, xT, bxT, j))
                if not own:
                    items.append(lambda: proj_tok(uu[:, 0, :], b_u[0], 0, xT, bxT, 3))
                    return items
                for ec in range(4):
                    items.append(lambda ec=ec: proj_T(qT[:, ec, m * 512:(m + 1) * 512], b_qT[m], 512 + ec * 128, xT, bxT, scale=0.125))
                first = [lambda j=j: proj_tok(uu[:, 1 + j, :], b_u[1 + j], 0, xT, bxT, j) for j in range(4)]
                stages = pool_stages(m)
                mixed = []
                for i, it in enumerate(items):
                    mixed.append(it)
                    if i < len(stages):
                        mixed.append(stages[i])
                mixed.extend(stages[len(items):])
                return first + mixed

            def pool_stages(m):
                st = []

                def band(g):
                    pa, pb = pacc()
                    for j in range(4):
                        bcol = (C_BAND0 if (m == 0 and j == 0) else C_BAND) + g * 128
                        P.I("pe", "matmul", pa[:, j * 128:(j + 1) * 128], lhsT=uu[:, 1 + j, g * 128:(g + 1) * 128],
                            rhs=cst[:, bcol:bcol + 128], start=True, stop=False, reads=[b_u[1 + j], b_cst], writes=[pb])
                        P.I("pe", "matmul", pa[:, j * 128:(j + 1) * 128], lhsT=uu[:, j, g * 128:(g + 1) * 128],
                            rhs=cst[:, C_BANDP + g * 128:C_BANDP + (g + 1) * 128], start=False, stop=True,
                            reads=[b_u[j], b_cst], writes=[pb])
                    evac(pooledT[:, g, :], pa[:], [pb], [b_pooled])

                def mapped(g):
                    pm, pmb = pacc()
                    P.I("pe", "matmul", pm[:], lhsT=pwb[:, g, :], rhs=pooledT[:, g, :], start=True, stop=True,
                        reads=[b_pwb, b_pooled], writes=[pmb])
                    P.I("dve", "tensor_scalar", out=yp32, in0=pm[:], scalar1=pscale[:, g:g + 1], scalar2=None, op0=ALU.mult,
                        reads=[pmb, b_cst], writes=[b_yp32])
                    P.I("pool", "tensor_copy", out=ypT[:, g, m * 512:(m + 1) * 512], in_=yp32, reads=[b_yp32], writes=[b_yp[m]])
                    P.I("act", "activation", out=sqp[:, g, :], in_=yp32, func=AF.Square, reads=[b_yp32], writes=[b_sqp])

                def stats():
                    ssq, bssq = new_stat(8)
                    pq, pqb = pacc()
                    for j in range(4):
                        for g in range(4):
                            P.I("pe", "matmul", pq[:, j:j + 1], lhsT=sqp[:, g, j * 128:(j + 1) * 128], rhs=ones[:, 0:1],
                                start=(g == 0), stop=(g == 3), reads=[b_sqp, b_ones], writes=[pqb])
                    P.I("act", "activation", out=ssq[:, 0:4], in_=pq[:, 0:4], func=AF.Sqrt, scale=1.0 / 512, bias=EPS,
                        reads=[pqb], writes=[bssq])
                    P.I("dve", "reciprocal", out=rr[:, m * 4:(m + 1) * 4], in_=ssq[:, 0:4], reads=[bssq], writes=[b_rr])

                for g in range(4):
                    st.append(lambda g=g: band(g))
                for g in range(4):
                    st.append(lambda g=g: mapped(g))
                st.append(stats)
                return st

            def next_x():
                if nxs[0] < NBLK:
                    load_x(nxs[0])
                    nxs[0] += 1

            nxs = [nx]
            for j in range(4):
                norm_front(j, 0, j)
                norm_back(j, 0, j)
                next_x()
            for i in range(4, 16):
                load_win(i)
            alias(b_sqp, [b_stg[2], b_stg[3]])
            for kt in range(8):
                own = kt % 2 == 1
                m = kt // 2
                items = tile_items(kt)
                n = len(items)
                for q in range(4):
                    if kt + 1 < 8:
                        norm_front((kt + 1) * 4 + q, kt + 1, q)
                    for it in items[q * n // 4:(q + 1) * n // 4]:
                        it()
                    if kt + 1 < 8:
                        norm_back((kt + 1) * 4 + q, kt + 1, q)
                        next_x()
                if not own:
                    continue

        c_stg = [view(128 * KB + i * 8 * KB, 8 * KB, F32) for i in range(2)]
        c_w_outb = view(144 * KB, 16 * KB, BF16).rearrange("p (c d) -> p c d", c=8)
        c_g2bc = view(160 * KB, 4 * KB, F32)
        c_b_wout = Buf()
        c_b_stg = [Buf() for _ in range(2)]
        c_b_g = Buf()
        c_ssem = [P.new_dma_sem() for _ in range(2)]
        c_stg_rr = [0]

        c_b_hb0 = [Buf() for _ in range(4)]
        c_hsem0 = P.new_dma_sem()

        def prefetch_x0():
            src = xk[512:1024, :].rearrange("(j p) d -> p j d", p=128)
            dst = view(0, 16 * KB, F32).rearrange("p (j d) -> p j d", j=4)
            P.D("sp", c_hsem0, dst, src, writes=c_b_hb0, deps=[("eng", "pe", P.cnt["pe"])])

        def prefetch_c():
            gsem0 = P.new_dma_sem()
            P.D("sp", gsem0, c_g2bc, g2bc_d, writes=[c_b_g])
            w_out_v = w_out.rearrange("(c p) d -> p c d", p=128)
            for gi in range(4):
                s_ = c_stg_rr[0] % 2
                c_stg_rr[0] += 1
                P.D("sp", c_ssem[s_], c_stg[s_].rearrange("p (c d) -> p c d", c=2), w_out_v[:, gi * 2:gi * 2 + 2, :], writes=[c_b_stg[s_]])
                sv = c_stg[s_].rearrange("p (c d) -> p c d", c=2)
                for cc in range(2):
                    c = gi * 2 + cc
                    P.I("dve", "tensor_scalar", out=c_w_outb[:, c, :], in0=sv[:, cc, :], scalar1=gout[:, c:c + 1],
                        scalar2=None, op0=ALU.mult, reads=[c_b_stg[s_], b_cst], writes=[c_b_wout])

        def phase_b():
            Y = 96 * KB
            Eb = [view(Y + i * 4 * KB, 4 * KB, F32) for i in range(2)]
            Lb = [view(Y + 8 * KB + i * 2 * KB, 2 * KB, BF16) for i in range(2)]
            Ab = [view(Y + 12 * KB + i * 2 * KB, 2 * KB, BF16) for i in range(2)]
            Rt = [view(Y + 16 * KB + i * KB, KB, BF16) for i in range(4)]
            sqa = view(Y + 20 * KB, 8 * KB, F32).rearrange("p (c t) -> p c t", c=4)
            o32s = [view(Y + 28 * KB + i * 2 * KB, 2 * KB, F32) for i in range(2)]
            pending = []
            b_E = [Buf() for _ in range(2)]
            b_L = [Buf() for _ in range(2)]
            b_A = [Buf() for _ in range(2)]
            b_Rt = [Buf() for _ in range(4)]
            b_sqa = Buf()
            b_o32s = [Buf(), Buf()]
            for i in range(4):
                P.I("pool", "memset", Rt[i], 0.0, writes=[b_Rt[i]])
            def zero_masked(q):
                P.I("pool", "memset", Lb[q].rearrange("p (h t) -> p h t", h=2)[:, :, 0:384], 0.0, writes=[b_L[q]])
                P.I("pool", "memset", Ab[q].rearrange("p (h t) -> p h t", h=2)[:, :, 0:384], 0.0, writes=[b_A[q]])

            for q in range(2):
                zero_masked(q)
            mhalf, b_mhalf = new_stat(4)
            P.I("pool", "memset", mhalf, -0.5, writes=[b_mhalf])
            for i in (6, 7):
                P.I("dve", "memset", ps[i], 0.0, writes=[psb[i]])
            b_z = [[Buf(), Buf()] for _ in range(2)]
            b_ops = [[Buf(), Buf()] for _ in range(2)]
            b_rps = [[Buf(), Buf()] for _ in range(2)]
            for q in range(2):
                for hp in range(2):
                    b_z[q][hp].w, b_z[q][hp].r = psb[2 * q + hp].w, dict(psb[2 * q + hp].r)
                    b_ops[q][hp].w, b_ops[q][hp].r = psb[4 + q].w, dict(psb[4 + q].r)
                    b_rps[q][hp].w, b_rps[q][hp].r = psb[6 + q].w, dict(psb[6 + q].r)

            preissued = set()
            for m in range(4):
                nk = 8 * (m + 1)
                for dp in range(2):
                    def QK(q, tau, m=m, dp=dp, nk=nk):
                        ec = dp * 2 + q
                        kb = nk - 1 - tau
                        di = kb - (nk - 4)
                        for hp in range(2):
                            z = ps[2 * q + hp]
                            pr = slice(hp * 64, hp * 64 + 64)
                            P.I("pe", "matmul", z, lhsT=kT[pr, ec, kb * 128:(kb + 1) * 128], rhs=qT[pr, ec, m * 512:(m + 1) * 512],
                                start=True, stop=(di < 0), reads=[b_kT[kb // 4], b_qT[m]], writes=[b_z[q][hp]])
                        if di >= 0:
                            for hp in range(2):
                                P.I("pe", "matmul", ps[2 * q + hp], lhsT=cst[:, C_ID:C_ID + 128],
                                    rhs=cst[:, C_MASK + di * 512:C_MASK + (di + 1) * 512],
                                    start=False, stop=True, reads=[b_cst], writes=[b_z[q][hp]])

                    def zpair(q):
                        return PSALL[:, 2 * q * 512:(2 * q + 2) * 512]

                    def cols(ap, tau):
                        c0 = max(0, 3 - tau) * 128
                        if c0 == 0:
                            return ap
                        return ap.rearrange("p (h t) -> p h t", h=2)[:, :, c0:512]

                    def EXP1(q, tau):
                        P.I("act", "activation", out=cols(Eb[q], tau), in_=cols(zpair(q), tau), func=AF.Exp, reads=b_z[q], writes=[b_E[q]])

                    def LN(q, tau):
                        P.I("act", "activation", out=cols(Lb[q], tau), in_=cols(Eb[q], tau), func=AF.Ln, bias=1.0, reads=[b_E[q]], writes=[b_L[q]])

                    def TRISEL(q, tau):
                        ri = q * 2 + tau % 2
                        for hp in range(2):
                            z = ps[2 * q + hp]
                            P.I("pe", "matmul", z, lhsT=cst[:, C_TRI:C_TRI + 128], rhs=Lb[q][:, hp * 512:(hp + 1) * 512],
                                start=False, stop=True, skip_group_check=True, reads=[b_L[q], b_cst], writes=[b_z[q][hp]])
                            if tau > 0 and OPTS["b_lvl"] != 10:
                                r0 = hp * 64
                                P.I("pe", "matmul", z, lhsT=cst[r0:r0 + 33, C_SEL:C_SEL + 128], rhs=Rt[ri][r0:r0 + 33, :],
                                    start=False, stop=True, skip_group_check=True, reads=[b_Rt[ri], b_cst], writes=[b_z[q][hp]])
                        if tau > 0 and OPTS["b_lvl"] == 10:
                            for hp in range(2):
                                r0 = hp * 64
                                P.I("pe", "matmul", ps[2 * q + hp], lhsT=cst[r0:r0 + 33, C_SEL:C_SEL + 128], rhs=Rt[ri][r0:r0 + 33, :],
                                    start=False, stop=True, skip_group_check=True, reads=[b_Rt[ri], b_cst], writes=[b_z[q][hp]])

                    def COL(q, tau):
                        if tau >= nk - 1:
                            return
                        rn = q * 2 + (tau + 1) % 2
                        RPS = ps[6 + q]
                        for hp in range(2):
                            r0 = hp * 64
                            P.I("pe", "matmul", RPS[r0:r0 + 33, :], lhsT=cst[:, C_NONE:C_NONE + 33], rhs=Lb[q][:, hp * 512:(hp + 1) * 512],
                                start=(tau == 0), stop=True, skip_group_check=(tau > 0), reads=[b_L[q], b_cst], writes=[b_rps[q][hp]])
                        P.I("dve", "tensor_copy", out=Rt[rn][0:97, :], in_=RPS[0:97, :], reads=b_rps[q], writes=[b_Rt[rn]])
                        for hp in range(2):
                            r1 = hp * 64 + 32
                            P.I("dve", "tensor_tensor", out=Rt[rn][r1:r1 + 1, :], in0=RPS[r1:r1 + 1, :], in1=Rt[rn][r1:r1 + 1, :],
                                op=ALU.subtract, reads=[b_rps[q][hp], b_Rt[rn]], writes=[b_Rt[rn]])

                    def EXP2(q, tau):
                        P.I("act", "activation", out=cols(Ab[q], tau), in_=cols(zpair(q), tau), func=AF.Exp, reads=b_z[q], writes=[b_A[q]])

                    def AV(q, tau):
                        kb = nk - 1 - tau
                        OPS = ps[4 + q]
                        for hp in range(2):
                            h = (dp * 2 + q) * 2 + hp
                            pr = slice(hp * 64, hp * 64 + 64)
                            P.I("pe", "matmul", OPS[pr, :], lhsT=vv[:, kb, h * 64:(h + 1) * 64], rhs=Ab[q][:, hp * 512:(hp + 1) * 512],
                                start=(tau == 0), stop=(tau == nk - 1), reads=[b_v[kb // 4], b_A[q]], writes=[b_ops[q][hp]])

                    def WARM(q, tau, n):
                        if tau == 0 or tau == nk - 1:
                            return
                        for _ in range(n):
                            P.I("pe", "matmul", ps[4 + q][0:64, :], lhsT=cst[:, C_ZERO:C_ZERO + 64], rhs=cst[:, C_MASK:C_MASK + 512],
                                start=False, stop=False, reads=[b_cst], writes=[b_ops[q][0]])

                    if m == 3 and dp == 1 and "C" in phases:
                        prefetch_x0()
                    if (m, dp) not in preissued:
                        for q in range(2):
                            QK(q, 0)
                    for tau in range(nk):
                        if tau == 2 and pending:
                            for fn in pending:
                                fn()
                            del pending[:]
                        if OPTS["b_order"] == 0:
                            EXP1(0, tau)
                            LN(0, tau)
                            TRISEL(0, tau)
                            EXP1(1, tau)
                            LN(1, tau)
                            TRISEL(1, tau)
                            COL(0, tau)
                            COL(1, tau)
                            WARM(0, tau, OPTS["b_dummy2"])
                            EXP2(0, tau)
                            AV(0, tau)
                            if tau + 1 < nk:
                                QK(0, tau + 1)
                            EXP2(1, tau)
                            AV(1, tau)
                            if tau + 1 < nk:
                                QK(1, tau + 1)
                            WARM(1, tau, OPTS["b_dummy"])
                        elif OPTS["b_order"] == 2:
                            EXP1(0, tau)
                            EXP1(1, tau)
                            LN(0, tau)
                            TRISEL(0, tau)
                            LN(1, tau)
                            TRISEL(1, tau)
                            COL(0, tau)
                            COL(1, tau)
                            WARM(0, tau, OPTS["b_dummy2"])
                            EXP2(0, tau)
                            AV(0, tau)
                            if tau + 1 < nk:
                                QK(0, tau + 1)
                            else:
                                zero_masked(0)
                            EXP2(1, tau)
                            AV(1, tau)
                            if tau + 1 < nk:
                                QK(1, tau + 1)
                            else:
                                zero_masked(1)
                            WARM(1, tau, OPTS["b_dummy"])
                        else:
                            EXP1(0, tau)
                            EXP1(1, tau)
                            WARM(0, tau, OPTS["b_dummy"])
                            LN(0, tau)
                            TRISEL(0, tau)
                            COL(0, tau)
                            LN(1, tau)
                            TRISEL(1, tau)
                            COL(1, tau)
                            WARM(1, tau, OPTS["b_dummy2"])
                            EXP2(0, tau)
                            AV(0, tau)
                            if tau + 1 < nk:
                                QK(0, tau + 1)
                            EXP2(1, tau)
                            AV(1, tau)
                            if tau + 1 < nk:
                                QK(1, tau + 1)
                    for q in range(2):
                        ec = dp * 2 + q
                        P.I("dve", "tensor_copy", out=o32s[q], in_=ps[4 + q], reads=b_ops[q], writes=[b_o32s[q]])
                        P.I("pool", "tensor_copy", out=yaT[:, ec, m * 512:(m + 1) * 512], in_=o32s[q], reads=[b_o32s[q]], writes=[b_ya[m]])
                        pending.append(lambda q=q, ec=ec: P.I("act", "activation", out=sqa[:, ec, :], in_=o32s[q], func=AF.Square,
                                                              reads=[b_o32s[q]], writes=[b_sqa]))
                def slot_stats(m=m):
                    ssq, bssq = new_stat(4)
                    RPS = ps[6]
                    for j in range(4):
                        for ec in range(4):
                            P.I("pe", "matmul", RPS[:, j:j + 1], lhsT=sqa[:, ec, j * 128:(j + 1) * 128], rhs=ones[:, 0:1],
                                start=(ec == 0), stop=(ec == 3), reads=[b_sqa, b_ones], writes=b_rps[0])
                    P.I("dve", "tensor_copy", out=ssq[:, 0:4], in_=RPS[:, 0:4], reads=b_rps[0], writes=[bssq])
                    P.I("pool", "tensor_scalar", out=ssq[:, 0:4], in0=ssq[:, 0:4], scalar1=1.0 / 512, scalar2=EPS, op0=ALU.mult, op1=ALU.add,
                        reads=[bssq], writes=[bssq])
                    P.I("pool", "tensor_tensor", out=rr[:, 16 + m * 4:16 + (m + 1) * 4], in0=ssq[:, 0:4], in1=mhalf[:, 0:4], op=ALU.pow,
                        reads=[bssq, b_mhalf], writes=[b_rr])
                if m + 1 < 4:
                    for q in range(2):
                        QK(q, 0, m=m + 1, dp=0, nk=8 * (m + 2))
                    preissued.add((m + 1, 0))
                for fn in pending:
                    fn()
                del pending[:]
                slot_stats()

        def phase_c():
            hb = view(0, 64 * KB, F32).rearrange("p (j d) -> p j d", j=16)
            wupb = [view(64 * KB + i * 8 * KB, 8 * KB, BF16).rearrange("p (c f) -> p c f", c=8) for i in range(2)]
            hT = view(96 * KB, 32 * KB, BF16).rearrange("p (c t) -> p c t", c=8)
            stg = c_stg
            w_outb = c_w_outb
            wdnb = [view(144 * KB + i * 8 * KB, 8 * KB, BF16).rearrange("p (c d) -> p c d", c=4) for i in range(2)]
            g2bc = c_g2bc
            hh = [view(164 * KB + i * 2 * KB, 2 * KB, BF16) for i in range(2)]
            junk = view(168 * KB, 2 * KB, BF16)
            actG = [view(160 * KB, 16 * KB, BF16).rearrange("p (f t) -> p f t", f=4),
                    view(80 * KB, 16 * KB, BF16).rearrange("p (f t) -> p f t", f=4)]
            ypf = ypT[:].rearrange("p c t -> p (c t)").bitcast(F32)
            gfbc = ypf[:, 0:1024]
            sqs = [ypf[:, 1024 + i * 512:1024 + (i + 1) * 512] for i in range(2)]
            junk2 = ypf[:, 2048:2560].bitcast(BF16)
            b_wout = c_b_wout
            b_stg = c_b_stg
            b_junk = Buf()
            b_junk2 = Buf()
            b_wup = [Buf() for _ in range(2)]
            b_wdn = [Buf() for _ in range(2)]
            b_g = c_b_g
            b_gf = Buf()
            b_hb = [Buf() for _ in range(16)]
            b_hh = [Buf() for _ in range(2)]
            b_hT = [Buf() for _ in range(4)]
            b_act = [Buf() for _ in range(2)]
            b_sqs = [Buf() for _ in range(2)]
            ssem = c_ssem
            hsem = [P.new_dma_sem() for _ in range(4)]
            osem = [P.new_dma_sem() for _ in range(4)]
            gsem = P.new_dma_sem()
            stg_rr = c_stg_rr
            cast_rr = [0]

            early0 = "B" in phases
            if early0:
                for j in range(4):
                    b_hb[j] = c_b_hb0[j]
            for m in range(1 if early0 else 0, 4):
                src = xk[(2 * m + 1) * 512:(2 * m + 2) * 512, :].rearrange("(j p) d -> p j d", p=128)
                P.D("sp", hsem[m], hb[:, m * 4:(m + 1) * 4, :], src, writes=[b_hb[m * 4 + j] for j in range(4)])

            def stage_load(src_ap, pattern, **kw):
                s_ = stg_rr[0] % 2
                stg_rr[0] += 1
                P.D("sp", ssem[s_], stg[s_].rearrange(pattern, **kw), src_ap, writes=[b_stg[s_]])
                return s_

            def outproj(blk):
                m = blk // 4
                bh = b_hb[blk]
                for dh in range(2):
                    k = (blk * 2 + dh) % 2
                    p1, p1b = ps[k * 2], psb[k * 2]
                    p2, p2b = ps[k * 2 + 1], psb[k * 2 + 1]
                    for ec in range(4):
                        P.I("pe", "matmul", p1[:], lhsT=ypT[:, ec, blk * 128:(blk + 1) * 128], rhs=w_outb[:, ec, dh * 512:(dh + 1) * 512],
                            start=(ec == 0), stop=(ec == 3), reads=[b_yp[m], b_wout], writes=[p1b])
                    for ec in range(4):
                        P.I("pe", "matmul", p2[:], lhsT=yaT[:, ec, blk * 128:(blk + 1) * 128], rhs=w_outb[:, 4 + ec, dh * 512:(dh + 1) * 512],
                            start=(ec == 0), stop=(ec == 3), reads=[b_ya[m], b_wout], writes=[p2b])
                    hs = hb[:, blk, dh * 512:(dh + 1) * 512]
                    P.I("dve", "scalar_tensor_tensor", out=hs, in0=p1[:], scalar=rr[:, blk:blk + 1], in1=hs, op0=ALU.mult, op1=ALU.add,
                        reads=[p1b, b_rr, bh], writes=[bh])
                    P.I("dve", "scalar_tensor_tensor", out=hs, in0=p2[:], scalar=rr[:, 16 + blk:17 + blk], in1=hs, op0=ALU.mult, op1=ALU.add,
                        reads=[p2b, b_rr, bh], writes=[bh])

            def norm2(blk):
                m = blk // 4
                bh = b_hb[blk]
                ss, bss = new_stat(3)
                P.I("act", "activation", out=junk, in_=hb[:, blk, :], func=AF.Square, accum_out=ss[:, 0:1], reads=[bh], writes=[b_junk, bss])
                P.I("act", "activation", out=ss[:, 1:2], in_=ss[:, 0:1], func=AF.Sqrt, scale=1.0 / D, bias=EPS, reads=[bss], writes=[bss])
                P.I("dve", "reciprocal", out=ss[:, 2:3], in_=ss[:, 1:2], reads=[bss], writes=[bss])
                hhb, bhh = hh[blk % 2], b_hh[blk % 2]
                P.I("dve", "scalar_tensor_tensor", out=hhb, in0=hb[:, blk, :], scalar=ss[:, 2:3], in1=g2bc, op0=ALU.mult, op1=ALU.mult,
                    reads=[bh, bss, b_g], writes=[bhh])

            def norm2_back(blk):
                m = blk // 4
                hhb, bhh = hh[blk % 2], b_hh[blk % 2]
                pi = 4 + blk % 2
                pt = ps[pi][:].bitcast(BF16).rearrange("p (c t) -> p c t", c=8)
                for c in range(8):
                    P.I("pe", "transpose", out=pt[:, c, :], in_=hhb[:, c * 128:(c + 1) * 128], identity=cst[:, C_ID:C_ID + 128],
                        reads=[bhh, b_cst], writes=[psb[pi]])
                P.I("act", "copy", out=hT[:, :, blk * 128:(blk + 1) * 128], in_=pt, reads=[psb[pi]], writes=[b_hT[m]])

            for b in range(18):
                if b < 16:
                    outproj(b)
                if 0 <= b - 1 < 16:
                    norm2(b - 1)
                if 0 <= b - 2 < 16:
                    norm2_back(b - 2)

            w_up_v = w_up.rearrange("(c p) f -> p c f", p=128)
            w_dn_v = w_down.rearrange("(fc p) d -> p fc d", p=128)

            def cast(dst, src, reads, writes):
                cast_rr[0] += 1
                if cast_rr[0] % 2:
                    P.I("act", "copy", out=dst, in_=src, reads=reads, writes=writes)
                else:
                    P.I("pool", "tensor_copy", out=dst, in_=src, reads=reads, writes=writes)

            def load_w(G):
                for half in range(2):
                    f0 = G * 512 + half * 256
                    s = stage_load(w_up_v[:, :, f0:f0 + 256], "p (c f) -> p c f", c=8)
                    cast(wupb[G % 2][:, :, half * 256:(half + 1) * 256], stg[s].rearrange("p (c f) -> p c f", c=8), [b_stg[s]], [b_wup[G % 2]])
                for half in range(2):
                    fc0 = G * 4 + half * 2
                    s = stage_load(w_dn_v[:, fc0:fc0 + 2, :], "p (c d) -> p c d", c=2)
                    cast(wdnb[G % 2][:, half * 2:(half + 1) * 2, :], stg[s].rearrange("p (c d) -> p c d", c=2), [b_stg[s]], [b_wdn[G % 2]])

            up_rr = [0]

            def up(G):
                wb, bw = wupb[G % 2], b_wup[G % 2]
                for f4 in range(4):
                    for tt in range(4):
                        pi = up_rr[0] % 4
                        up_rr[0] += 1
                        pu, pub = ps[pi], psb[pi]
                        for c in range(8):
                            P.I("pe", "matmul", pu[:], lhsT=wb[:, c, f4 * 128:(f4 + 1) * 128], rhs=hT[:, c, tt * 512:(tt + 1) * 512],
                                start=(c == 0), stop=(c == 7), reads=[bw, b_hT[tt]], writes=[pub])
                        sq, bsq = sqs[pi % 2], b_sqs[pi % 2]
                        P.I("act", "copy", out=sq, in_=pu[:], reads=[pub], writes=[bsq])
                        P.I("dve", "scalar_tensor_tensor", out=actG[G % 2][:, f4, tt * 512:(tt + 1) * 512], in0=pu[:], scalar=0.0, in1=sq,
                            op0=ALU.max, op1=ALU.mult, reads=[pub, bsq], writes=[b_act[G % 2]])

            dn_rr = [0]

            def down(G, last=False):
                wb, bw = wdnb[G % 2], b_wdn[G % 2]
                for blk in range(16):
                    if last and blk > 0:
                        final_norm(blk - 1)
                    for dh in range(2):
                        pi = 4 + dn_rr[0] % 4
                        dn_rr[0] += 1
                        for f4 in range(4):
                            P.I("pe", "matmul", ps[pi][:], lhsT=actG[G % 2][:, f4, blk * 128:(blk + 1) * 128], rhs=wb[:, f4, dh * 512:(dh + 1) * 512],
                                start=(f4 == 0), stop=(f4 == 3), reads=[bw, b_act[G % 2]], writes=[psb[pi]])
                        hs = hb[:, blk, dh * 512:(dh + 1) * 512]
                        P.I("dve", "tensor_tensor", out=hs, in0=ps[pi][:], in1=hs, op=ALU.add, reads=[psb[pi], b_hb[blk]], writes=[b_hb[blk]])
                if last:
                    final_norm(15)

            def final_norm(blk):
                m = blk // 4
                bh = b_hb[blk]
                ss, bss = new_stat(3)
                P.I("act", "activation", out=junk2, in_=hb[:, blk, :], func=AF.Square, accum_out=ss[:, 0:1], reads=[bh], writes=[b_junk2, bss])
                P.I("act", "activation", out=ss[:, 1:2], in_=ss[:, 0:1], func=AF.Sqrt, scale=1.0 / D, bias=EPS, reads=[bss], writes=[bss])
                P.I("dve", "reciprocal", out=ss[:, 2:3], in_=ss[:, 1:2], reads=[bss], writes=[bss])
                P.I("dve", "scalar_tensor_tensor", out=hb[:, blk, :], in0=hb[:, blk, :], scalar=ss[:, 2:3], in1=gfbc, op0=ALU.mult, op1=ALU.mult,
                    reads=[bh, bss, b_gf], writes=[bh])
                if blk % 4 == 3:
                    dst = y[m * 512:(m + 1) * 512, :].rearrange("(j p) d -> p j d", p=128)
                    P.D("sp", osem[m], dst, hb[:, m * 4:(m + 1) * 4, :], reads=[b_hb[m * 4 + j] for j in range(4)])

            alias(b_act[0], [b_g, b_hh[0], b_hh[1], b_junk])
            alias(b_act[1], b_ya)
            for bb in b_wdn:
                alias(bb, [b_wout])
            for bb in [b_gf, b_junk2] + b_sqs:
                alias(bb, b_yp)
            P.D("sp", gsem, gfbc, gfbc_d, writes=[b_gf])
            NG = 8
            load_w(0)
            up(0)
            for G in range(NG):
                if G + 1 < NG:
                    load_w(G + 1)
                    up(G + 1)
                down(G, last=(G == NG - 1))

        if "A" in phases:
            phase_a()
        if "B" in phases:
            barrier()
            prefetch_c()
            phase_b()
        if "C" in phases:
            if "B" not in phases:
                prefetch_c()
            barrier()
            phase_c()

        if debug:
            barrier()
            dsem = P.new_dma_sem()
            tk = None
            for name, src in (("kT", view(0, 32 * KB, BF16)), ("v", view(32 * KB, 32 * KB, BF16)),
                              ("qT", view(64 * KB, 16 * KB, BF16)), ("yp", ypT[:].rearrange("p c t -> p (c t)")),
                              ("ya", view(80 * KB, 16 * KB, BF16)), ("rr", rr[:])):
                tk = P.D("sp", dsem, out=dbg[name], in_=src)
            P.wait("sp", [tk])
        P.wait("sp", [("dma", r[0], r[1]) for r in P.dsems if r[1] > 0])
        P.emit()
    return nc


def _bf16(a):
    return a.astype(ml_dtypes.bfloat16)


def make_consts(parity):
    C = np.zeros((128, C_END), np.float32)
    idx = np.arange(128)
    C[:, C_ID:C_ID + 128] = np.eye(128)
    C[:, C_TRI:C_TRI + 128] = -(idx[:, None] >= idx[None, :]).astype(np.float32)
    for r in (0, 32, 64, 96):
        C[r, C_SEL:C_SEL + 128] = 1.0
    C[:, C_NONE:C_NONE + 64] = -1.0
    t = np.arange(512)
    for i in range(4):
        C[:, C_MASK + i * 512:C_MASK + (i + 1) * 512] = -30000.0 * ((i * 128 + idx[:, None]) >= t[None, :]).astype(np.float32)
    tt = idx[:, None]
    to = idx[None, :]
    for g, w in enumerate(WINS):
        win = ((to - tt >= 0) & (to - tt < w)).astype(np.float32)
        band = win / w - np.eye(128)
        C[:, C_BAND + g * 128:C_BAND + (g + 1) * 128] = band
        if parity == 0:
            cnt = np.minimum(to + 1, w).astype(np.float32)
            C[:, C_BAND0 + g * 128:C_BAND0 + (g + 1) * 128] = win / cnt - np.eye(128)
        else:
            C[:, C_BAND0 + g * 128:C_BAND0 + (g + 1) * 128] = band
        C[:, C_BANDP + g * 128:C_BANDP + (g + 1) * 128] = ((to + 128 - tt) < w).astype(np.float32) / w
    return _bf16(C)


def make_in_maps(x, norm1_g, w_in, pool_w, pool_scale, pool_out_g, attn_out_g, w_out, norm2_g, w_up, w_down, final_g):
    f = lambda a: np.ascontiguousarray(np.asarray(a, dtype=np.float32))
    x = f(x)
    shared = {
        "w_in": f(w_in),
        "pool_w": f(np.transpose(np.asarray(pool_w), (1, 0, 2))),
        "w_out": f(w_out), "w_up": f(w_up), "w_down": f(w_down),
        "g1bc": f(np.broadcast_to(np.asarray(norm1_g)[None, :], (128, D))),
        "g2bc": f(np.broadcast_to(np.asarray(norm2_g)[None, :], (128, D))),
        "gfbc": f(np.broadcast_to(np.asarray(final_g)[None, :], (128, D))),
        "pscale": f(np.asarray(pool_scale).reshape(4, 128).T),
        "gout": f(np.concatenate([np.asarray(pool_out_g), np.asarray(attn_out_g)]).reshape(8, 128).T),
    }
    csts = [make_consts(0), make_consts(1)]
    maps = []
    for c in range(8):
        b, p = c // 2, c % 2
        if p == 1:
            xkc = x[b]
        else:
            xkc = np.concatenate([np.zeros((512, D), np.float32), x[b, :S - 512]], axis=0)
        mp = dict(shared)
        mp["xk"] = np.ascontiguousarray(xkc)
        mp["cst"] = csts[p]
        maps.append(mp)
    return maps


_NC_CACHE = {}


def kernel(x, norm1_g, w_in, pool_w, pool_scale, pool_out_g, attn_out_g, w_out, norm2_g, w_up, w_down, final_g):
    if "nc" not in _NC_CACHE:
        _NC_CACHE["nc"] = build_program()
    nc = _NC_CACHE["nc"]
    maps = make_in_maps(x, norm1_g, w_in, pool_w, pool_scale, pool_out_g, attn_out_g, w_out, norm2_g, w_up, w_down, final_g)
    res = run_bass_kernel_spmd(nc, maps, core_ids=list(range(8)))
    out = np.empty((4, S, D), np.float32)
    for c in range(8):
        b, p = c // 2, c % 2
        yc = np.asarray(res.results[c]["y"]).reshape(4, 512, D)
        for m in range(4):
            t0 = (2 * m + p) * 512
            out[b, t0:t0 + 512] = yc[m]
    return out
```

```python
import contextlib
import numpy as np
import ml_dtypes
import concourse.bass as bass
import concourse.mybir as mybir
from concourse.bass_utils import run_bass_kernel_spmd

F32 = mybir.dt.float32
BF16 = mybir.dt.bfloat16
AF = mybir.ActivationFunctionType
ALU = mybir.AluOpType

D = 1024
S = 4096
DFF = 4096
EPS = 1e-6
WINS = (2, 4, 8, 16)
ENGS = ("pe", "act", "dve", "pool", "sp")

C_ID = 0
C_TRI = 128
C_SEL = 256
C_NONE = 384
C_MASK = 448
C_BAND = C_MASK + 4 * 512
C_BAND0 = C_BAND + 512
C_BANDP = C_BAND0 + 512
C_ZERO = C_BANDP + 512
C_END = C_ZERO + 64

ARENA_F32 = 45056
OPTS = {"a_tiles": 8, "a_proj": True, "a_pool": True, "a_tr": True, "a_q": 1, "a_u": 1, "b_slots": 4, "b_pairs": 4, "b_nk": 0, "b_lvl": 10, "b_dummy": 8, "b_dummy2": 0, "b_order": 2}


class Buf:
    __slots__ = ("w", "r")

    def __init__(self):
        self.w = None
        self.r = {}


class Prog:
    def __init__(self, nc, stack):
        self.nc = nc
        self.stack = stack
        self.q = {e: [] for e in ENGS}
        self.cnt = {e: 0 for e in ENGS}
        self.sem = {e: stack.enter_context(nc.semaphore("prog_" + e)) for e in ENGS}
        self.waited = {e: {} for e in ENGS}
        self.nsem = 0
        self.dsems = []

    def _emit_waits(self, eng, deps):
        for d in deps:
            if d is None:
                continue
            kind, key, val = d
            if kind == "eng" and key == "pe" and eng == "pe":
                continue
            ident = key if kind == "eng" else id(key)
            if self.waited[eng].get(ident, 0) >= val:
                continue
            self.waited[eng][ident] = val
            sem = self.sem[key] if kind == "eng" else key
            self.q[eng].append(("wait", sem, val))

    @staticmethod
    def _deps(reads, writes):
        deps = []
        for b in reads:
            if b.w is not None:
                deps.append(b.w)
        for b in writes:
            if b.w is not None:
                deps.append(b.w)
            deps.extend(b.r.values())
        return deps

    @staticmethod
    def _record(tok, reads, writes):
        kind, key, val = tok
        rk = key if kind == "eng" else ("dma", id(key))
        for b in reads:
            b.r[rk] = tok
        for b in writes:
            b.w = tok
            b.r = {}

    def op(self, eng, fn, reads=(), writes=(), deps=()):
        self._emit_waits(eng, list(deps) + self._deps(reads, writes))
        self.cnt[eng] += 1
        self.q[eng].append(("op", fn, self.sem[eng]))
        tok = ("eng", eng, self.cnt[eng])
        self._record(tok, reads, writes)
        return tok

    def new_dma_sem(self):
        self.nsem += 1
        s = self.stack.enter_context(self.nc.semaphore("dsem%d" % self.nsem))
        rec = [s, 0]
        self.dsems.append(rec)
        return rec

    def dma(self, eng, semrec, fn, reads=(), writes=(), deps=()):
        self._emit_waits(eng, list(deps) + self._deps(reads, writes))
        semrec[1] += 16
        self.q[eng].append(("dma", fn, semrec[0]))
        tok = ("dma", semrec[0], semrec[1])
        self._record(tok, reads, writes)
        return tok

    def wait(self, eng, deps):
        self._emit_waits(eng, deps)

    def I(self, eng, method, *args, reads=(), writes=(), deps=(), **kw):
        return self.op(eng, lambda e: getattr(e, method)(*args, **kw), reads, writes, deps)

    def D(self, eng, semrec, out, in_, reads=(), writes=(), deps=()):
        return self.dma(eng, semrec, lambda e: e.dma_start(out=out, in_=in_), reads, writes, deps)

    def emit(self):
        nc = self.nc
        with nc.Block() as block:
            def run(engine, items):
                for it in items:
                    if it[0] == "wait":
                        engine.wait_ge(it[1], it[2])
                    elif it[0] == "op":
                        it[1](engine).then_inc(it[2], 1)
                    else:
                        it[1](engine).then_inc(it[2], 16)

            @block.tensor
            def _(e):
                run(e, self.q["pe"])

            @block.scalar
            def _(e):
                run(e, self.q["act"])

            @block.vector
            def _(e):
                run(e, self.q["dve"])

            @block.gpsimd
            def _(e):
                run(e, self.q["pool"])

            @block.sync
            def _(e):
                run(e, self.q["sp"])


def build_program(phases="ABC", debug=False):
    nc = bass.Bass("TRN2", target_bir_lowering=False)
    dram_in = lambda n, s, dt=F32: nc.dram_tensor(n, s, dt, kind="ExternalInput").ap()
    xk = dram_in("xk", [S, D])
    w_in = dram_in("w_in", [D, 2048])
    pool_w = dram_in("pool_w", [128, 4, 128])
    w_out = dram_in("w_out", [D, D])
    w_up = dram_in("w_up", [D, DFF])
    w_down = dram_in("w_down", [DFF, D])
    g1bc_d = dram_in("g1bc", [128, D])
    g2bc_d = dram_in("g2bc", [128, D])
    gfbc_d = dram_in("gfbc", [128, D])
    pscale_d = dram_in("pscale", [128, 4])
    gout_d = dram_in("gout", [128, 8])
    cst_d = dram_in("cst", [128, C_END], BF16)
    y = nc.dram_tensor("y", [2048, D], F32, kind="ExternalOutput").ap()
    dbg = {}
    if debug:
        dbg["kT"] = nc.dram_tensor("dbg_kT", [128, 4 * S], BF16, kind="ExternalOutput").ap()
        dbg["v"] = nc.dram_tensor("dbg_v", [128, 32 * 512], BF16, kind="ExternalOutput").ap()
        dbg["qT"] = nc.dram_tensor("dbg_qT", [128, 4 * 2048], BF16, kind="ExternalOutput").ap()
        dbg["yp"] = nc.dram_tensor("dbg_yp", [128, 4 * 2048], BF16, kind="ExternalOutput").ap()
        dbg["ya"] = nc.dram_tensor("dbg_ya", [128, 4 * 2048], BF16, kind="ExternalOutput").ap()
        dbg["rr"] = nc.dram_tensor("dbg_rr", [128, 32], F32, kind="ExternalOutput").ap()

    with contextlib.ExitStack() as st:
        P = Prog(nc, st)
        sb = lambda n, s, dt: st.enter_context(nc.sbuf_tensor("s_" + n, s, dt))

        cst = sb("cst", [128, C_END], BF16)
        pscale = sb("pscale", [128, 4], F32)
        gout = sb("gout", [128, 8], F32)
        pwst = sb("pwst", [128, 4, 128], F32)
        pwb = sb("pwb", [128, 4, 128], BF16)
        ones = sb("ones", [128, 2], F32)
        ypT = sb("ypT", [128, 4, 2048], BF16)
        rr = sb("rr", [128, 32], F32)
        stat = sb("stat", [128, 512], F32)
        AR = sb("arena", [128, ARENA_F32], F32)

        def view(off_bytes, nbytes, dt):
            assert off_bytes % 4 == 0 and nbytes % 4 == 0 and off_bytes + nbytes <= ARENA_F32 * 4
            a = AR[:, off_bytes // 4:(off_bytes + nbytes) // 4]
            return a if dt == F32 else a.bitcast(dt)

        KB = 1024
        PSALL = st.enter_context(nc.psum_tensor("psall", [128, 8 * 512], F32))
        ps = [PSALL[:, i * 512:(i + 1) * 512] for i in range(8)]
        psb = [Buf() for _ in range(8)]

        stat_col = [0]

        def new_stat(n):
            c = stat_col[0]
            stat_col[0] += n
            assert stat_col[0] <= 512
            return stat[:, c:c + n], Buf()

        def barrier():
            for e in ENGS:
                deps = [("eng", o, P.cnt[o]) for o in ENGS if P.cnt[o] > 0]
                P.wait(e, deps)

        csem = P.new_dma_sem()
        b_cst = Buf()
        g1bc = view(171 * KB, 4 * KB, F32)
        for dst, src in ((cst[:], cst_d), (g1bc, g1bc_d), (pscale[:], pscale_d), (gout[:], gout_d),
                         (pwst[:], pool_w)):
            P.D("sp", csem, out=dst, in_=src, writes=[b_cst])
        b_pwb = Buf()
        b_ones = Buf()
        P.I("dve", "tensor_copy", out=pwb[:], in_=pwst[:], reads=[b_cst], writes=[b_pwb])
        P.I("dve", "memset", ones[:], 1.0, writes=[b_ones])

        kT = view(0, 32 * KB, BF16).rearrange("p (c t) -> p c t", c=4)
        vv = view(32 * KB, 32 * KB, BF16).rearrange("p (b e) -> p b e", b=32)
        qT = view(64 * KB, 16 * KB, BF16).rearrange("p (c t) -> p c t", c=4)
        yaT = view(80 * KB, 16 * KB, BF16).rearrange("p (c t) -> p c t", c=4)
        b_kT = [Buf() for _ in range(8)]
        b_v = [Buf() for _ in range(8)]
        b_qT = [Buf() for _ in range(4)]
        b_yp = [Buf() for _ in range(4)]
        b_ya = [Buf() for _ in range(4)]
        b_rr = Buf()
        P.I("dve", "memset", rr[:], 1.0, writes=[b_rr])

        def alias(dst, srcs):
            for sbuf in srcs:
                for k, t in list(sbuf.r.items()) + ([(None, sbuf.w)] if sbuf.w is not None else []):
                    key = (t[1] if t[0] == "eng" else ("dma", id(t[1])))
                    if key not in dst.r or dst.r[key][2] < t[2]:
                        dst.r[key] = t

        def phase_a():
            Y = 80 * KB
            w_inb = view(Y, 32 * KB, BF16).rearrange("p (c e) -> p c e", c=8)
            stg = [view(Y + 32 * KB + i * 4 * KB, 4 * KB, F32) for i in range(2)]
            xbs = [view(Y + 40 * KB + i * 4 * KB, 4 * KB, F32) for i in range(3)]
            xh = [view(Y + 52 * KB + i * 2 * KB, 2 * KB, BF16) for i in range(2)]
            xhT = [view(Y + 56 * KB + i * 8 * KB, 8 * KB, BF16).rearrange("p (c t) -> p c t", c=8)
                   for i in range(2)]
            uu = view(Y + 72 * KB, 5 * KB, BF16).rearrange("p (b e) -> p b e", b=5)
            pooledT = view(Y + 77 * KB, 4 * KB, BF16).rearrange("p (g t) -> p g t", g=4)
            sqp = view(Y + 81 * KB, 8 * KB, F32).rearrange("p (g t) -> p g t", g=4)
            junk = view(Y + 89 * KB, 2 * KB, BF16)
            yp32 = view(Y + 32 * KB, 2 * KB, F32)
            stg = stg + [view(Y + 81 * KB + i * 4 * KB, 4 * KB, F32) for i in range(2)]
            b_stg = [Buf() for _ in range(4)]
            b_win = [Buf() for _ in range(8)]
            b_xb = [Buf() for _ in range(3)]
            b_xh = [Buf() for _ in range(2)]
            b_xhT = [Buf() for _ in range(2)]
            b_u = [Buf() for _ in range(5)]
            b_pooled = Buf()
            b_sqp = Buf()
            b_yp32 = b_stg[0]
            b_junk = Buf()
            xsem = [P.new_dma_sem() for _ in range(3)]
            ssem = [P.new_dma_sem() for _ in range(4)]

            w_in_v = w_in.rearrange("(c p) e -> p c e", p=128)
            WIN_ORDER = (2, 3, 0, 1)

            def load_win(i):
                cb, cp = WIN_ORDER[i // 4], i % 4
                s_ = i % 4
                P.D("sp", ssem[s_], out=stg[s_].rearrange("p (c e) -> p c e", c=2),
                    in_=w_in_v[:, cp * 2:cp * 2 + 2, cb * 512:(cb + 1) * 512], writes=[b_stg[s_]])
                eng = ("dve", "act")[i % 2]
                dst = w_inb[:, cp * 2:cp * 2 + 2, cb * 512:(cb + 1) * 512]
                src = stg[s_].rearrange("p (c e) -> p c e", c=2)
                if eng == "act":
                    P.I("act", "copy", out=dst, in_=src, reads=[b_stg[s_]], writes=[b_win[cb]])
                else:
                    P.I(eng, "tensor_copy", out=dst, in_=src, reads=[b_stg[s_]], writes=[b_win[cb]])

            def load_x(g):
                xb = xbs[g % 3]
                P.D("sp", xsem[g % 3], out=xb, in_=xk[g * 128:(g + 1) * 128, :],
                      writes=[b_xb[g % 3]])

            pacc_rr = [0]

            def pacc():
                i = 2 + pacc_rr[0] % 6
                pacc_rr[0] += 1
                return ps[i], psb[i]

            evac_rr = [0]

            def evac(out_ap, in_ap, reads, writes, scale=None):
                evac_rr[0] ^= 1
                if scale is not None and OPTS["a_q"] == 3:
                    return P.I("act", "mul", out=out_ap, in_=in_ap, mul=scale, reads=reads, writes=writes)
                if scale is not None and OPTS["a_q"] == 4:
                    return P.I("dve", "tensor_scalar", out=out_ap, in0=in_ap, scalar1=scale, scalar2=None,
                               op0=ALU.mult, reads=reads, writes=writes)
                if scale is not None and OPTS["a_q"] == 5:
                    return P.I("act", "activation", out=out_ap, in_=in_ap, func=AF.Copy, scale=scale, reads=reads, writes=writes)
                if evac_rr[0]:
                    if scale is None:
                        return P.I("act", "copy", out=out_ap, in_=in_ap, reads=reads, writes=writes)
                    return P.I("act", "activation", out=out_ap, in_=in_ap, func=AF.Copy, scale=scale,
                                reads=reads, writes=writes)
                if scale is None:
                    return P.I("dve", "tensor_copy", out=out_ap, in_=in_ap, reads=reads, writes=writes)
                return P.I("dve", "tensor_scalar", out=out_ap, in0=in_ap, scalar1=scale, scalar2=None,
                                                             op0=ALU.mult, reads=reads, writes=writes)

            def norm_front(g, kt, j):
                xb, bxb = xbs[g % 3], b_xb[g % 3]
                ss, bss = new_stat(3)
                P.I("act", "activation", out=junk, in_=xb, func=AF.Square, accum_out=ss[:, 0:1],
                     reads=[bxb], writes=[b_junk, bss])
                P.I("act", "activation", out=ss[:, 1:2], in_=ss[:, 0:1], func=AF.Sqrt, scale=1.0 / D, bias=EPS,
                     reads=[bss], writes=[bss])
                P.I("dve", "reciprocal", out=ss[:, 2:3], in_=ss[:, 1:2], reads=[bss], writes=[bss])
                xhb, bxh = xh[g % 2], b_xh[g % 2]
                P.I("dve", "scalar_tensor_tensor", out=xhb, in0=xb, scalar=ss[:, 2:3], in1=g1bc,
                                                             op0=ALU.mult, op1=ALU.mult,
                     reads=[bxb, bss, b_cst], writes=[bxh])

            def norm_back(g, kt, j):
                xhb, bxh = xh[g % 2], b_xh[g % 2]
                pt = ps[g % 2][:].bitcast(BF16).rearrange("p (c t) -> p c t", c=8)
                for c in range(8):
                    P.I("pe", "transpose", out=pt[:, c, :], in_=xhb[:, c * 128:(c + 1) * 128], identity=cst[:, C_ID:C_ID + 128],
                         reads=[bxh, b_cst], writes=[psb[g % 2]])
                evac(xhT[kt % 2][:, :, j * 128:(j + 1) * 128], pt, [psb[g % 2]], [b_xhT[kt % 2]])

            def proj_T(dst_ap, dst_buf, col0, xT, bxT, scale=None):
                pa, pb = pacc()
                for c in range(8):
                    P.I("pe", "matmul", pa[:], lhsT=w_inb[:, c, col0:col0 + 128], rhs=xT[:, c, :],
                                                       start=(c == 0), stop=(c == 7),
                         reads=[b_win[col0 // 512], bxT], writes=[pb])
                evac(dst_ap, pa[:], [pb], [dst_buf], scale=scale)

            def proj_tok(dst_ap, dst_buf, col0, xT, bxT, j):
                pa, pb = pacc()
                for c in range(8):
                    P.I("pe", "matmul", pa[:], lhsT=xT[:, c, j * 128:(j + 1) * 128], rhs=w_inb[:, c, col0:col0 + 512],
                                                       start=(c == 0), stop=(c == 7),
                         reads=[b_win[col0 // 512], bxT], writes=[pb])
                evac(dst_ap, pa[:], [pb], [dst_buf])

            NBLK = 32
            nx = 0
            for i in range(3):
                load_x(nx)
                nx += 1
            for i in range(4):
                load_win(i)

            def tile_items(kt):
                own = kt % 2 == 1
                m = kt // 2
                xT, bxT = xhT[kt % 2], b_xhT[kt % 2]
                items = []
                for ec in range(4):
                    items.append(lambda ec=ec: proj_T(kT[:, ec, kt * 512:(kt + 1) * 512], b_kT[kt], 1024 + ec * 128, xT, bxT))
                for j in range(4):
                    items.append(lambda j=j: proj_tok(vv[:, kt * 4 + j, :], b_v[kt], 1536, xT, bxT, j))
                if not own:
                    items.append(lambda: proj_tok(uu[:, 0, :], b_u[0], 0, xT, bxT, 3))
                    return items
                for ec in range(4):
                    items.append(lambda ec=ec: proj_T(qT[:, ec, m * 512:(m + 1) * 512], b_qT[m], 512 + ec * 128, xT, bxT, scale=0.125))
                first = [lambda j=j: proj_tok(uu[:, 1 + j, :], b_u[1 + j], 0, xT, bxT, j) for j in range(4)]
                stages = pool_stages(m)
                mixed = []
                for i, it in enumerate(items):
                    mixed.append(it)
                    if i < len(stages):
                        mixed.append(stages[i])
                mixed.extend(stages[len(items):])
                return first + mixed

            def pool_stages(m):
                st = []

                def band(g):
                    pa, pb = pacc()
                    for j in range(4):
                        bcol = (C_BAND0 if (m == 0 and j == 0) else C_BAND) + g * 128
                        P.I("pe", "matmul", pa[:, j * 128:(j + 1) * 128], lhsT=uu[:, 1 + j, g * 128:(g + 1) * 128],
                            rhs=cst[:, bcol:bcol + 128], start=True, stop=False, reads=[b_u[1 + j], b_cst], writes=[pb])
                        P.I("pe", "matmul", pa[:, j * 128:(j + 1) * 128], lhsT=uu[:, j, g * 128:(g + 1) * 128],
                            rhs=cst[:, C_BANDP + g * 128:C_BANDP + (g + 1) * 128], start=False, stop=True,
                            reads=[b_u[j], b_cst], writes=[pb])
                    evac(pooledT[:, g, :], pa[:], [pb], [b_pooled])

                def mapped(g):
                    pm, pmb = pacc()
                    P.I("pe", "matmul", pm[:], lhsT=pwb[:, g, :], rhs=pooledT[:, g, :], start=True, stop=True,
                        reads=[b_pwb, b_pooled], writes=[pmb])
                    P.I("dve", "tensor_scalar", out=yp32, in0=pm[:], scalar1=pscale[:, g:g + 1], scalar2=None, op0=ALU.mult,
                        reads=[pmb, b_cst], writes=[b_yp32])
                    P.I("pool", "tensor_copy", out=ypT[:, g, m * 512:(m + 1) * 512], in_=yp32, reads=[b_yp32], writes=[b_yp[m]])
                    P.I("act", "activation", out=sqp[:, g, :], in_=yp32, func=AF.Square, reads=[b_yp32], writes=[b_sqp])

                def stats():
                    ssq, bssq = new_stat(8)
                    pq, pqb = pacc()
                    for j in range(4):
                        for g in range(4):
                            P.I("pe", "matmul", pq[:, j:j + 1], lhsT=sqp[:, g, j * 128:(j + 1) * 128], rhs=ones[:, 0:1],
                                start=(g == 0), stop=(g == 3), reads=[b_sqp, b_ones], writes=[pqb])
                    P.I("act", "activation", out=ssq[:, 0:4], in_=pq[:, 0:4], func=AF.Sqrt, scale=1.0 / 512, bias=EPS,
                        reads=[pqb], writes=[bssq])
                    P.I("dve", "reciprocal", out=rr[:, m * 4:(m + 1) * 4], in_=ssq[:, 0:4], reads=[bssq], writes=[b_rr])

                for g in range(4):
                    st.append(lambda g=g: band(g))
                for g in range(4):
                    st.append(lambda g=g: mapped(g))
                st.append(stats)
                return st

            def next_x():
                if nxs[0] < NBLK:
                    load_x(nxs[0])
                    nxs[0] += 1

            nxs = [nx]
            for j in range(4):
                norm_front(j, 0, j)
                norm_back(j, 0, j)
                next_x()
            for i in range(4, 16):
                load_win(i)
            alias(b_sqp, [b_stg[2], b_stg[3]])
            for kt in range(8):
                own = kt % 2 == 1
                m = kt // 2
                items = tile_items(kt)
                n = len(items)
                for q in range(4):
                    if kt + 1 < 8:
                        norm_front((kt + 1) * 4 + q, kt + 1, q)
                    for it in items[q * n // 4:(q + 1) * n // 4]:
                        it()
                    if kt + 1 < 8:
                        norm_back((kt + 1) * 4 + q, kt + 1, q)
                        next_x()
                if not own:
                    continue

        c_stg = [view(128 * KB + i * 8 * KB, 8 * KB, F32) for i in range(2)]
        c_w_outb = view(144 * KB, 16 * KB, BF16).rearrange("p (c d) -> p c d", c=8)
        c_g2bc = view(160 * KB, 4 * KB, F32)
        c_b_wout = Buf()
        c_b_stg = [Buf() for _ in range(2)]
        c_b_g = Buf()
        c_ssem = [P.new_dma_sem() for _ in range(2)]
        c_stg_rr = [0]

        c_b_hb0 = [Buf() for _ in range(4)]
        c_hsem0 = P.new_dma_sem()

        def prefetch_x0():
            src = xk[512:1024, :].rearrange("(j p) d -> p j d", p=128)
            dst = view(0, 16 * KB, F32).rearrange("p (j d) -> p j d", j=4)
            P.D("sp", c_hsem0, dst, src, writes=c_b_hb0, deps=[("eng", "pe", P.cnt["pe"])])

        def prefetch_c():
            gsem0 = P.new_dma_sem()
            P.D("sp", gsem0, c_g2bc, g2bc_d, writes=[c_b_g])
            w_out_v = w_out.rearrange("(c p) d -> p c d", p=128)
            for gi in range(4):
                s_ = c_stg_rr[0] % 2
                c_stg_rr[0] += 1
                P.D("sp", c_ssem[s_], c_stg[s_].rearrange("p (c d) -> p c d", c=2), w_out_v[:, gi * 2:gi * 2 + 2, :], writes=[c_b_stg[s_]])
                sv = c_stg[s_].rearrange("p (c d) -> p c d", c=2)
                for cc in range(2):
                    c = gi * 2 + cc
                    P.I("dve", "tensor_scalar", out=c_w_outb[:, c, :], in0=sv[:, cc, :], scalar1=gout[:, c:c + 1],
                        scalar2=None, op0=ALU.mult, reads=[c_b_stg[s_], b_cst], writes=[c_b_wout])

        def phase_b():
            Y = 96 * KB
            Eb = [view(Y + i * 4 * KB, 4 * KB, F32) for i in range(2)]
            Lb = [view(Y + 8 * KB + i * 2 * KB, 2 * KB, BF16) for i in range(2)]
            Ab = [view(Y + 12 * KB + i * 2 * KB, 2 * KB, BF16) for i in range(2)]
            Rt = [view(Y + 16 * KB + i * KB, KB, BF16) for i in range(4)]
            sqa = view(Y + 20 * KB, 8 * KB, F32).rearrange("p (c t) -> p c t", c=4)
            o32s = [view(Y + 28 * KB + i * 2 * KB, 2 * KB, F32) for i in range(2)]
            pending = []
            b_E = [Buf() for _ in range(2)]
            b_L = [Buf() for _ in range(2)]
            b_A = [Buf() for _ in range(2)]
            b_Rt = [Buf() for _ in range(4)]
            b_sqa = Buf()
            b_o32s = [Buf(), Buf()]
            for i in range(4):
                P.I("pool", "memset", Rt[i], 0.0, writes=[b_Rt[i]])
            def zero_masked(q):
                P.I("pool", "memset", Lb[q].rearrange("p (h t) -> p h t", h=2)[:, :, 0:384], 0.0, writes=[b_L[q]])
                P.I("pool", "memset", Ab[q].rearrange("p (h t) -> p h t", h=2)[:, :, 0:384], 0.0, writes=[b_A[q]])

            for q in range(2):
                zero_masked(q)
            mhalf, b_mhalf = new_stat(4)
            P.I("pool", "memset", mhalf, -0.5, writes=[b_mhalf])
            for i in (6, 7):
                P.I("dve", "memset", ps[i], 0.0, writes=[psb[i]])
            b_z = [[Buf(), Buf()] for _ in range(2)]
            b_ops = [[Buf(), Buf()] for _ in range(2)]
            b_rps = [[Buf(), Buf()] for _ in range(2)]
            for q in range(2):
                for hp in range(2):
                    b_z[q][hp].w, b_z[q][hp].r = psb[2 * q + hp].w, dict(psb[2 * q + hp].r)
                    b_ops[q][hp].w, b_ops[q][hp].r = psb[4 + q].w, dict(psb[4 + q].r)
                    b_rps[q][hp].w, b_rps[q][hp].r = psb[6 + q].w, dict(psb[6 + q].r)

            preissued = set()
            for m in range(4):
                nk = 8 * (m + 1)
                for dp in range(2):
                    def QK(q, tau, m=m, dp=dp, nk=nk):
                        ec = dp * 2 + q
                        kb = nk - 1 - tau
                        di = kb - (nk - 4)
                        for hp in range(2):
                            z = ps[2 * q + hp]
                            pr = slice(hp * 64, hp * 64 + 64)
                            P.I("pe", "matmul", z, lhsT=kT[pr, ec, kb * 128:(kb + 1) * 128], rhs=qT[pr, ec, m * 512:(m + 1) * 512],
                                start=True, stop=(di < 0), reads=[b_kT[kb // 4], b_qT[m]], writes=[b_z[q][hp]])
                        if di >= 0:
                            for hp in range(2):
                                P.I("pe", "matmul", ps[2 * q + hp], lhsT=cst[:, C_ID:C_ID + 128],
                                    rhs=cst[:, C_MASK + di * 512:C_MASK + (di + 1) * 512],
                                    start=False, stop=True, reads=[b_cst], writes=[b_z[q][hp]])

                    def zpair(q):
                        return PSALL[:, 2 * q * 512:(2 * q + 2) * 512]

                    def cols(ap, tau):
                        c0 = max(0, 3 - tau) * 128
                        if c0 == 0:
                            return ap
                        return ap.rearrange("p (h t) -> p h t", h=2)[:, :, c0:512]

                    def EXP1(q, tau):
                        P.I("act", "activation", out=cols(Eb[q], tau), in_=cols(zpair(q), tau), func=AF.Exp, reads=b_z[q], writes=[b_E[q]])

                    def LN(q, tau):
                        P.I("act", "activation", out=cols(Lb[q], tau), in_=cols(Eb[q], tau), func=AF.Ln, bias=1.0, reads=[b_E[q]], writes=[b_L[q]])

                    def TRISEL(q, tau):
                        ri = q * 2 + tau % 2
                        for hp in range(2):
                            z = ps[2 * q + hp]
                            P.I("pe", "matmul", z, lhsT=cst[:, C_TRI:C_TRI + 128], rhs=Lb[q][:, hp * 512:(hp + 1) * 512],
                                start=False, stop=True, skip_group_check=True, reads=[b_L[q], b_cst], writes=[b_z[q][hp]])
                            if tau > 0 and OPTS["b_lvl"] != 10:
                                r0 = hp * 64
                                P.I("pe", "matmul", z, lhsT=cst[r0:r0 + 33, C_SEL:C_SEL + 128], rhs=Rt[ri][r0:r0 + 33, :],
                                    start=False, stop=True, skip_group_check=True, reads=[b_Rt[ri], b_cst], writes=[b_z[q][hp]])
                        if tau > 0 and OPTS["b_lvl"] == 10:
                            for hp in range(2):
                                r0 = hp * 64
                                P.I("pe", "matmul", ps[2 * q + hp], lhsT=cst[r0:r0 + 33, C_SEL:C_SEL + 128], rhs=Rt[ri][r0:r0 + 33, :],
                                    start=False, stop=True, skip_group_check=True, reads=[b_Rt[ri], b_cst], writes=[b_z[q][hp]])

                    def COL(q, tau):
                        if tau >= nk - 1:
                            return
                        rn = q * 2 + (tau + 1) % 2
                        RPS = ps[6 + q]
                        for hp in range(2):
                            r0 = hp * 64
                            P.I("pe", "matmul", RPS[r0:r0 + 33, :], lhsT=cst[:, C_NONE:C_NONE + 33], rhs=Lb[q][:, hp * 512:(hp + 1) * 512],
                                start=(tau == 0), stop=True, skip_group_check=(tau > 0), reads=[b_L[q], b_cst], writes=[b_rps[q][hp]])
                        P.I("dve", "tensor_copy", out=Rt[rn][0:97, :], in_=RPS[0:97, :], reads=b_rps[q], writes=[b_Rt[rn]])
                        for hp in range(2):
                            r1 = hp * 64 + 32
                            P.I("dve", "tensor_tensor", out=Rt[rn][r1:r1 + 1, :], in0=RPS[r1:r1 + 1, :], in1=Rt[rn][r1:r1 + 1, :],
                                op=ALU.subtract, reads=[b_rps[q][hp], b_Rt[rn]], writes=[b_Rt[rn]])

                    def EXP2(q, tau):
                        P.I("act", "activation", out=cols(Ab[q], tau), in_=cols(zpair(q), tau), func=AF.Exp, reads=b_z[q], writes=[b_A[q]])

                    def AV(q, tau):
                        kb = nk - 1 - tau
                        OPS = ps[4 + q]
                        for hp in range(2):
                            h = (dp * 2 + q) * 2 + hp
                            pr = slice(hp * 64, hp * 64 + 64)
                            P.I("pe", "matmul", OPS[pr, :], lhsT=vv[:, kb, h * 64:(h + 1) * 64], rhs=Ab[q][:, hp * 512:(hp + 1) * 512],
                                start=(tau == 0), stop=(tau == nk - 1), reads=[b_v[kb // 4], b_A[q]], writes=[b_ops[q][hp]])

                    def WARM(q, tau, n):
                        if tau == 0 or tau == nk - 1:
                            return
                        for _ in range(n):
                            P.I("pe", "matmul", ps[4 + q][0:64, :], lhsT=cst[:, C_ZERO:C_ZERO + 64], rhs=cst[:, C_MASK:C_MASK + 512],
                                start=False, stop=False, reads=[b_cst], writes=[b_ops[q][0]])

                    if m == 3 and dp == 1 and "C" in phases:
                        prefetch_x0()
                    if (m, dp) not in preissued:
                        for q in range(2):
                            QK(q, 0)
                    for tau in range(nk):
                        if tau == 2 and pending:
                            for fn in pending:
                                fn()
                            del pending[:]
                        if OPTS["b_order"] == 0:
                            EXP1(0, tau)
                            LN(0, tau)
                            TRISEL(0, tau)
                            EXP1(1, tau)
                            LN(1, tau)
                            TRISEL(1, tau)
                            COL(0, tau)
                            COL(1, tau)
                            WARM(0, tau, OPTS["b_dummy2"])
                            EXP2(0, tau)
                            AV(0, tau)
                            if tau + 1 < nk:
                                QK(0, tau + 1)
                            EXP2(1, tau)
                            AV(1, tau)
                            if tau + 1 < nk:
                                QK(1, tau + 1)
                            WARM(1, tau, OPTS["b_dummy"])
                        elif OPTS["b_order"] == 2:
                            EXP1(0, tau)
                            EXP1(1, tau)
                            LN(0, tau)
                            TRISEL(0, tau)
                            LN(1, tau)
                            TRISEL(1, tau)
                            COL(0, tau)
                            COL(1, tau)
                            WARM(0, tau, OPTS["b_dummy2"])
                            EXP2(0, tau)
                            AV(0, tau)
                            if tau + 1 < nk:
                                QK(0, tau + 1)
                            else:
                                zero_masked(0)
                            EXP2(1, tau)
                            AV(1, tau)
                            if tau + 1 < nk:
                                QK(1, tau + 1)
                            else:
                                zero_masked(1)
                            WARM(1, tau, OPTS["b_dummy"])
                        else:
                            EXP1(0, tau)
                            EXP1(1, tau)
                            WARM(0, tau, OPTS["b_dummy"])
                            LN(0, tau)
                            TRISEL(0, tau)
                            COL(0, tau)
                            LN(1, tau)
                            TRISEL(1, tau)
                            COL(1, tau)
                            WARM(1, tau, OPTS["b_dummy2"])
                            EXP2(0, tau)
                            AV(0, tau)
                            if tau + 1 < nk:
                                QK(0, tau + 1)
                            EXP2(1, tau)
                            AV(1, tau)
                            if tau + 1 < nk:
                                QK(1, tau + 1)
                    for q in range(2):
                        ec = dp * 2 + q
                        P.I("dve", "tensor_copy", out=o32s[q], in_=ps[4 + q], reads=b_ops[q], writes=[b_o32s[q]])
                        P.I("pool", "tensor_copy", out=yaT[:, ec, m * 512:(m + 1) * 512], in_=o32s[q], reads=[b_o32s[q]], writes=[b_ya[m]])
                        pending.append(lambda q=q, ec=ec: P.I("act", "activation", out=sqa[:, ec, :], in_=o32s[q], func=AF.Square,
                                                              reads=[b_o32s[q]], writes=[b_sqa]))
                def slot_stats(m=m):
                    ssq, bssq = new_stat(4)
                    RPS = ps[6]
                    for j in range(4):
                        for ec in range(4):
                            P.I("pe", "matmul", RPS[:, j:j + 1], lhsT=sqa[:, ec, j * 128:(j + 1) * 128], rhs=ones[:, 0:1],
                                start=(ec == 0), stop=(ec == 3), reads=[b_sqa, b_ones], writes=b_rps[0])
                    P.I("dve", "tensor_copy", out=ssq[:, 0:4], in_=RPS[:, 0:4], reads=b_rps[0], writes=[bssq])
                    P.I("pool", "tensor_scalar", out=ssq[:, 0:4], in0=ssq[:, 0:4], scalar1=1.0 / 512, scalar2=EPS, op0=ALU.mult, op1=ALU.add,
                        reads=[bssq], writes=[bssq])
                    P.I("pool", "tensor_tensor", out=rr[:, 16 + m * 4:16 + (m + 1) * 4], in0=ssq[:, 0:4], in1=mhalf[:, 0:4], op=ALU.pow,
                        reads=[bssq, b_mhalf], writes=[b_rr])
                if m + 1 < 4:
                    for q in range(2):
                        QK(q, 0, m=m + 1, dp=0, nk=8 * (m + 2))
                    preissued.add((m + 1, 0))
                for fn in pending:
                    fn()
                del pending[:]
                slot_stats()

        def phase_c():
            hb = view(0, 64 * KB, F32).rearrange("p (j d) -> p j d", j=16)
            wupb = [view(64 * KB + i * 8 * KB, 8 * KB, BF16).rearrange("p (c f) -> p c f", c=8) for i in range(2)]
            hT = view(96 * KB, 32 * KB, BF16).rearrange("p (c t) -> p c t", c=8)
            stg = c_stg
            w_outb = c_w_outb
            wdnb = [view(144 * KB + i * 8 * KB, 8 * KB, BF16).rearrange("p (c d) -> p c d", c=4) for i in range(2)]
            g2bc = c_g2bc
            hh = [view(164 * KB + i * 2 * KB, 2 * KB, BF16) for i in range(2)]
            junk = view(168 * KB, 2 * KB, BF16)
            actG = [view(160 * KB, 16 * KB, BF16).rearrange("p (f t) -> p f t", f=4),
                    view(80 * KB, 16 * KB, BF16).rearrange("p (f t) -> p f t", f=4)]
            ypf = ypT[:].rearrange("p c t -> p (c t)").bitcast(F32)
            gfbc = ypf[:, 0:1024]
            sqs = [ypf[:, 1024 + i * 512:1024 + (i + 1) * 512] for i in range(2)]
            junk2 = ypf[:, 2048:2560].bitcast(BF16)
            b_wout = c_b_wout
            b_stg = c_b_stg
            b_junk = Buf()
            b_junk2 = Buf()
            b_wup = [Buf() for _ in range(2)]
            b_wdn = [Buf() for _ in range(2)]
            b_g = c_b_g
            b_gf = Buf()
            b_hb = [Buf() for _ in range(16)]
            b_hh = [Buf() for _ in range(2)]
            b_hT = [Buf() for _ in range(4)]
            b_act = [Buf() for _ in range(2)]
            b_sqs = [Buf() for _ in range(2)]
            ssem = c_ssem
            hsem = [P.new_dma_sem() for _ in range(4)]
            osem = [P.new_dma_sem() for _ in range(4)]
            gsem = P.new_dma_sem()
            stg_rr = c_stg_rr
            cast_rr = [0]

            early0 = "B" in phases
            if early0:
                for j in range(4):
                    b_hb[j] = c_b_hb0[j]
            for m in range(1 if early0 else 0, 4):
                src = xk[(2 * m + 1) * 512:(2 * m + 2) * 512, :].rearrange("(j p) d -> p j d", p=128)
                P.D("sp", hsem[m], hb[:, m * 4:(m + 1) * 4, :], src, writes=[b_hb[m * 4 + j] for j in range(4)])

            def stage_load(src_ap, pattern, **kw):
                s_ = stg_rr[0] % 2
                stg_rr[0] += 1
                P.D("sp", ssem[s_], stg[s_].rearrange(pattern, **kw), src_ap, writes=[b_stg[s_]])
                return s_

            def outproj(blk):
                m = blk // 4
                bh = b_hb[blk]
                for dh in range(2):
                    k = (blk * 2 + dh) % 2
                    p1, p1b = ps[k * 2], psb[k * 2]
                    p2, p2b = ps[k * 2 + 1], psb[k * 2 + 1]
                    for ec in range(4):
                        P.I("pe", "matmul", p1[:], lhsT=ypT[:, ec, blk * 128:(blk + 1) * 128], rhs=w_outb[:, ec, dh * 512:(dh + 1) * 512],
                            start=(ec == 0), stop=(ec == 3), reads=[b_yp[m], b_wout], writes=[p1b])
                    for ec in range(4):
                        P.I("pe", "matmul", p2[:], lhsT=yaT[:, ec, blk * 128:(blk + 1) * 128], rhs=w_outb[:, 4 + ec, dh * 512:(dh + 1) * 512],
                            start=(ec == 0), stop=(ec == 3), reads=[b_ya[m], b_wout], writes=[p2b])
                    hs = hb[:, blk, dh * 512:(dh + 1) * 512]
                    P.I("dve", "scalar_tensor_tensor", out=hs, in0=p1[:], scalar=rr[:, blk:blk + 1], in1=hs, op0=ALU.mult, op1=ALU.add,
                        reads=[p1b, b_rr, bh], writes=[bh])
                    P.I("dve", "scalar_tensor_tensor", out=hs, in0=p2[:], scalar=rr[:, 16 + blk:17 + blk], in1=hs, op0=ALU.mult, op1=ALU.add,
                        reads=[p2b, b_rr, bh], writes=[bh])

            def norm2(blk):
                m = blk // 4
                bh = b_hb[blk]
                ss, bss = new_stat(3)
                P.I("act", "activation", out=junk, in_=hb[:, blk, :], func=AF.Square, accum_out=ss[:, 0:1], reads=[bh], writes=[b_junk, bss])
                P.I("act", "activation", out=ss[:, 1:2], in_=ss[:, 0:1], func=AF.Sqrt, scale=1.0 / D, bias=EPS, reads=[bss], writes=[bss])
                P.I("dve", "reciprocal", out=ss[:, 2:3], in_=ss[:, 1:2], reads=[bss], writes=[bss])
                hhb, bhh = hh[blk % 2], b_hh[blk % 2]
                P.I("dve", "scalar_tensor_tensor", out=hhb, in0=hb[:, blk, :], scalar=ss[:, 2:3], in1=g2bc, op0=ALU.mult, op1=ALU.mult,
                    reads=[bh, bss, b_g], writes=[bhh])

            def norm2_back(blk):
                m = blk // 4
                hhb, bhh = hh[blk % 2], b_hh[blk % 2]
                pi = 4 + blk % 2
                pt = ps[pi][:].bitcast(BF16).rearrange("p (c t) -> p c t", c=8)
                for c in range(8):
                    P.I("pe", "transpose", out=pt[:, c, :], in_=hhb[:, c * 128:(c + 1) * 128], identity=cst[:, C_ID:C_ID + 128],
                        reads=[bhh, b_cst], writes=[psb[pi]])
                P.I("act", "copy", out=hT[:, :, blk * 128:(blk + 1) * 128], in_=pt, reads=[psb[pi]], writes=[b_hT[m]])

            for b in range(18):
                if b < 16:
                    outproj(b)
                if 0 <= b - 1 < 16:
                    norm2(b - 1)
                if 0 <= b - 2 < 16:
                    norm2_back(b - 2)

            w_up_v = w_up.rearrange("(c p) f -> p c f", p=128)
            w_dn_v = w_down.rearrange("(fc p) d -> p fc d", p=128)

            def cast(dst, src, reads, writes):
                cast_rr[0] += 1
                if cast_rr[0] % 2:
                    P.I("act", "copy", out=dst, in_=src, reads=reads, writes=writes)
                else:
                    P.I("pool", "tensor_copy", out=dst, in_=src, reads=reads, writes=writes)

            def load_w(G):
                for half in range(2):
                    f0 = G * 512 + half * 256
                    s = stage_load(w_up_v[:, :, f0:f0 + 256], "p (c f) -> p c f", c=8)
                    cast(wupb[G % 2][:, :, half * 256:(half + 1) * 256], stg[s].rearrange("p (c f) -> p c f", c=8), [b_stg[s]], [b_wup[G % 2]])
                for half in range(2):
                    fc0 = G * 4 + half * 2
                    s = stage_load(w_dn_v[:, fc0:fc0 + 2, :], "p (c d) -> p c d", c=2)
                    cast(wdnb[G % 2][:, half * 2:(half + 1) * 2, :], stg[s].rearrange("p (c d) -> p c d", c=2), [b_stg[s]], [b_wdn[G % 2]])

            up_rr = [0]

            def up(G):
                wb, bw = wupb[G % 2], b_wup[G % 2]
                for f4 in range(4):
                    for tt in range(4):
                        pi = up_rr[0] % 4
                        up_rr[0] += 1
                        pu, pub = ps[pi], psb[pi]
                        for c in range(8):
                            P.I("pe", "matmul", pu[:], lhsT=wb[:, c, f4 * 128:(f4 + 1) * 128], rhs=hT[:, c, tt * 512:(tt + 1) * 512],
                                start=(c == 0), stop=(c == 7), reads=[bw, b_hT[tt]], writes=[pub])
                        sq, bsq = sqs[pi % 2], b_sqs[pi % 2]
                        P.I("act", "copy", out=sq, in_=pu[:], reads=[pub], writes=[bsq])
                        P.I("dve", "scalar_tensor_tensor", out=actG[G % 2][:, f4, tt * 512:(tt + 1) * 512], in0=pu[:], scalar=0.0, in1=sq,
                            op0=ALU.max, op1=ALU.mult, reads=[pub, bsq], writes=[b_act[G % 2]])

            dn_rr = [0]

            def down(G, last=False):
                wb, bw = wdnb[G % 2], b_wdn[G % 2]
                for blk in range(16):
                    if last and blk > 0:
                        final_norm(blk - 1)
                    for dh in range(2):
                        pi = 4 + dn_rr[0] % 4
                        dn_rr[0] += 1
                        for f4 in range(4):
                            P.I("pe", "matmul", ps[pi][:], lhsT=actG[G % 2][:, f4, blk * 128:(blk + 1) * 128], rhs=wb[:, f4, dh * 512:(dh + 1) * 512],
                                start=(f4 == 0), stop=(f4 == 3), reads=[bw, b_act[G % 2]], writes=[psb[pi]])
                        hs = hb[:, blk, dh * 512:(dh + 1) * 512]
                        P.I("dve", "tensor_tensor", out=hs, in0=ps[pi][:], in1=hs, op=ALU.add, reads=[psb[pi], b_hb[blk]], writes=[b_hb[blk]])
                if last:
                    final_norm(15)

            def final_norm(blk):
                m = blk // 4
                bh = b_hb[blk]
                ss, bss = new_stat(3)
                P.I("act", "activation", out=junk2, in_=hb[:, blk, :], func=AF.Square, accum_out=ss[:, 0:1], reads=[bh], writes=[b_junk2, bss])
                P.I("act", "activation", out=ss[:, 1:2], in_=ss[:, 0:1], func=AF.Sqrt, scale=1.0 / D, bias=EPS, reads=[bss], writes=[bss])
                P.I("dve", "reciprocal", out=ss[:, 2:3], in_=ss[:, 1:2], reads=[bss], writes=[bss])
                P.I("dve", "scalar_tensor_tensor", out=hb[:, blk, :], in0=hb[:, blk, :], scalar=ss[:, 2:3], in1=gfbc, op0=ALU.mult, op1=ALU.mult,
                    reads=[bh, bss, b_gf], writes=[bh])
                P.D("sp", osem[m], y[blk * 128:(blk + 1) * 128, :], hb[:, blk, :], reads=[bh])

            alias(b_act[0], [b_g, b_hh[0], b_hh[1], b_junk])
            alias(b_act[1], b_ya)
            for bb in b_wdn:
                alias(bb, [b_wout])
            for bb in [b_gf, b_junk2] + b_sqs:
                alias(bb, b_yp)
            P.D("sp", gsem, gfbc, gfbc_d, writes=[b_gf])
            NG = 8
            load_w(0)
            up(0)
            for G in range(NG):
                if G + 1 < NG:
                    load_w(G + 1)
                    up(G + 1)
                down(G, last=(G == NG - 1))

        if "A" in phases:
            phase_a()
        if "B" in phases:
            barrier()
            prefetch_c()
            phase_b()
        if "C" in phases:
            if "B" not in phases:
                prefetch_c()
            barrier()
            phase_c()

        if debug:
            barrier()
            dsem = P.new_dma_sem()
            tk = None
            for name, src in (("kT", view(0, 32 * KB, BF16)), ("v", view(32 * KB, 32 * KB, BF16)),
                              ("qT", view(64 * KB, 16 * KB, BF16)), ("yp", ypT[:].rearrange("p c t -> p (c t)")),
                              ("ya", view(80 * KB, 16 * KB, BF16)), ("rr", rr[:])):
                tk = P.D("sp", dsem, out=dbg[name], in_=src)
            P.wait("sp", [tk])
        P.wait("sp", [("dma", r[0], r[1]) for r in P.dsems if r[1] > 0])
        P.emit()
    return nc


def _bf16(a):
    return a.astype(ml_dtypes.bfloat16)


def make_consts(parity):
    C = np.zeros((128, C_END), np.float32)
    idx = np.arange(128)
    C[:, C_ID:C_ID + 128] = np.eye(128)
    C[:, C_TRI:C_TRI + 128] = -(idx[:, None] >= idx[None, :]).astype(np.float32)
    for r in (0, 32, 64, 96):
        C[r, C_SEL:C_SEL + 128] = 1.0
    C[:, C_NONE:C_NONE + 64] = -1.0
    t = np.arange(512)
    for i in range(4):
        C[:, C_MASK + i * 512:C_MASK + (i + 1) * 512] = -30000.0 * ((i * 128 + idx[:, None]) >= t[None, :]).astype(np.float32)
    tt = idx[:, None]
    to = idx[None, :]
    for g, w in enumerate(WINS):
        win = ((to - tt >= 0) & (to - tt < w)).astype(np.float32)
        band = win / w - np.eye(128)
        C[:, C_BAND + g * 128:C_BAND + (g + 1) * 128] = band
        if parity == 0:
            cnt = np.minimum(to + 1, w).astype(np.float32)
            C[:, C_BAND0 + g * 128:C_BAND0 + (g + 1) * 128] = win / cnt - np.eye(128)
        else:
            C[:, C_BAND0 + g * 128:C_BAND0 + (g + 1) * 128] = band
        C[:, C_BANDP + g * 128:C_BANDP + (g + 1) * 128] = ((to + 128 - tt) < w).astype(np.float32) / w
    return _bf16(C)


def make_in_maps(x, norm1_g, w_in, pool_w, pool_scale, pool_out_g, attn_out_g, w_out, norm2_g, w_up, w_down, final_g):
    f = lambda a: np.ascontiguousarray(np.asarray(a, dtype=np.float32))
    x = f(x)
    shared = {
        "w_in": f(w_in),
        "pool_w": f(np.transpose(np.asarray(pool_w), (1, 0, 2))),
        "w_out": f(w_out), "w_up": f(w_up), "w_down": f(w_down),
        "g1bc": f(np.broadcast_to(np.asarray(norm1_g)[None, :], (128, D))),
        "g2bc": f(np.broadcast_to(np.asarray(norm2_g)[None, :], (128, D))),
        "gfbc": f(np.broadcast_to(np.asarray(final_g)[None, :], (128, D))),
        "pscale": f(np.asarray(pool_scale).reshape(4, 128).T),
        "gout": f(np.concatenate([np.asarray(pool_out_g), np.asarray(attn_out_g)]).reshape(8, 128).T),
    }
    csts = [make_consts(0), make_consts(1)]
    maps = []
    for c in range(8):
        b, p = c // 2, c % 2
        if p == 1:
            xkc = x[b]
        else:
            xkc = np.concatenate([np.zeros((512, D), np.float32), x[b, :S - 512]], axis=0)
        mp = dict(shared)
        mp["xk"] = np.ascontiguousarray(xkc)
        mp["cst"] = csts[p]
        maps.append(mp)
    return maps


_NC_CACHE = {}


def kernel(x, norm1_g, w_in, pool_w, pool_scale, pool_out_g, attn_out_g, w_out, norm2_g, w_up, w_down, final_g):
    if "nc" not in _NC_CACHE:
        _NC_CACHE["nc"] = build_program()
    nc = _NC_CACHE["nc"]
    maps = make_in_maps(x, norm1_g, w_in, pool_w, pool_scale, pool_out_g, attn_out_g, w_out, norm2_g, w_up, w_down, final_g)
    res = run_bass_kernel_spmd(nc, maps, core_ids=list(range(8)))
    out = np.empty((4, S, D), np.float32)
    for c in range(8):
        b, p = c // 2, c % 2
        yc = np.asarray(res.results[c]["y"]).reshape(4, 512, D)
        for m in range(4):
            t0 = (2 * m + p) * 512
            out[b, t0:t0 + 512] = yc[m]
    return out
```
